# Optimizing a Trainium2 kernel written in Bass

```python
import jax
import jax.numpy as jnp
from jax import lax
import numpy as np

D_MODEL = 1024
BATCH = 2
SEQ = 8192
DEPTH = 2

GRID_W = 64
CTX_LEN = 256
EPS = 1e-6
D_FF = 4 * D_MODEL
N_MOD = 6

NA_HEADS = 8
NA_HEAD_DIM = D_MODEL // 16
NA_WIDTH = NA_HEADS * NA_HEAD_DIM
NA_KH = 8
NA_KW = 16

LRU_WIDTH = D_MODEL // 2
LRU_HEADS = 8
LRU_BLOCK = LRU_WIDTH // LRU_HEADS
LRU_CONV = 4
LRU_C = 8.0

GLA_HEADS = 4
GLA_DK = D_MODEL // 16
GLA_DV = D_MODEL // 8
GLA_KEY = GLA_HEADS * GLA_DK
GLA_VAL = GLA_HEADS * GLA_DV
GLA_RANK = 16
GLA_TAU = 16.0
GLA_CHUNK = 64

GQA_HEADS = 8
GQA_KV_HEADS = 2
GQA_HEAD_DIM = D_MODEL // 16
GQA_Q = GQA_HEADS * GQA_HEAD_DIM
GQA_KV = GQA_KV_HEADS * GQA_HEAD_DIM
Q_BLOCK = 128
ROPE_THETA = 10000.0

EVEN_IN = 3 * NA_WIDTH + 2 * LRU_WIDTH
EVEN_SPLITS = (NA_WIDTH, 2 * NA_WIDTH, 3 * NA_WIDTH, 3 * NA_WIDTH + LRU_WIDTH)
ODD_IN = 2 * GLA_KEY + 2 * GLA_VAL + 2 * GLA_RANK + GQA_Q + 2 * GQA_KV
ODD_SPLITS = (GLA_KEY, 2 * GLA_KEY, 2 * GLA_KEY + GLA_VAL, 2 * GLA_KEY + 2 * GLA_VAL,
              2 * GLA_KEY + 2 * GLA_VAL + 2 * GLA_RANK,
              2 * GLA_KEY + 2 * GLA_VAL + 2 * GLA_RANK + GQA_Q,
              2 * GLA_KEY + 2 * GLA_VAL + 2 * GLA_RANK + GQA_Q + GQA_KV)

kernel_name = 'hybrid_natten_rglru_gla_gqa_prefix'

F32 = jnp.float32


def rms_norm(x, w):
    xf = x.astype(F32)
    y = xf * lax.rsqrt(jnp.mean(xf * xf, axis=-1, keepdims=True) + EPS)
    return (y * w.astype(F32)).astype(x.dtype)


def modulate(x, shift, scale):
    return x * (1.0 + scale) + shift


def _split_heads(t, n_heads):
    return t.reshape(*t.shape[:-1], n_heads, t.shape[-1] // n_heads)


def _flip(t, direction):
    return t[:, ::-1] if direction else t


def squared_relu_mlp(u, w1, w2):
    return jnp.square(jax.nn.relu(u @ w1)) @ w2


def _rope_1d(x, pos):
    f = x.shape[-1] // 2
    inv = ROPE_THETA ** (-jnp.arange(f, dtype=F32) / f)
    ang = pos.astype(F32)[:, None] * inv[None, :]
    cos = jnp.cos(ang)[None, :, None, :]
    sin = jnp.sin(ang)[None, :, None, :]
    x1 = x[..., :f].astype(F32)
    x2 = x[..., f:].astype(F32)
    return jnp.concatenate([x1 * cos - x2 * sin, x2 * cos + x1 * sin], axis=-1)


def rope_2d(x, pos_row, pos_col):
    half = x.shape[-1] // 2
    return jnp.concatenate([_rope_1d(x[..., :half], pos_row),
                            _rope_1d(x[..., half:], pos_col)], axis=-1).astype(x.dtype)


def neighbourhood_attention(q_c, k_c, v_c, q_l, k_l, v_l, rpb, need_ctx):
    bsz, s, h, dh = q_l.shape
    rows = s // GRID_W
    kh = min(NA_KH, rows)
    scale = dh ** -0.5
    kg = k_l.reshape(bsz, rows, GRID_W, h, dh)
    vg = v_l.reshape(bsz, rows, GRID_W, h, dh)
    cols = jnp.arange(GRID_W)
    col_start = jnp.clip(cols - NA_KW // 2, 0, GRID_W - NA_KW)
    col_valid = (cols[None, :] >= col_start[:, None]) & (cols[None, :] < col_start[:, None] + NA_KW)
    d_col = jnp.clip(cols[None, :] - cols[:, None], 1 - NA_KW, NA_KW - 1) + (NA_KW - 1)

    def row_block(args):
        q_row, r = args
        start = jnp.clip(r - kh // 2, 0, rows - kh)
        k_rows = lax.dynamic_slice_in_dim(kg, start, kh, axis=1)
        v_rows = lax.dynamic_slice_in_dim(vg, start, kh, axis=1)
        d_row = start + jnp.arange(kh) - r + (NA_KH - 1)
        bias = rpb[:, d_row[None, :, None], d_col[:, None, :]]
        s_loc = jnp.einsum('bwhd,bkvhd->bhwkv', q_row, k_rows).astype(F32) * scale + bias.astype(F32)
        s_loc = jnp.where(col_valid[:, None, :], s_loc, -jnp.inf)
        s_ctx = jnp.einsum('bwhd,bchd->bhwc', q_row, k_c).astype(F32) * scale
        n_loc = kh * GRID_W
        p = jax.nn.softmax(jnp.concatenate([s_loc.reshape(bsz, h, GRID_W, n_loc), s_ctx], axis=-1), axis=-1)
        p = p.astype(v_l.dtype)
        p_loc = p[..., :n_loc].reshape(bsz, h, GRID_W, kh, GRID_W)
        return (jnp.einsum('bhwkv,bkvhd->bwhd', p_loc, v_rows)
                + jnp.einsum('bhwc,bchd->bwhd', p[..., n_loc:], v_c))

    q_rows = jnp.moveaxis(q_l.reshape(bsz, rows, GRID_W, h, dh), 1, 0)
    o = lax.map(row_block, (q_rows, jnp.arange(rows)))
    o_l = jnp.moveaxis(o, 0, 1).reshape(bsz, s, h * dh)
    o_c = None
    if need_ctx:
        sc = jnp.einsum('bqhd,bkhd->bhqk', q_c, k_c).astype(F32) * scale
        pc = jax.nn.softmax(sc, axis=-1).astype(v_c.dtype)
        o_c = jnp.einsum('bhqk,bkhd->bqhd', pc, v_c).reshape(bsz, q_c.shape[1], h * dh)
    return o_c, o_l


def depthwise_conv_centred(x, w, b):
    k, ch = w.shape
    y = lax.conv_general_dilated(x, w[:, None, :].astype(x.dtype), window_strides=(1,),
                                 padding=[(k // 2, k - 1 - k // 2)],
                                 dimension_numbers=('NWC', 'WIO', 'NWC'),
                                 feature_group_count=ch)
    return y + b


def rglru_gates(x, wa, ba, wx, bx, lam):
    bsz, t, ch = x.shape
    xb = x.reshape(bsz, t, LRU_HEADS, LRU_BLOCK)
    r = jax.nn.sigmoid((jnp.einsum('bthi,hij->bthj', xb, wa).reshape(bsz, t, ch) + ba).astype(F32))
    i = jax.nn.sigmoid((jnp.einsum('bthi,hij->bthj', xb, wx).reshape(bsz, t, ch) + bx).astype(F32))
    log_a = -LRU_C * r * jax.nn.softplus(-lam.astype(F32))
    a = jnp.exp(log_a)
    b = jnp.sqrt(-jnp.expm1(2.0 * log_a)) * (i * x.astype(F32))
    return a, b


def linear_recurrence(a, b, h0):
    b = b.at[:, 0].add(a[:, 0] * h0)

    def combine(left, right):
        a_l, b_l = left
        a_r, b_r = right
        return a_l * a_r, a_r * b_l + b_r

    _, h = lax.associative_scan(combine, (a, b), axis=1)
    return h


def rglru_mixer(x_c, g_c, x_l, g_l, conv_w, conv_b, wa, ba, wx, bx, lam, need_ctx):
    xc = depthwise_conv_centred(x_c, conv_w, conv_b)
    xl = depthwise_conv_centred(x_l, conv_w, conv_b)
    bsz = x_l.shape[0]
    ys_c, ys_l = [], []
    for d in range(2):
        a_c, b_c = rglru_gates(_flip(xc, d), wa[d], ba[d], wx[d], bx[d], lam[d])
        h_c = linear_recurrence(a_c, b_c, jnp.zeros((bsz, LRU_WIDTH), F32))
        a_l, b_l = rglru_gates(_flip(xl, d), wa[d], ba[d], wx[d], bx[d], lam[d])
        h_l = linear_recurrence(a_l, b_l, h_c[:, -1])
        ys_c.append(_flip(h_c, d))
        ys_l.append(_flip(h_l, d))
    out_l = (ys_l[0] + ys_l[1]).astype(x_l.dtype) * jax.nn.gelu(g_l)
    out_c = (ys_c[0] + ys_c[1]).astype(x_c.dtype) * jax.nn.gelu(g_c) if need_ctx else None
    return out_c, out_l


def even_mixer(u_c, u_l, w_in, rpb, conv_w, conv_b, wa, ba, wx, bx, lam, w_out, need_ctx):
    qc, kc, vc, xc, gc = jnp.split(u_c @ w_in, EVEN_SPLITS, axis=-1)
    ql, kl, vl, xl, gl = jnp.split(u_l @ w_in, EVEN_SPLITS, axis=-1)
    hs = lambda t: _split_heads(t, NA_HEADS)
    na_c, na_l = neighbourhood_attention(hs(qc), hs(kc), hs(vc), hs(ql), hs(kl), hs(vl), rpb, need_ctx)
    lru_c, lru_l = rglru_mixer(xc, gc, xl, gl, conv_w, conv_b, wa, ba, wx, bx, lam, need_ctx)
    y_l = jnp.concatenate([na_l, lru_l], axis=-1) @ w_out
    y_c = jnp.concatenate([na_c, lru_c], axis=-1) @ w_out if need_ctx else None
    return y_c, y_l


def gla_chunked(q, k, v, log_a, s0):
    bsz, t, h, dk = q.shape
    dv = v.shape[-1]
    n = t // GLA_CHUNK
    rs = lambda z: z.reshape(bsz, n, GLA_CHUNK, h, z.shape[-1]).transpose(1, 0, 3, 2, 4)
    q, k, v, g = rs(q), rs(k), rs(v), rs(log_a)
    b = jnp.cumsum(g, axis=3)
    b_last = b[..., -1:, :]
    q_in = q * jnp.exp(b)
    k_in = k * jnp.exp(-b)
    mask = jnp.tril(jnp.ones((GLA_CHUNK, GLA_CHUNK), dtype=bool))
    att = jnp.where(mask, jnp.einsum('nbhtd,nbhsd->nbhts', q_in, k_in), 0.0)
    o_intra = jnp.einsum('nbhts,nbhsv->nbhtv', att, v)
    u = jnp.einsum('nbhsd,nbhsv->nbhdv', k * jnp.exp(b_last - b), v)
    decay = jnp.exp(b_last[..., 0, :])

    def step(state, inp):
        dec, u_c = inp
        return dec[..., None] * state + u_c, state

    s_final, s_prev = lax.scan(step, s0, (decay, u))
    o = o_intra + jnp.einsum('nbhtd,nbhdv->nbhtv', q_in, s_prev)
    return o.transpose(1, 0, 3, 2, 4).reshape(bsz, t, h, dv), s_final


def gla_mixer(zc, zl, wa2, ba, norm_w, need_ctx):
    def prep(z, d):
        q, k, v, _, lr = z
        qf = _split_heads(q, GLA_HEADS).astype(F32) * (GLA_DK ** -0.5)
        kf = _split_heads(k, GLA_HEADS).astype(F32)
        vf = _split_heads(v, GLA_HEADS).astype(F32)
        lr_d = lr[..., d * GLA_RANK:(d + 1) * GLA_RANK]
        log_a = jax.nn.log_sigmoid((lr_d @ wa2[d] + ba[d]).astype(F32)) / GLA_TAU
        log_a = _split_heads(log_a, GLA_HEADS)
        return _flip(qf, d), _flip(kf, d), _flip(vf, d), _flip(log_a, d)

    bsz = zl[0].shape[0]
    outs_c, outs_l = [], []
    for d in range(2):
        s0 = jnp.zeros((bsz, GLA_HEADS, GLA_DK, GLA_DV), F32)
        o_c, s_ctx = gla_chunked(*prep(zc, d), s0)
        o_l, _ = gla_chunked(*prep(zl, d), s_ctx)
        outs_c.append(_flip(o_c, d))
        outs_l.append(_flip(o_l, d))

    def finish(o, gate):
        o = rms_norm(o.astype(gate.dtype), norm_w) * jax.nn.silu(_split_heads(gate, GLA_HEADS))
        return o.reshape(*o.shape[:-2], GLA_VAL)

    y_l = finish(outs_l[0] + outs_l[1], zl[3])
    y_c = finish(outs_c[0] + outs_c[1], zc[3]) if need_ctx else None
    return y_c, y_l


def gqa_mixer(zc, zl, q_norm_w, k_norm_w, pos_row, pos_col, need_ctx):
    group = GQA_HEADS // GQA_KV_HEADS
    scale = GQA_HEAD_DIM ** -0.5

    def heads(z):
        q, k, v = z
        q = rms_norm(_split_heads(q, GQA_HEADS), q_norm_w)
        k = rms_norm(_split_heads(k, GQA_KV_HEADS), k_norm_w)
        return q, k, _split_heads(v, GQA_KV_HEADS)

    qc, kc, vc = heads(zc)
    ql, kl, vl = heads(zl)
    ql = rope_2d(ql, pos_row, pos_col)
    kl = rope_2d(kl, pos_row, pos_col)
    bsz, s = ql.shape[:2]
    k_all = jnp.concatenate([kc, kl], axis=1)
    v_all = jnp.concatenate([vc, vl], axis=1)

    def to_groups(q):
        return q.reshape(q.shape[0], q.shape[1], GQA_KV_HEADS, group, GQA_HEAD_DIM).transpose(0, 2, 3, 1, 4)

    def attend(q_grp, k, v):
        sc = jnp.einsum('bkgqd,bskd->bkgqs', q_grp, k).astype(F32) * scale
        p = jax.nn.softmax(sc, axis=-1).astype(v.dtype)
        return jnp.einsum('bkgqs,bskd->bkgqd', p, v)

    nb = s // Q_BLOCK
    q_blocks = to_groups(ql).reshape(bsz, GQA_KV_HEADS, group, nb, Q_BLOCK, GQA_HEAD_DIM).transpose(3, 0, 1, 2, 4, 5)
    o = lax.map(lambda qb: attend(qb, k_all, v_all), q_blocks)
    o_l = o.transpose(1, 0, 4, 2, 3, 5).reshape(bsz, s, GQA_Q)
    o_c = None
    if need_ctx:
        oc = attend(to_groups(qc), kc, vc)
        o_c = oc.transpose(0, 3, 1, 2, 4).reshape(bsz, qc.shape[1], GQA_Q)
    return o_c, o_l


def odd_mixer(u_c, u_l, w_in, wa2, ba, gla_norm_w, q_norm_w, k_norm_w, w_out, pos_row, pos_col, need_ctx):
    zc = jnp.split(u_c @ w_in, ODD_SPLITS, axis=-1)
    zl = jnp.split(u_l @ w_in, ODD_SPLITS, axis=-1)
    gla_c, gla_l = gla_mixer(zc[:5], zl[:5], wa2, ba, gla_norm_w, need_ctx)
    gqa_c, gqa_l = gqa_mixer(zc[5:], zl[5:], q_norm_w, k_norm_w, pos_row, pos_col, need_ctx)
    y_l = jnp.concatenate([gla_l, gqa_l], axis=-1) @ w_out
    y_c = jnp.concatenate([gla_c, gqa_c], axis=-1) @ w_out if need_ctx else None
    return y_c, y_l


def setup_inputs(seed: int = 0) -> dict:
    key = jax.random.key(seed)
    keys = jax.random.split(key, 32)
    counter = iter(range(32))
    n_even = (DEPTH + 1) // 2
    n_odd = DEPTH // 2

    def nrm(shape, scale):
        return scale * jax.random.normal(keys[next(counter)], shape, F32)

    def gain(shape):
        return 1.0 + 0.1 * jax.random.normal(keys[next(counter)], shape, F32)

    a_pow = jax.random.uniform(keys[next(counter)], (n_even, 2, LRU_WIDTH), F32, 0.9, 0.999)
    a = a_pow ** (1.0 / LRU_C)
    lru_lambda = jnp.log(a) - jnp.log1p(-a)
    return {
        'x': nrm((BATCH, SEQ, D_MODEL), 1.0),
        'c': nrm((BATCH, D_MODEL), 1.0),
        'ctx': nrm((BATCH, CTX_LEN, D_MODEL), 1.0),
        'c_ctx': nrm((D_MODEL,), 0.5),
        'norm1_w': gain((DEPTH, D_MODEL)),
        'norm2_w': gain((DEPTH, D_MODEL)),
        'w_mod': nrm((DEPTH, D_MODEL, N_MOD * D_MODEL), 0.5 * D_MODEL ** -0.5),
        'b_mod': nrm((DEPTH, N_MOD * D_MODEL), 0.02),
        'w_ff1': nrm((DEPTH, D_MODEL, D_FF), D_MODEL ** -0.5),
        'w_ff2': nrm((DEPTH, D_FF, D_MODEL), D_FF ** -0.5),
        'w_in_even': nrm((n_even, D_MODEL, EVEN_IN), D_MODEL ** -0.5),
        'na_rpb': nrm((n_even, NA_HEADS, 2 * NA_KH - 1, 2 * NA_KW - 1), 0.1),
        'lru_conv_w': nrm((n_even, LRU_CONV, LRU_WIDTH), LRU_CONV ** -0.5),
        'lru_conv_b': nrm((n_even, LRU_WIDTH), 0.02),
        'lru_wa': nrm((n_even, 2, LRU_HEADS, LRU_BLOCK, LRU_BLOCK), LRU_BLOCK ** -0.5),
        'lru_ba': nrm((n_even, 2, LRU_WIDTH), 0.02),
        'lru_wx': nrm((n_even, 2, LRU_HEADS, LRU_BLOCK, LRU_BLOCK), LRU_BLOCK ** -0.5),
        'lru_bx': nrm((n_even, 2, LRU_WIDTH), 0.02),
        'lru_lambda': lru_lambda,
        'w_out_even': nrm((n_even, NA_WIDTH + LRU_WIDTH, D_MODEL), (NA_WIDTH + LRU_WIDTH) ** -0.5),
        'w_in_odd': nrm((n_odd, D_MODEL, ODD_IN), D_MODEL ** -0.5),
        'gla_wa2': nrm((n_odd, 2, GLA_RANK, GLA_KEY), GLA_RANK ** -0.5),
        'gla_ba': nrm((n_odd, 2, GLA_KEY), 0.1),
        'gla_norm_w': gain((n_odd, GLA_DV)),
        'gqa_q_norm_w': gain((n_odd, GQA_HEAD_DIM)),
        'gqa_k_norm_w': gain((n_odd, GQA_HEAD_DIM)),
        'w_out_odd': nrm((n_odd, GLA_VAL + GQA_Q, D_MODEL), (GLA_VAL + GQA_Q) ** -0.5),
        'final_norm_w': gain((D_MODEL,)),
    }


def reference(x, c, ctx, c_ctx, norm1_w, norm2_w, w_mod, b_mod, w_ff1, w_ff2,
              w_in_even, na_rpb, lru_conv_w, lru_conv_b, lru_wa, lru_ba, lru_wx, lru_bx, lru_lambda, w_out_even,
              w_in_odd, gla_wa2, gla_ba, gla_norm_w, gqa_q_norm_w, gqa_k_norm_w, w_out_odd, final_norm_w):
    s = x.shape[1]
    t = jnp.arange(s, dtype=jnp.int32)
    pos_row = t // GRID_W
    pos_col = t % GRID_W
    silu_c = jax.nn.silu(c)
    silu_cc = jax.nn.silu(c_ctx)
    h_l, h_c = x, ctx
    for i in range(DEPTH):
        need_ctx = i < DEPTH - 1
        mod_l = jnp.split((silu_c @ w_mod[i] + b_mod[i])[:, None, :], N_MOD, axis=-1)
        mod_c = jnp.split(silu_cc @ w_mod[i] + b_mod[i], N_MOD, axis=-1)
        u_l = modulate(rms_norm(h_l, norm1_w[i]), mod_l[0], mod_l[1])
        u_c = modulate(rms_norm(h_c, norm1_w[i]), mod_c[0], mod_c[1])
        j = i // 2
        if i % 2 == 0:
            y_c, y_l = even_mixer(u_c, u_l, w_in_even[j], na_rpb[j], lru_conv_w[j], lru_conv_b[j],
                                  lru_wa[j], lru_ba[j], lru_wx[j], lru_bx[j], lru_lambda[j],
                                  w_out_even[j], need_ctx)
        else:
            y_c, y_l = odd_mixer(u_c, u_l, w_in_odd[j], gla_wa2[j], gla_ba[j], gla_norm_w[j],
                                 gqa_q_norm_w[j], gqa_k_norm_w[j], w_out_odd[j], pos_row, pos_col, need_ctx)
        h_l = h_l + mod_l[2] * y_l
        h_l = h_l + mod_l[5] * squared_relu_mlp(
            modulate(rms_norm(h_l, norm2_w[i]), mod_l[3], mod_l[4]), w_ff1[i], w_ff2[i])
        if need_ctx:
            h_c = h_c + mod_c[2] * y_c
            h_c = h_c + mod_c[5] * squared_relu_mlp(
                modulate(rms_norm(h_c, norm2_w[i]), mod_c[3], mod_c[4]), w_ff1[i], w_ff2[i])
    return rms_norm(h_l, final_norm_w)
```

```python
import numpy as np
from contextlib import ExitStack
import concourse.bass as bass
import concourse.mybir as mybir
from concourse.bass_utils import run_bass_kernel_spmd

F32 = mybir.dt.float32
BF16 = mybir.dt.bfloat16
AF = mybir.ActivationFunctionType
ALU = mybir.AluOpType
AX = mybir.AxisListType

NEG = -30000.0
EPS = 1e-6
RG = [[0, 1, 2, 3], [4, 5, 6, 7]]


class Sched:
    ENG = ("pe", "dve", "act", "pool", "sp")

    def __init__(self, nc):
        self.nc = nc
        self.streams = {e: [] for e in self.ENG}
        self.sem = {e: nc.alloc_semaphore(name="s_" + e) for e in self.ENG}
        self.cnt = {e: 0 for e in self.ENG}
        self.waited = {e: {} for e in self.ENG}
        self.lastw = {}
        self.readers = {}
        self.dsem = {}
        self.dcnt = {}

    def _deps(self, eng, reads, writes, pe_acc=False):
        waits = {}

        def add(ev):
            if ev is None:
                return
            s, v = ev
            if waits.get(s, 0) < v:
                waits[s] = v

        for k in reads:
            add(self.lastw.get(k))
        for k in writes:
            lw = self.lastw.get(k)
            rds = self.readers.get(k, ())
            same = lw is not None and lw[0] == ("e", eng)
            if not ((pe_acc and same) or (same and len(rds) > 0)):
                add(lw)
            for r in rds:
                add(r)
        out = []
        for s, v in waits.items():
            if self.waited[eng].get(s, 0) >= v:
                continue
            self.waited[eng][s] = v
            out.append((s, v))
        return out

    def _commit(self, ev, reads, writes):
        for k in reads:
            self.readers.setdefault(k, []).append(ev)
        for k in writes:
            self.lastw[k] = ev
            self.readers[k] = []

    def op(self, eng, fn, reads=(), writes=(), pe_acc=False):
        waits = self._deps(eng, reads, writes, pe_acc)
        self.cnt[eng] += 1
        ev = (("e", eng), self.cnt[eng])
        self.streams[eng].append((waits, fn, ("e", eng), 1))
        self._commit(ev, reads, writes)

    def dma(self, eng, fn, dsem, reads=(), writes=(), inc=16):
        if dsem not in self.dsem:
            self.dsem[dsem] = self.nc.alloc_semaphore(name="d_" + dsem)
            self.dcnt[dsem] = 0
        waits = self._deps(eng, reads, writes)
        self.dcnt[dsem] += inc
        ev = (("d", dsem), self.dcnt[dsem])
        self.streams[eng].append((waits, fn, ("d", dsem), inc))
        self._commit(ev, reads, writes)

    def wait_all(self, eng, keys):
        waits = self._deps(eng, keys, ())
        if waits:
            self.streams[eng].append((waits, None, None, 0))

    def barrier(self):
        evs = [(("e", e), self.cnt[e]) for e in self.ENG if self.cnt[e] > 0]
        evs += [(("d", n), self.dcnt[n]) for n in self.dsem if self.dcnt[n] > 0]
        for e in self.ENG:
            waits = []
            for s_, v in evs:
                if self.waited[e].get(s_, 0) < v:
                    self.waited[e][s_] = v
                    waits.append((s_, v))
            if waits:
                self.streams[e].append((waits, None, None, 0))
        self.lastw = {}
        self.readers = {}

    def _semh(self, s):
        kind, name = s
        return self.sem[name] if kind == "e" else self.dsem[name]

    def emit(self):
        nc = self.nc
        engs = {"pe": "tensor", "dve": "vector", "act": "scalar", "pool": "gpsimd", "sp": "sync"}
        with nc.Block() as block:
            for e, bname in engs.items():
                stream = self.streams[e]

                def body(engine, stream=stream):
                    for waits, fn, s, inc in stream:
                        for ws, wv in waits:
                            engine.wait_ge(self._semh(ws), wv)
                        if fn is not None:
                            ins = fn(engine)
                            ins.then_inc(self._semh(s), inc)

                getattr(block, bname)(body)


VC = {}
_off = 0


def _vc(name, n):
    global _off
    VC[name] = _off
    _off += n


_vc("c", 8); _vc("cctx", 8)
for _l in range(2):
    _vc(f"n1w{_l}", 8); _vc(f"n2w{_l}", 8); _vc(f"bmod{_l}", 48)
_vc("fnw", 8)
_vc("cw", 16); _vc("cb", 4)
for _d in range(2):
    _vc(f"ba{_d}", 4); _vc(f"bx{_d}", 4); _vc(f"lam{_d}", 4)
_vc("sel", 4); _vc("flag", 2)
_vc("gba0", 2); _vc("gba1", 2); _vc("gnw", 1)
NV = _off


class Builder:
    def __init__(self, stop_after=None, debug=(), skip_l0=False):
        self.skip_l0 = skip_l0
        self.sd = 0
        self.spair = 0
        self.stop_after = stop_after
        self.debug = set(debug)
        self.nc = nc = bass.Bass("TRN2", target_bir_lowering=False)
        self.es = ExitStack()
        self.cur = self.es
        self.stk = []
        self.S = Sched(nc)
        self.dbg_out = {}
        self.uid = 0

    def sb(self, name, shape, dt):
        self.nalloc = getattr(self, "nalloc", 0) + 1
        return self.cur.enter_context(self.nc.sbuf_tensor(f"sb_{name}_{self.nalloc}", list(shape), dt))

    def ps(self, name, shape, dt):
        return self.es.enter_context(self.nc.psum_tensor("ps_" + name, list(shape), dt))

    def din(self, name, shape, dt=F32):
        return self.nc.dram_tensor(name, list(shape), dt, kind="ExternalInput").ap()

    def dout(self, name, shape, dt=F32):
        return self.nc.dram_tensor(name, list(shape), dt, kind="ExternalOutput").ap()

    def dscr(self, name, shape, dt=F32):
        return self.nc.dram_tensor(name, list(shape), dt).ap()

    def stage_begin(self):
        self.stk.append(self.cur)
        self.cur = ExitStack()

    def stage_end(self):
        self.marks = getattr(self, "marks", [])
        self.marks.append(dict(self.S.cnt))
        self.S.barrier()
        self.cur.close()
        self.cur = self.stk.pop()

    @staticmethod
    def interleave(gens, width):
        gens = list(gens)
        active = []
        while gens or active:
            while gens and len(active) < width:
                active.append(gens.pop(0))
            nxt = []
            for g in active:
                try:
                    next(g)
                    nxt.append(g)
                except StopIteration:
                    pass
            active = nxt

    def dump(self, name, tile_ap, shape, keys, dt=F32):
        if name not in self.debug:
            return
        d = self.dout("dbg_" + name, shape, dt)
        self.dbg_out[name] = d
        self.S.dma("sp", lambda e: e.dma_start(out=d, in_=tile_ap), "dbg", reads=keys, writes=[("dbg", name)])

    def build(self):
        nc, S = self.nc, self.S
        op, dma = S.op, S.dma
        shapes = {
            "xh": [2560, 1024], "ctxb": [256, 1024], "vecs": [128, NV],
            "w_mod": [2, 1024, 6144], "b_mod": [2, 6144], "w_ff1": [2, 1024, 4096], "w_ff2": [2, 4096, 1024],
            "w_in_even": [1024, 2560], "w_out_even": [1024, 1024], "lru_wa": [2, 8, 64, 64], "lru_wx": [2, 8, 64, 64],
            "natab": [128, 8, 22 * 64], "qaug": [40, 2048], "kaug": [40, 2560],
            "w_in_odd": [1024, 2336], "w_out_odd": [1024, 1024], "gla_wa2": [2, 16, 256], "barow": [1, 512],
            "cosT": [2048, 64], "sinT": [2048, 32], "nsinT": [2048, 32], "qnw_bc": [128, 64], "knw_bc": [128, 64],
            "gmask": [128, 4, 128], "fnw_bc": [128, 1024],
        }
        self.ins = {}

        class Lazy:
            def __init__(s2, name):
                s2.name = name

            def ap(s2):
                if s2.name not in self.ins:
                    self.ins[s2.name] = self.din(s2.name, shapes[s2.name])
                return self.ins[s2.name]

            def __getitem__(s2, k):
                return s2.ap()[k]

        xh, ctxb, vecs_d = Lazy("xh"), Lazy("ctxb"), Lazy("vecs")
        w_mod, b_mod, w_ff1, w_ff2 = Lazy("w_mod"), Lazy("b_mod"), Lazy("w_ff1"), Lazy("w_ff2")
        w_in_e, w_out_e, lru_wa, lru_wx = Lazy("w_in_even"), Lazy("w_out_even"), Lazy("lru_wa"), Lazy("lru_wx")
        natab_d, qaug_d, kaug_d = Lazy("natab"), Lazy("qaug"), Lazy("kaug")
        w_in_o, w_out_o, gla_wa2_d, barow_d = Lazy("w_in_odd"), Lazy("w_out_odd"), Lazy("gla_wa2"), Lazy("barow")
        cosT_d, sinT_d, nsinT_d, qnw_d, knw_d = Lazy("cosT"), Lazy("sinT"), Lazy("nsinT"), Lazy("qnw_bc"), Lazy("knw_bc")
        gmask_d, fnw_d = Lazy("gmask"), Lazy("fnw_bc")
        out_d = self.dout("out", [2048, 1024])
        hmid = self.dscr("hmid", [2304, 1024])
        h1 = self.dscr("h1", [2304, 1024])
        cc1_in = self.dscr("cc1_in", [128, 16])
        cc1_out = self.dscr("cc1_out", [512, 16])
        mbscr2 = self.dscr("mbscr2", [2, 128, 2, 2048])
        qscr = self.dscr("qscr", [64, 8, 2048], BF16)
        kctx = self.dscr("kctx", [64, 2, 256], BF16)
        vctx = self.dscr("vctx", [256, 128], BF16)
        cc2k_in = self.dscr("cc2k_in", [64, 4096], BF16)
        cc2k_out = self.dscr("cc2k_out", [256, 4096], BF16)
        cc2v_in = self.dscr("cc2v_in", [2048, 128], BF16)
        cc2v_out = self.dscr("cc2v_out", [8192, 128], BF16)
        cc3_in = self.dscr("cc3_in", [128, 516])
        cc3_out = self.dscr("cc3_out", [512, 516])

        ident_f = self.sb("ident_f", [128, 128], F32)
        ident_b = self.sb("ident_b", [128, 128], BF16)
        ones_f = self.sb("ones_f", [128, 128], F32)
        vecs = self.sb("vecs", [128, NV], F32)
        op("pool", lambda e: e.memset(ident_f[:], 1.0), writes=["ident_f"])
        op("pool", lambda e: e.affine_select(out=ident_f[:], in_=ident_f[:], pattern=[[-1, 128]], compare_op=ALU.is_equal, fill=0.0, base=0, channel_multiplier=1), reads=["ident_f"], writes=["ident_f"])
        op("pool", lambda e: e.tensor_copy(out=ident_b[:], in_=ident_f[:]), reads=["ident_f"], writes=["ident_b"])
        op("pool", lambda e: e.memset(ones_f[:], 1.0), writes=["ones_f"])
        dma("sp", lambda e: e.dma_start(out=vecs[:], in_=vecs_d.ap()), "vecs", writes=["vecs"])

        def V(name, n=1, off=0):
            o = VC[name] + off
            return vecs[:, o:o + n]

        pT = self.ps("pT", [128, 1024], F32)
        pA = self.ps("pA", [128, 512], F32)
        pB = self.ps("pB", [128, 512], F32)
        pY = [self.ps(f"pY{i}", [128, 512], F32) for i in range(4)]
        pAB = [pA, pB]
        kAB = ["pA", "pB"]

        uT = self.sb("uT", [128, 8, 2816], BF16)
        mixAB = self.sb("mixAB", [128, 8, 2304], BF16)
        mixA = mixAB[:, 0:4, :]
        mixB = mixAB[:, 4:8, :]
        G1 = self.sb("G1", [128, 8, 2], F32)
        S1 = self.sb("S1", [128, 8, 2], F32)
        G2 = self.sb("G2", [128, 8, 2], F32)
        S2 = self.sb("S2", [128, 8, 2], F32)
        junk = self.sb("junk", [128, 1024], BF16)
        st = self.sb("st", [128, 16], F32)
        junk2 = [junk, self.sb("junkb", [128, 1024], BF16)]
        tmpf = self.sb("tmpf", [128, 1024], F32)

        def uk(c0, c1):
            return [("uT", t) for t in range(c0 // 128, (c1 + 127) // 128)]

        G1b = self.sb("G1b", [128, 8, 2], F32)
        S1b = self.sb("S1b", [128, 8, 2], F32)
        G2b = self.sb("G2b", [128, 8, 2], F32)
        S2b = self.sb("S2b", [128, 8, 2], F32)
        GS = {0: (G1, S1, G2, S2, "G1", "S1", "G2", "S2"), 1: (G1b, S1b, G2b, S2b, "G1b", "S1b", "G2b", "S2b")}

        def mod_gen(l, dual, look=False):
            G1_, S1_, G2_, S2_, g1k, s1k, g2k, s2k = GS[l]
            sc_b = self.sb("sc_b", [128, 8, 2], BF16)
            sc_rep = self.sb("sc_rep", [128, 2, 8, 128], BF16)
            nbuf = 2 if (dual or look) else 1
            wm = [self.sb(f"wm{i}", [128, 8, 1024], BF16) for i in range(nbuf)]
            wf = self.sb("wf", [128, 8, 1024], F32) if dual else None
            sct = self.sb(f"sct{l}", [128, 8, 2], F32)
            MBt = [self.sb(f"MBt{i}", [128, 512], F32) for i in range(2)]
            sck = f"sct{l}"
            op("act", lambda e: e.activation(out=sct[:, :, 0], in_=V("c", 8), func=AF.Silu), reads=["vecs"], writes=[sck])
            op("act", lambda e: e.activation(out=sct[:, :, 1], in_=V("cctx", 8), func=AF.Silu), reads=["vecs", sck], writes=[sck])
            op("dve", lambda e: e.tensor_copy(out=sc_b[:], in_=sct[:]), reads=[sck], writes=["sc_b"])
            for w in range(2):
                op("dve", lambda e, w=w: e.tensor_copy(out=sc_rep[:, w], in_=sct[:, :, w:w + 1].to_broadcast([128, 8, 128])), reads=[sck], writes=["sc_rep"])
            modT = self.sb(f"modT{l}", [128, 4, 8, 2], F32)
            fm_idx = {0: 0, 1: 1, 3: 2, 4: 3}
            yield

            def D(m):
                buf = wm[m % nbuf]
                bk = f"wm{m % nbuf}"
                if m % 2 == 0 or not dual:
                    for kc in range(8):
                        dma("pool", lambda e, kc=kc: e.dma_start(out=buf[:, kc, :], in_=w_mod[l, kc * 128:(kc + 1) * 128, m * 1024:(m + 1) * 1024]), bk, writes=[bk])
                else:
                    for kc in range(8):
                        dma("sp" if kc % 2 == 0 else "act", lambda e, kc=kc: e.dma_start(out=wf[:, kc, :], in_=w_mod[l, kc * 128:(kc + 1) * 128, m * 1024:(m + 1) * 1024]), f"wf{kc}", writes=[("wf", kc)])
                        ceng = "dve" if kc % 2 == 0 else "pool"
                        op(ceng, lambda e, kc=kc: e.tensor_copy(out=buf[:, kc, :], in_=wf[:, kc, :]), reads=[("wf", kc)], writes=[bk])

            def M(m):
                buf = wm[m % nbuf]
                bk = f"wm{m % nbuf}"
                if m in fm_idx:
                    for oc in range(8):
                        for kc in range(8):
                            op("pe", lambda e, oc=oc, kc=kc: e.matmul(pA[:, oc * 2:oc * 2 + 2], lhsT=buf[:, kc, oc * 128:(oc + 1) * 128], rhs=sc_b[:, kc, :], start=(kc == 0), stop=(kc == 7)), reads=[bk, "sc_b"], writes=["pA"], pe_acc=True)
                    mi = fm_idx[m]
                    op("dve", lambda e: e.tensor_tensor(out=modT[:, mi], in0=pA[:, 0:16].rearrange("p (c w) -> p c w", w=2), in1=V(f"bmod{l}", 8, m * 8).unsqueeze(2).to_broadcast([128, 8, 2]), op=ALU.add), reads=["pA", "vecs"], writes=[("modT", l, mi)])
                else:
                    gi = 0 if m == 2 else 1
                    dma("sp", lambda e: e.dma_start(out=tmpf[0:1, :], in_=b_mod[l:l + 1, m * 1024:(m + 1) * 1024]), "tmpf", writes=["tmpf"])
                    for w in range(2):
                        for nh in range(2):
                            pb = pAB[(w * 2 + nh) % 2]
                            pk = kAB[(w * 2 + nh) % 2]
                            op("pe", lambda e, nh=nh, pb=pb: e.matmul(pb[:], lhsT=ones_f[0:1, :], rhs=tmpf[0:1, nh * 512:(nh + 1) * 512], start=True, stop=False), reads=["ones_f", "tmpf"], writes=[pk], pe_acc=True)
                            for kc in range(8):
                                op("pe", lambda e, w=w, nh=nh, kc=kc, pb=pb: e.matmul(pb[:], lhsT=sc_rep[:, w, kc, :], rhs=buf[:, kc, nh * 512:(nh + 1) * 512], start=False, stop=(kc == 7)), reads=[bk, "sc_rep"], writes=[pk], pe_acc=True)
                            mt_ = MBt[(w * 2 + nh) % 2]
                            mtk = f"MBt{(w * 2 + nh) % 2}"
                            op("act", lambda e, pb=pb, mt_=mt_: e.activation(out=mt_[:], in_=pb[:], func=AF.Copy), reads=[pk], writes=[mtk])
                            dma("sp", lambda e, w=w, nh=nh, mt_=mt_: e.dma_start(out=mbscr2[l, :, gi, w * 1024 + nh * 512:w * 1024 + (nh + 1) * 512], in_=mt_[:]), mtk, reads=[mtk], writes=[("mbscr", l, gi)])

            def GSset(Gt, St, nw, ms, mh, gk_, sk_):
                op("dve", lambda e: e.tensor_scalar(out=Gt[:], in0=modT[:, ms], scalar1=1.0, scalar2=None, op0=ALU.add), reads=[("modT", l, ms)], writes=[gk_])
                op("dve", lambda e: e.tensor_tensor(out=Gt[:], in0=Gt[:], in1=V(nw, 8).unsqueeze(2).to_broadcast([128, 8, 2]), op=ALU.mult), reads=[gk_, "vecs"], writes=[gk_])
                op("dve", lambda e: e.tensor_copy(out=St[:], in_=modT[:, mh]), reads=[("modT", l, mh)], writes=[sk_])

            if nbuf == 2:
                D(0)
                yield
            for m in range(6):
                if nbuf == 2:
                    if m + 1 < 6:
                        D(m + 1)
                else:
                    D(m)
                M(m)
                if m == 1:
                    GSset(G1_, S1_, f"n1w{l}", 1, 0, g1k, s1k)
                if m == 4:
                    GSset(G2_, S2_, f"n2w{l}", 3, 2, g2k, s2k)
                yield

        def mod_stage(l):
            self.stage_begin()
            for _ in mod_gen(l, True):
                pass
            self.stage_end()

        def norm_gen(xtile, xkey, w, dst0, Gt, St, gk, sk, ntile=None, nkey=None):
            if ntile is None:
                ntile, nkey = xtile, xkey
            u = self.uid = self.uid + 1
            sc = st[:, (u % 4) * 4:(u % 4) * 4 + 4]
            skk = ("st", u % 4)
            jk = junk2[u % 2]
            jkk = f"junk{u % 2}"
            op("act", lambda e: e.activation(out=jk[:], in_=xtile[:], func=AF.Square, accum_out=sc[:, 0:1]), reads=[xkey], writes=[jkk, skk])
            yield
            op("dve", lambda e: e.tensor_scalar(out=sc[:, 1:2], in0=sc[:, 0:1], scalar1=1.0 / 1024, scalar2=EPS, op0=ALU.mult, op1=ALU.add), reads=[skk], writes=[skk])
            yield
            op("act", lambda e: e.activation(out=sc[:, 2:3], in_=sc[:, 1:2], func=AF.Sqrt), reads=[skk], writes=[skk])
            yield
            op("dve", lambda e: e.reciprocal(out=sc[:, 3:4], in_=sc[:, 2:3]), reads=[skk], writes=[skk])
            yield
            op("dve", lambda e: e.tensor_scalar(out=ntile[:], in0=xtile[:], scalar1=sc[:, 3:4], scalar2=None, op0=ALU.mult), reads=[xkey, skk], writes=[nkey])
            yield
            if dst0 is None:
                return

            def tps(c):
                if u % 2 == 0:
                    return pT[:, c * 128:(c + 1) * 128], "pT"
                return pY[2 + c // 4][:, (c % 4) * 128:(c % 4 + 1) * 128], f"pY{2 + c // 4}"

            for c in range(8):
                tp, tpk = tps(c)
                op("pe", lambda e, c=c, tp=tp: e.transpose(out=tp, in_=ntile[:, c * 128:(c + 1) * 128], identity=ident_f[:]), reads=[nkey, "ident_f"], writes=[tpk], pe_acc=True)
                yield
            for c in range(8):
                dst = uT[:, c, dst0:dst0 + 128]
                src, tpk = tps(c)
                if c % 2 == 0:
                    op("dve", lambda e, dst=dst, src=src, c=c: e.tensor_scalar(out=dst, in0=src, scalar1=Gt[:, c, w:w + 1], scalar2=St[:, c, w:w + 1], op0=ALU.mult, op1=ALU.add), reads=[tpk, gk, sk], writes=uk(dst0, dst0 + 128))
                    yield
                else:
                    op("act", lambda e, dst=dst, src=src, c=c: e.activation(out=dst, in_=src, func=AF.Identity, scale=Gt[:, c, w:w + 1], bias=St[:, c, w:w + 1]), reads=[tpk, gk, sk], writes=uk(dst0, dst0 + 128))
                    yield

        def norm_from_sbuf(*a_, **k_):
            for _ in norm_gen(*a_, **k_):
                pass

        pi = [0]

        def nextp():
            pi[0] += 1
            return pAB[pi[0] % 2], kAB[pi[0] % 2]

        def out_proj(l, Wout, old_tile, mixsel, ntile=18, after_wo=None):
            self.stage_begin()
            Wo = self.sb("Wo", [128, 8, 1024], BF16)
            for kc in range(8):
                dma("pool", lambda e, kc=kc: e.dma_start(out=Wo[:, kc, :], in_=Wout[kc * 128:(kc + 1) * 128, :]), "Wo", writes=["Wo"])
            if after_wo is not None:
                after_wo()
            xt = [self.sb(f"xt{i}", [128, 1024], F32) for i in range(3)]
            tm = [self.sb(f"tm{i}", [128, 1024], F32) for i in range(2)]
            MBg = self.sb("MBg", [128, 2, 1024], F32)
            dma("sp", lambda e: e.dma_start(out=MBg[:].rearrange("p w f -> p (w f)"), in_=mbscr2[l, :, 0, :]), "MBg", reads=[("mbscr", l, 0)], writes=["MBg"])
            def op_tile(tt):
                w = 0 if tt < 16 else 1
                xb, xk = xt[tt % 3], f"xt{tt % 3}"
                tb, tk = tm[tt % 2], f"tm{tt % 2}"
                ybk = [(pY[0], "pY0"), (pY[1], "pY1")] if tt % 2 == 0 else [(pA, "pA"), (pB, "pB")]
                src, skeys = old_tile(tt)
                dma("sp", lambda e: e.dma_start(out=xb[:], in_=src), xk, reads=skeys, writes=[xk])
                yield
                for nh in range(2):
                    yb, ybkk = ybk[nh]
                    for kc in range(8):
                        mt, mk = mixsel(kc)
                        op("pe", lambda e, nh=nh, kc=kc, mt=mt, yb=yb: e.matmul(yb[:], lhsT=mt[:, tt * 128:(tt + 1) * 128], rhs=Wo[:, kc, nh * 512:(nh + 1) * 512], start=(kc == 0), stop=(kc == 7)), reads=["Wo"] + mk, writes=[ybkk], pe_acc=True)
                    yield
                    op("dve", lambda e, nh=nh, yb=yb: e.tensor_tensor(out=tb[:, nh * 512:(nh + 1) * 512], in0=yb[:], in1=MBg[:, w, nh * 512:(nh + 1) * 512], op=ALU.mult), reads=[ybkk, "MBg"], writes=[tk])
                    yield
                op("dve", lambda e: e.tensor_tensor(out=xb[:], in0=xb[:], in1=tb[:], op=ALU.add), reads=[xk, tk], writes=[xk])
                yield
                dma("sp", lambda e: e.dma_start(out=hmid[tt * 128:(tt + 1) * 128, :], in_=xb[:]), f"hmst{tt}", reads=[xk], writes=[("hmid", tt)])
                yield
                yield from norm_gen(xb, xk, w, tt * 128, GS[l][2], GS[l][3], GS[l][6], GS[l][7], ntile=tb, nkey=tk)

            self.interleave([op_tile(tt) for tt in range(ntile)], 2)
            self.stage_end()

        def ffn_alloc():
            W1a = self.sb("W1a", [128, 8, 2048], BF16)
            W2h = self.sb("W2h", [128, 16, 1024], BF16)
            return W1a, W2h

        def ffn_load(l, half, W1, w1k, W2h):
            for kc in range(8):
                dma("pool", lambda e, kc=kc: e.dma_start(out=W1[:, kc, 0:2048], in_=w_ff1[l, kc * 128:(kc + 1) * 128, half * 2048:(half + 1) * 2048]), w1k, writes=[w1k])
            if W2h is not None:
                for fc in range(16):
                    dma("pool", lambda e, fc=fc: e.dma_start(out=W2h[:, fc, :], in_=w_ff2[l, half * 2048 + fc * 128:half * 2048 + (fc + 1) * 128, :]), "W2h", writes=["W2h"])

        def ffn(l, final_tile, W1a, W2h, ngrp=9):
            self.stage_begin()
            h1T = [self.sb(f"h1T{i}", [128, 16, 256], BF16) for i in range(2)]
            rl = [self.sb(f"rl{i}", [128, 256], F32) for i in range(2)]
            xt = [self.sb(f"xt{i}", [128, 1024], F32) for i in range(3)]
            tm = [self.sb(f"tm{i}", [128, 1024], F32) for i in range(2)]
            MBg = self.sb("MBg", [128, 2, 1024], F32)
            dma("sp", lambda e: e.dma_start(out=MBg[:].rearrange("p w f -> p (w f)"), in_=mbscr2[l, :, 1, :]), "MBg", reads=[("mbscr", l, 1)], writes=["MBg"])
            cnt = 0
            W1bufs = [(W1a, "W1h0"), (mixAB, "W1h1")]
            ffn_load(l, 1, mixAB, "W1h1", None)
            for half in range(2):
                W1h, w1k = W1bufs[half]
                if half == 1:
                    for fc in range(16):
                        dma("pool", lambda e, fc=fc: e.dma_start(out=W2h[:, fc, :], in_=w_ff2[l, 2048 + fc * 128:2048 + (fc + 1) * 128, :]), "W2h", writes=["W2h"])
                for grp in range(ngrp):
                    t0 = grp * 256
                    hbuf, hk = h1T[grp % 2], f"h1T{grp % 2}"
                    for fc in range(16):
                        pb, pk = nextp()
                        for kc in range(8):
                            op("pe", lambda e, kc=kc, fc=fc, t0=t0, pb=pb, W1h=W1h: e.matmul(pb[:, 0:256], lhsT=W1h[:, kc, fc * 128:(fc + 1) * 128], rhs=uT[:, kc, t0:t0 + 256], start=(kc == 0), stop=(kc == 7)), reads=[w1k] + uk(t0, t0 + 256), writes=[pk], pe_acc=True)
                        r_, rk = rl[fc % 2], f"rl{fc % 2}"
                        op("act", lambda e, pb=pb, r_=r_: e.activation(out=r_[:], in_=pb[:, 0:256], func=AF.Relu), reads=[pk], writes=[rk])
                        eng = "pool" if fc % 2 == 0 else "dve"
                        op(eng, lambda e, r_=r_, hbuf=hbuf, fc=fc: e.tensor_tensor(out=hbuf[:, fc, :], in0=r_[:], in1=r_[:], op=ALU.mult), reads=[rk], writes=[(hk, fc)])
                    for s_ in range(2):
                        tt = grp * 2 + s_
                        w = 0 if tt < 16 else 1
                        cnt += 1
                        xb, xk = xt[cnt % 3], f"xt{cnt % 3}"
                        tb, tk = tm[cnt % 2], f"tm{cnt % 2}"
                        dma("sp", lambda e, xb=xb, tt=tt: e.dma_start(out=xb[:], in_=hmid[tt * 128:(tt + 1) * 128, :]), xk, reads=[("hmid", tt)], writes=[xk])
                        for nh in range(2):
                            py, pyk = pY[s_ * 2 + nh], f"pY{s_ * 2 + nh}"
                            for fc in range(16):
                                op("pe", lambda e, nh=nh, fc=fc, s_=s_, hbuf=hbuf, py=py: e.matmul(py[:], lhsT=hbuf[:, fc, s_ * 128:(s_ + 1) * 128], rhs=W2h[:, fc, nh * 512:(nh + 1) * 512], start=(fc == 0), stop=(fc == 15)), reads=["W2h", (hk, fc)], writes=[pyk], pe_acc=True)
                            op("dve", lambda e, nh=nh, tb=tb, w=w, py=py: e.tensor_tensor(out=tb[:, nh * 512:(nh + 1) * 512], in0=py[:], in1=MBg[:, w, nh * 512:(nh + 1) * 512], op=ALU.mult), reads=[pyk, "MBg"], writes=[tk])
                        op("pool", lambda e, xb=xb, tb=tb: e.tensor_tensor(out=xb[:], in0=xb[:], in1=tb[:], op=ALU.add), reads=[xk, tk], writes=[xk])
                        if half == 0:
                            dma("sp", lambda e, xb=xb, tt=tt: e.dma_start(out=hmid[tt * 128:(tt + 1) * 128, :], in_=xb[:]), f"hmst{tt}", reads=[xk], writes=[("hmid", tt)])
                        else:
                            final_tile(tt, xb, xk, tb, tk)
            self.stage_end()

        def rv(ap_, d):
            return ap_ if d == 0 else ap_[:, ::-1]

        def layer0():
            self.stage_begin()
            mg0 = mod_gen(0, False, look=True)
            for _ in range(4):
                next(mg0)
            xt = [self.sb(f"xt{i}", [128, 1024], F32) for i in range(3)]

            def n1tile(t):
                xb = xt[t % 3]
                xk = f"xt{t % 3}"
                src = xh[t * 128:(t + 1) * 128, :] if t < 20 else ctxb[(t - 20) * 128:(t - 19) * 128, :]
                dma("sp", lambda e: e.dma_start(out=xb[:], in_=src), xk, writes=[xk])
                yield
                yield from norm_gen(xb, xk, 0 if t < 20 else 1, t * 128, G1, S1, "G1", "S1")

            chains = [n1tile(t) for t in range(22)]
            active, ndone = [], 0
            while chains or active:
                while chains and len(active) < 2:
                    active.append(chains.pop(0))
                nxt_ = []
                for g_ in active:
                    try:
                        next(g_)
                        nxt_.append(g_)
                    except StopIteration:
                        ndone += 1
                        if ndone % 5 == 0:
                            next(mg0, None)
                active = nxt_
            for _ in mg0:
                pass
            self.stage_end()

            self.stage_begin()
            xcv = self.sb("xcv", [128, 4, 2304], F32)
            gg = self.sb("gg", [128, 4, 2304], BF16)
            PS = self.sb("PS", [128, 4, 4], F32)
            CF = self.sb("CF", [128, 4, 2], F32)
            sumr = self.sb("sumr", [128, 2, 2], F32)
            cl = self.sb("cl", [128, 2, 4], F32)
            for d in range(2):
                op("act", lambda e, d=d: e.activation(out=cl[:, d, :], in_=V(f"lam{d}", 4), func=AF.Exp, scale=-1.0), reads=["vecs"], writes=["cl"])
            op("act", lambda e: e.activation(out=cl[:], in_=cl[:], func=AF.Ln, bias=1.0), reads=["cl"], writes=["cl"])
            op("dve", lambda e: e.tensor_scalar(out=cl[:], in0=cl[:], scalar1=-8.0, scalar2=None, op0=ALU.mult), reads=["cl"], writes=["cl"])
            self.stage_begin()
            Wxg = self.sb("Wxg", [128, 8, 1024], BF16)
            for kc in range(8):
                dma("pool", lambda e, kc=kc: e.dma_start(out=Wxg[:, kc, :], in_=w_in_e[kc * 128:(kc + 1) * 128, 1536:2560]), "Wxg", writes=["Wxg"])
            xls = self.sb("xls", [128, 2820], F32)
            op("pool", lambda e: e.memset(xls[:, 2560:2820], 0.0), writes=["xls"])
            for c in range(4):
                groups = [(g * 512, 512, g * 512) for g in range(5)] + [(2560, 256, 2562)]
                for (c0, n, d0) in groups:
                    pb, pk = nextp()
                    for kc in range(8):
                        op("pe", lambda e, kc=kc, c=c, c0=c0, n=n, pb=pb: e.matmul(pb[:, 0:n], lhsT=Wxg[:, kc, c * 128:(c + 1) * 128], rhs=uT[:, kc, c0:c0 + n], start=(kc == 0), stop=(kc == 7)), reads=["Wxg"] + uk(c0, c0 + n), writes=[pk], pe_acc=True)
                    op("act", lambda e, n=n, d0=d0, pb=pb: e.activation(out=xls[:, d0:d0 + n], in_=pb[:, 0:n], func=AF.Copy), reads=[pk], writes=["xls"])
                ggroups = [(256 + g * 512, 512, g * 512) for g in range(4)] + [(2560, 256, 2048)]
                for (c0, n, d0) in ggroups:
                    pb, pk = nextp()
                    for kc in range(8):
                        op("pe", lambda e, kc=kc, c=c, c0=c0, n=n, pb=pb: e.matmul(pb[:, 0:n], lhsT=Wxg[:, kc, 512 + c * 128:512 + (c + 1) * 128], rhs=uT[:, kc, c0:c0 + n], start=(kc == 0), stop=(kc == 7)), reads=["Wxg"] + uk(c0, c0 + n), writes=[pk], pe_acc=True)
                    op("act", lambda e, n=n, d0=d0, pb=pb, c=c: e.activation(out=gg[:, c, d0:d0 + n], in_=pb[:, 0:n], func=AF.Gelu), reads=[pk], writes=[("gg", c)])
                op("dve", lambda e: e.tensor_scalar(out=xls[:, 254:256], in0=xls[:, 254:256], scalar1=V("flag", 1, 0), scalar2=None, op0=ALU.mult), reads=["xls", "vecs"], writes=["xls"])
                op("dve", lambda e: e.tensor_scalar(out=xls[:, 2304:2305], in0=xls[:, 2304:2305], scalar1=V("flag", 1, 1), scalar2=None, op0=ALU.mult), reads=["xls", "vecs"], writes=["xls"])
                for (dst0, n, src0) in ((0, 2048, 256), (2048, 256, 2562)):
                    for j in range(4):
                        wj = V("cw", 1, c * 4 + j)
                        srcap = xls[:, src0 + j - 2:src0 + j - 2 + n]
                        dstap = xcv[:, c, dst0:dst0 + n]
                        if j == 0:
                            op("dve", lambda e, dstap=dstap, srcap=srcap, wj=wj, c=c: e.tensor_scalar(out=dstap, in0=srcap, scalar1=wj, scalar2=V("cb", 1, c), op0=ALU.mult, op1=ALU.add), reads=["xls", "vecs"], writes=[("xcv", c)])
                        else:
                            op("dve", lambda e, dstap=dstap, srcap=srcap, wj=wj: e.scalar_tensor_tensor(out=dstap, in0=srcap, scalar=wj, in1=dstap, op0=ALU.mult, op1=ALU.add), reads=["xls", "vecs", ("xcv", c)], writes=[("xcv", c)])
            if self.stop_after == "lruconv":
                self.dump("xcv", xcv[:], [128, 4, 2304], [("xcv", c) for c in range(4)])
                self.dump("gg", gg[:], [128, 4, 2304], [("gg", c) for c in range(4)], BF16)
                return True
            self.stage_end()
            self.stage_begin()
            WBD = self.sb("WBD", [128, 4, 2, 2, 128], F32)
            op("pool", lambda e: e.memset(WBD[:], 0.0), writes=["WBD"])
            for d in range(2):
                for hh_ in range(8):
                    for wi, src in enumerate((lru_wa, lru_wx)):
                        p0 = (hh_ % 2) * 64
                        dma("sp", lambda e, d=d, hh_=hh_, wi=wi, src=src, p0=p0: e.dma_start(out=WBD[p0:p0 + 64, hh_ // 2, d, wi, p0:p0 + 64], in_=src[d, hh_]), "WBD", writes=["WBD"])
            rr = self.sb("rr", [128, 2304], F32)
            aa = self.sb("aa", [128, 2304], F32)
            bb = self.sb("bb", [128, 2304], F32)
            hh = self.sb("hh", [128, 2304], F32)

            SEGS = [(0, 1024), (1024, 2048), (2048, 2304)]

            def seg_chain(c, d, si, need_sum):
                a0, a1 = SEGS[si]
                kr, kb, ka = ("rr", si), ("bb", si), ("aa", si)
                for c0 in range(a0, a1, 512):
                    n = min(512, a1 - c0)
                    for wi, (dst, dk, bn) in enumerate(((rr, kr, f"ba{d}"), (bb, kb, f"bx{d}"))):
                        pb, pk = nextp()
                        op("pe", lambda e, wi=wi, c0=c0, n=n, pb=pb: e.matmul(pb[:, 0:n], lhsT=WBD[:, c, d, wi, :], rhs=xcv[:, c, c0:c0 + n], start=True, stop=True), reads=["WBD", ("xcv", c)], writes=[pk])
                        op("act", lambda e, dst=dst, c0=c0, n=n, pb=pb, bn=bn: e.activation(out=dst[:, c0:c0 + n], in_=pb[:, 0:n], func=AF.Sigmoid, bias=V(bn, 1, c)), reads=[pk, "vecs"], writes=[dk])
                        yield
                if need_sum and si < 2:
                    op("dve", lambda e: e.tensor_reduce(out=sumr[:, d, si:si + 1], in_=rr[:, a0:a1], axis=AX.X, op=ALU.add), reads=[kr], writes=[("sumr", d, si)])
                    yield
                op("dve", lambda e: e.tensor_tensor(out=bb[:, a0:a1], in0=bb[:, a0:a1], in1=xcv[:, c, a0:a1], op=ALU.mult), reads=[kb, ("xcv", c)], writes=[kb])
                yield
                op("act", lambda e: e.activation(out=aa[:, a0:a1], in_=rr[:, a0:a1], func=AF.Exp, scale=cl[:, d, c:c + 1]), reads=[kr, "cl"], writes=[ka])
                yield
                op("pool", lambda e: e.tensor_tensor(out=rr[:, a0:a1], in0=aa[:, a0:a1], in1=aa[:, a0:a1], op=ALU.mult), reads=[ka], writes=[kr])
                yield
                op("act", lambda e: e.activation(out=rr[:, a0:a1], in_=rr[:, a0:a1], func=AF.Sqrt, scale=-1.0, bias=1.0), reads=[kr], writes=[kr])
                yield
                op("pool", lambda e: e.tensor_tensor(out=bb[:, a0:a1], in0=bb[:, a0:a1], in1=rr[:, a0:a1], op=ALU.mult), reads=[kr, kb], writes=[kb])
                yield

            def seg_scans(c, d, dst, dn, init_ap):
                sc_ = lambda a0, a1, ini: (lambda e: e.tensor_tensor_scan(out=rv(dst[:, a0:a1], d), data0=rv(aa[:, a0:a1], d), data1=rv(bb[:, a0:a1], d), initial=ini, op0=ALU.mult, op1=ALU.add))
                op("dve", sc_(2048, 2304, 0.0), reads=[("aa", 2), ("bb", 2)], writes=[(dn, 2)])
                order = [0, 1] if d == 0 else [1, 0]
                s0, s1 = order
                a0, a1 = SEGS[s0]
                extra = ["hin"] if init_ap is not None else []
                op("dve", sc_(a0, a1, init_ap if init_ap is not None else 0.0), reads=[("aa", s0), ("bb", s0)] + extra, writes=[(dn, s0)])
                carry = a1 - 1 if d == 0 else a0
                b0, b1 = SEGS[s1]
                op("dve", sc_(b0, b1, dst[:, carry:carry + 1]), reads=[("aa", s1), ("bb", s1), (dn, s0)], writes=[(dn, s1)])

            def lru_cd(c, d, need_sum):
                order = [2, 0, 1] if d == 0 else [2, 1, 0]
                self.interleave([seg_chain(c, d, si, need_sum) for si in order], 3)

            for c in range(4):
                for d in range(2):
                    lru_cd(c, d, True)
                    seg_scans(c, d, hh, "hh", None)
                    last = 2303 if d == 0 else 2048
                    op("dve", lambda e, c=c, d=d, last=last: e.tensor_copy(out=CF[:, c, d:d + 1], in_=hh[:, last:last + 1]), reads=[("hh", 2)], writes=["CF"])
                    last = 2047 if d == 0 else 0
                    op("dve", lambda e, c=c, d=d, last=last: e.tensor_copy(out=PS[:, c, 2 * d + 1:2 * d + 2], in_=hh[:, last:last + 1]), reads=[("hh", 0), ("hh", 1)], writes=["PS"])
                    op("dve", lambda e, d=d: e.tensor_tensor(out=sumr[:, d, 0:1], in0=sumr[:, d, 0:1], in1=sumr[:, d, 1:2], op=ALU.add), reads=[("sumr", d, 0), ("sumr", d, 1)], writes=[("sumr", d, 0)])
                    op("act", lambda e, c=c, d=d: e.activation(out=PS[:, c, 2 * d:2 * d + 1], in_=sumr[:, d, 0:1], func=AF.Exp, scale=cl[:, d, c:c + 1]), reads=[("sumr", d, 0), "cl"], writes=["PS"])
            PSall = self.sb("PSall", [128, 4, 16], F32)
            dma("sp", lambda e: e.dma_start(out=cc1_in, in_=PS[:].rearrange("p c q -> p (c q)")), "cc1a", reads=["PS"], writes=["cc1_in"])
            dma("pool", lambda e: e.collective_compute("AllGather", ALU.bypass, replica_groups=RG, ins=[cc1_in.opt()], outs=[cc1_out.opt()]), "cc1", reads=["cc1_in"], writes=["cc1_out"], inc=1)
            dma("sp", lambda e: e.dma_start(out=PSall[:], in_=cc1_out.rearrange("(r p) f -> p r f", p=128)), "cc1b", reads=["cc1_out"], writes=["PSall"])
            hin = self.sb("hin", [128, 2, 4], F32)
            hcur = self.sb("hcur", [128, 4], F32)
            PSv = PSall[:].rearrange("p r (c q) -> p r c q", q=4)
            for d in range(2):
                order = [0, 1, 2, 3] if d == 0 else [3, 2, 1, 0]
                op("dve", lambda e, d=d: e.tensor_copy(out=hcur[:], in_=CF[:, :, d]), reads=["CF"], writes=["hcur"])
                op("dve", lambda e, d=d, r0=order[0]: e.tensor_scalar(out=hin[:, d, :], in0=hcur[:], scalar1=V("sel", 1, r0), scalar2=None, op0=ALU.mult), reads=["hcur", "vecs"], writes=["hin"])
                for i_ in range(3):
                    r = order[i_]
                    rn = order[i_ + 1]
                    op("dve", lambda e, d=d, r=r: e.tensor_tensor(out=hcur[:], in0=hcur[:], in1=PSv[:, r, :, 2 * d], op=ALU.mult), reads=["hcur", "PSall"], writes=["hcur"])
                    op("dve", lambda e, d=d, r=r: e.tensor_tensor(out=hcur[:], in0=hcur[:], in1=PSv[:, r, :, 2 * d + 1], op=ALU.add), reads=["hcur", "PSall"], writes=["hcur"])
                    op("dve", lambda e, d=d, rn=rn: e.scalar_tensor_tensor(out=hin[:, d, :], in0=hcur[:], scalar=V("sel", 1, rn), in1=hin[:, d, :], op0=ALU.mult, op1=ALU.add), reads=["hcur", "hin", "vecs"], writes=["hin"])
            for c in range(4):
                for d in range(2):
                    lru_cd(c, d, False)
                    if d == 0:
                        seg_scans(c, d, hh, "hh", hin[:, d, c:c + 1])
                    else:
                        seg_scans(c, d, rr, "rr", hin[:, d, c:c + 1])
                allk = [("hh", i) for i in range(3)] + [("rr", i) for i in range(3)]
                op("dve", lambda e: e.tensor_tensor(out=hh[:], in0=hh[:], in1=rr[:], op=ALU.add), reads=allk, writes=[("hh", i) for i in range(3)])
                op("pool", lambda e, c=c: e.tensor_tensor(out=mixB[:, c, :], in0=hh[:], in1=gg[:, c, :], op=ALU.mult), reads=[("hh", i) for i in range(3)] + [("gg", c)], writes=[("mixB", c)])
            if self.stop_after == "lru":
                self.dump("mixB", mixB[:], [128, 4, 2304], [("mixB", c) for c in range(4)], BF16)
                self.dump("hin", hin[:], [128, 2, 4], ["hin"])
                return True
            self.stage_end()
            self.stage_end()
            self.stage_begin()
            Wv = self.sb("Wv", [128, 8, 512], BF16)
            Vaug = self.sb("Vaug", [128, 22, 8, 66], BF16)
            op("pool", lambda e: e.memset(Vaug[:], 1.0), writes=["Vaug"])
            for kc in range(8):
                dma("pool", lambda e, kc=kc: e.dma_start(out=Wv[:, kc, :], in_=w_in_e[kc * 128:(kc + 1) * 128, 1024:1536]), "Wv", writes=["Wv"])
            QT = [self.sb(f"QT{i}", [128, 2304], BF16) for i in range(2)]
            KT = [self.sb(f"KT{i}", [128, 2816], BF16) for i in range(2)]
            Wqk = [self.sb(f"Wqk{i}", [128, 8, 128], BF16) for i in range(2)]
            tab = [self.sb(f"tab{i}", [128, 22 * 64], BF16) for i in range(2)]
            PT = [self.sb(f"PT{i}", [128, 512], BF16) for i in range(3)]
            rec = self.sb("rec", [128, 512], F32)
            Ou = self.sb("Ou", [64, 512], F32)
            mtmp = self.sb("mtmp", [64, 2304], BF16)
            for i in range(2):
                op("pool", lambda e, i=i: e.memset(QT[i][64:104, 2048:2304], 0.0), writes=[f"QTa{i}"])
                op("pool", lambda e, i=i: e.memset(KT[i][64:104, 2560:2816], 0.0), writes=[f"KTa{i}"])
                dma("pool", lambda e, i=i: e.dma_start(out=QT[i][64:104, 0:2048], in_=qaug_d.ap()), f"QTa{i}", reads=[f"QTa{i}"], writes=[f"QTa{i}"])
                dma("pool", lambda e, i=i: e.dma_start(out=KT[i][64:104, 0:2560], in_=kaug_d.ap()), f"KTa{i}", reads=[f"KTa{i}"], writes=[f"KTa{i}"])
            for kt in range(22):
                c0 = kt * 128
                pb, pk = nextp()
                for kc in range(8):
                    op("pe", lambda e, kc=kc, c0=c0, pb=pb: e.matmul(pb[:], lhsT=uT[:, kc, c0:c0 + 128], rhs=Wv[:, kc, :], start=(kc == 0), stop=(kc == 7)), reads=["Wv"] + uk(c0, c0 + 128), writes=[pk], pe_acc=True)
                src = pb[:].rearrange("p (h d) -> p h d", d=64)
                if kt % 2 == 0:
                    op("act", lambda e, kt=kt, src=src: e.activation(out=Vaug[:, kt, :, 0:64], in_=src, func=AF.Copy), reads=[pk], writes=[("Vaug", kt)])
                else:
                    op("dve", lambda e, kt=kt, src=src: e.tensor_copy(out=Vaug[:, kt, :, 0:64], in_=src), reads=[pk], writes=[("Vaug", kt)])
            SB = [(pA, "pA"), (pB, "pB"), (pY[0], "pY0"), (pY[1], "pY1")]
            si = [0]
            pti = [0]

            PT4 = PT + [self.sb(f"PT3x{i}", [128, 512], BF16) for i in range(5)]
            recs = [rec, self.sb("rec_b", [128, 512], F32)]
            Ous = [Ou, self.sb("Ou_b", [64, 512], F32)]

            def na_head(h, hb_):
                items = []
                for qb in range(4):
                    kts = [(4 * qb + j, j) for j in range(8)] + [(20, None), (21, None)]
                    for idx, (kt, j) in enumerate(kts):
                        items.append((qb, qb * 512, 512, kt, j, idx == 0, idx == len(kts) - 1))
                for idx, kt in enumerate((20, 21)):
                    items.append((4, 2048, 256, kt, None, idx == 0, idx == 1))
                pend = {}
                LA = 3

                def qk(i):
                    blk, q0, nq, kt, j, first, last = items[i]
                    Sp, Sk = SB[i % 4]
                    P_, Pk = PT4[i % 8], f"PTn{i % 8}"
                    if j is not None:
                        op("pe", lambda e: e.matmul(Sp[:, 0:nq], lhsT=KT[hb_][0:104, kt * 128:(kt + 1) * 128], rhs=QT[hb_][0:104, q0:q0 + nq], start=True, stop=False), reads=[f"KT{hb_}", f"KTa{hb_}", f"QT{hb_}", f"QTa{hb_}"], writes=[Sk])
                        t0 = (14 - 2 * j) * 64
                        op("pe", lambda e: e.matmul(Sp[:, 0:nq], lhsT=ident_b[:], rhs=tab[hb_][:, t0:t0 + nq], start=False, stop=True), reads=["ident_b", f"tab{hb_}"], writes=[Sk], pe_acc=True)
                    else:
                        op("pe", lambda e: e.matmul(Sp[:, 0:nq], lhsT=KT[hb_][0:104, kt * 128:(kt + 1) * 128], rhs=QT[hb_][0:104, q0:q0 + nq], start=True, stop=True), reads=[f"KT{hb_}", f"KTa{hb_}", f"QT{hb_}", f"QTa{hb_}"], writes=[Sk])
                    op("act", lambda e: e.activation(out=P_[:, 0:nq], in_=Sp[:, 0:nq], func=AF.Exp), reads=[Sk], writes=[Pk])

                def pv(i, step):
                    blk, q0, nq, kt, j, first, last = items[i]
                    O, Ok = pY[2 + blk % 2], f"pY{2 + blk % 2}"
                    P_, Pk = PT4[i % 8], f"PTn{i % 8}"
                    op("pe", lambda e: e.matmul(O[0:66, 0:nq], lhsT=Vaug[:, kt, h, :], rhs=P_[:, 0:nq], start=first, stop=last), reads=[("Vaug", kt), Pk], writes=[Ok], pe_acc=True)
                    if last:
                        rec, rk_ = recs[blk % 2], f"rec{blk % 2}"
                        Ou, ok_ = Ous[blk % 2], f"Ou{blk % 2}"
                        op("dve", lambda e: e.reciprocal(out=rec[64:65, 0:nq], in_=O[64:65, 0:nq]), reads=[Ok], writes=[rk_])
                        op("act", lambda e: e.activation(out=Ou[:, 0:nq], in_=O[0:64, 0:nq], func=AF.Copy), reads=[Ok], writes=[ok_])

                        def fin():
                            op("pe", lambda e: e.matmul(pT[0:64, 0:nq], lhsT=ones_f[64:65, 0:64], rhs=rec[64:65, 0:nq], start=True, stop=True), reads=["ones_f", rk_], writes=["pT"])
                            if h % 2 == 0:
                                op("dve", lambda e: e.tensor_tensor(out=mixA[0:64, h // 2, q0:q0 + nq], in0=Ou[:, 0:nq], in1=pT[0:64, 0:nq], op=ALU.mult), reads=[ok_, "pT"], writes=[("mixA", h // 2, 0)])
                            else:
                                op("dve", lambda e: e.tensor_tensor(out=mtmp[:, q0:q0 + nq], in0=Ou[:, 0:nq], in1=pT[0:64, 0:nq], op=ALU.mult), reads=[ok_, "pT"], writes=["mtmp"])
                        pend.setdefault(step + 2, []).append(fin)

                nI = len(items)
                for step in range(nI + LA + 4):
                    if step < nI:
                        qk(step)
                    if 0 <= step - LA < nI:
                        pv(step - LA, step)
                    for f_ in pend.pop(step, []):
                        f_()

            mg1 = mod_gen(1, False)
            for h in range(8):
                hb_ = h % 2
                dma("pool", lambda e, h=h, hb_=hb_: e.dma_start(out=Wqk[hb_][:, :, 0:64], in_=w_in_e[:, h * 64:(h + 1) * 64].rearrange("(kc p) n -> p kc n", p=128)), f"Wqk{hb_}", writes=[f"Wqk{hb_}"])
                dma("pool", lambda e, h=h, hb_=hb_: e.dma_start(out=Wqk[hb_][:, :, 64:128], in_=w_in_e[:, 512 + h * 64:512 + (h + 1) * 64].rearrange("(kc p) n -> p kc n", p=128)), f"Wqk{hb_}", writes=[f"Wqk{hb_}"])
                dma("pool", lambda e, h=h, hb_=hb_: e.dma_start(out=tab[hb_][:], in_=natab_d[:, h, :]), f"tab{hb_}", writes=[f"tab{hb_}"])
                for (c0, n, d0) in [(256 + g * 512, 512, g * 512) for g in range(4)] + [(2560, 256, 2048)]:
                    pb, pk = nextp()
                    for kc in range(8):
                        op("pe", lambda e, kc=kc, c0=c0, n=n, pb=pb, hb_=hb_: e.matmul(pb[0:64, 0:n], lhsT=Wqk[hb_][:, kc, 0:64], rhs=uT[:, kc, c0:c0 + n], start=(kc == 0), stop=(kc == 7)), reads=[f"Wqk{hb_}"] + uk(c0, c0 + n), writes=[pk], pe_acc=True)
                    op("act", lambda e, n=n, d0=d0, pb=pb, hb_=hb_: e.activation(out=QT[hb_][0:64, d0:d0 + n], in_=pb[0:64, 0:n], func=AF.Copy, scale=0.125), reads=[pk], writes=[f"QT{hb_}"])
                for (c0, n) in [(g * 512, 512) for g in range(5)] + [(2560, 256)]:
                    pb, pk = nextp()
                    for kc in range(8):
                        op("pe", lambda e, kc=kc, c0=c0, n=n, pb=pb, hb_=hb_: e.matmul(pb[0:64, 0:n], lhsT=Wqk[hb_][:, kc, 64:128], rhs=uT[:, kc, c0:c0 + n], start=(kc == 0), stop=(kc == 7)), reads=[f"Wqk{hb_}"] + uk(c0, c0 + n), writes=[pk], pe_acc=True)
                    op("dve", lambda e, n=n, c0=c0, pb=pb, hb_=hb_: e.tensor_copy(out=KT[hb_][0:64, c0:c0 + n], in_=pb[0:64, 0:n]), reads=[pk], writes=[f"KT{hb_}"])
                na_head(h, hb_)
                next(mg1, None)
                if h % 2 == 1:
                    dma("sp", lambda e, h=h: e.dma_start(out=mixA[64:128, h // 2, :], in_=mtmp[:, :]), "mixAup", reads=["mtmp"], writes=[("mixA", h // 2, 1)])
            for _ in mg1:
                pass
            if self.stop_after == "na":
                self.dump("mixA", mixA[:], [128, 4, 2304], [("mixA", c, q) for c in range(4) for q in range(2)], BF16)
                return True
            self.stage_end()
            def old0(tt):
                return (xh[256 + tt * 128:256 + (tt + 1) * 128, :] if tt < 16 else ctxb[(tt - 16) * 128:(tt - 15) * 128, :]), []

            def mix0(kc):
                if kc < 4:
                    return mixA[:, kc, :], [("mixA", kc, 0), ("mixA", kc, 1)]
                return mixB[:, kc - 4, :], [("mixB", kc - 4)]

            self.stage_begin()
            W1a0, W2h0 = ffn_alloc()
            out_proj(0, w_out_e, old0, mix0, after_wo=lambda: ffn_load(0, 0, W1a0, "W1h0", W2h0))
            if self.stop_after == "p4":
                self.dump("u2T", uT[:, :, 0:2304], [128, 8, 2304], uk(0, 2304), BF16)
                return True

            def fin0(tt, xb, xk, tb, tk):
                dma("sp", lambda e, xb=xb, tt=tt: e.dma_start(out=h1[tt * 128:(tt + 1) * 128, :], in_=xb[:]), "h1st_" + xk, reads=[xk], writes=[("h1", tt)])

            ffn(0, fin0, W1a0, W2h0)
            self.stage_end()
            if self.stop_after == "l0":
                if "h1" in self.debug:
                    d = self.dout("dbg_h1", [2304, 1024])
                    self.dbg_out["h1"] = d
                    S.dma("sp", lambda e: e.dma_start(out=d, in_=h1), "dbg", reads=[("h1", t) for t in range(18)], writes=[("dbg", "h1")])
                return True

        if not self.skip_l0:
            if layer0():
                return self.finish()
        else:
            self.stage_begin()
            xtz = [self.sb(f"xtz{i}", [128, 1024], F32) for i in range(2)]
            for tt in range(18):
                xb, xk = xtz[tt % 2], f"xtz{tt % 2}"
                src = xh[256 + tt * 128:256 + (tt + 1) * 128, :] if tt < 16 else ctxb[(tt - 16) * 128:(tt - 15) * 128, :]
                dma("sp", lambda e, xb=xb, src=src: e.dma_start(out=xb[:], in_=src), xk, writes=[xk])
                dma("sp", lambda e, xb=xb, tt=tt: e.dma_start(out=h1[tt * 128:(tt + 1) * 128, :], in_=xb[:]), f"xtzs{tt % 2}", reads=[xk], writes=[("h1", tt)])
            self.stage_end()
        self.stage_begin()
        xt = [self.sb(f"xt{i}", [128, 1024], F32) for i in range(3)]
        def n1btile(tt):
            xb, xk = xt[tt % 3], f"xt{tt % 3}"
            dma("sp", lambda e: e.dma_start(out=xb[:], in_=h1[tt * 128:(tt + 1) * 128, :]), xk, reads=[("h1", tt)], writes=[xk])
            yield
            yield from norm_gen(xb, xk, 0 if tt < 16 else 1, tt * 128, G1b, S1b, "G1b", "S1b")

        self.interleave([n1btile(tt) for tt in range(18)], 2)
        self.stage_end()

        self.stage_begin()
        Wq = self.sb("Wq", [128, 8, 512], BF16)
        Wkv = self.sb("Wkv", [128, 8, 256], BF16)
        for kc in range(8):
            dma("pool", lambda e, kc=kc: e.dma_start(out=Wq[:, kc, :], in_=w_in_o[kc * 128:(kc + 1) * 128, 1568:2080]), "Wq", writes=["Wq"])
            dma("pool", lambda e, kc=kc: e.dma_start(out=Wkv[:, kc, :], in_=w_in_o[kc * 128:(kc + 1) * 128, 2080:2336]), "Wkv", writes=["Wkv"])
        qnw = self.sb("qnw", [128, 64], F32)
        knw = self.sb("knw", [128, 64], F32)
        dma("sp", lambda e: e.dma_start(out=qnw[:], in_=qnw_d.ap()), "qnw", writes=["qnw"])
        dma("sp", lambda e: e.dma_start(out=knw[:], in_=knw_d.ap()), "knw", writes=["knw"])
        cosb = [self.sb(f"cosb{i}", [128, 64], F32) for i in range(2)]
        sinb = [self.sb(f"sinb{i}", [128, 2, 32], F32) for i in range(2)]
        wkA = [[self.sb(f"wk{j}_{i}", [128, 512], F32) for i in range(5)] for j in range(2)]
        ssbA = [self.sb(f"ssb{j}", [128, 4, 8], F32) for j in range(2)]
        wkc = [0]
        QTst = [self.sb(f"QTst{i}", [64, 8, 128], BF16) for i in range(2)]
        KTst = [self.sb(f"KTst{i}", [64, 2, 128], BF16) for i in range(2)]
        Vst = [self.sb(f"Vst{i}", [128, 128], BF16) for i in range(2)]

        def qk_gen(ps_ap, pkey, H, wbc, wbk, rope, tb):
            n = H * 64
            wkc[0] += 1
            j_ = wkc[0] % 2
            sq, qn, A_, t1, ob = wkA[j_]
            ssb = ssbA[j_]
            kk = [f"wk{j_}_{i}" for i in range(5)]
            sk_ = f"ssb{j_}"
            ps3 = ps_ap.rearrange("p (h d) -> p h d", d=64)
            op("act", lambda e: e.activation(out=sq[:, 0:n], in_=ps_ap, func=AF.Square), reads=[pkey], writes=[kk[0]])
            yield
            op("dve", lambda e: e.tensor_reduce(out=ssb[:, 0, 0:H], in_=sq[:, 0:n].rearrange("p (h d) -> p h d", d=64), axis=AX.X, op=ALU.add), reads=[kk[0]], writes=[sk_])
            yield
            op("dve", lambda e: e.tensor_scalar(out=ssb[:, 1, 0:H], in0=ssb[:, 0, 0:H], scalar1=1.0 / 64, scalar2=EPS, op0=ALU.mult, op1=ALU.add), reads=[sk_], writes=[sk_])
            yield
            op("act", lambda e: e.activation(out=ssb[:, 2, 0:H], in_=ssb[:, 1, 0:H], func=AF.Sqrt), reads=[sk_], writes=[sk_])
            yield
            op("dve", lambda e: e.reciprocal(out=ssb[:, 3, 0:H], in_=ssb[:, 2, 0:H]), reads=[sk_], writes=[sk_])
            yield
            qn3 = qn[:, 0:n].rearrange("p (h d) -> p h d", d=64)
            op("dve", lambda e: e.tensor_tensor(out=qn3, in0=ps3, in1=ssb[:, 3, 0:H].unsqueeze(2).to_broadcast([128, H, 64]), op=ALU.mult), reads=[pkey, sk_], writes=[kk[1]])
            yield
            op("pool", lambda e: e.tensor_tensor(out=qn3, in0=qn3, in1=wbc[:].unsqueeze(1).to_broadcast([128, H, 64]), op=ALU.mult), reads=[kk[1], wbk], writes=[kk[1]])
            yield
            if not rope:
                return qn, kk[1]
            cb_, sb_ = cosb[tb], sinb[tb]
            A3 = A_[:, 0:n].rearrange("p (h d) -> p h d", d=64)
            op("dve", lambda e: e.tensor_tensor(out=A3, in0=qn3, in1=cb_[:].unsqueeze(1).to_broadcast([128, H, 64]), op=ALU.mult), reads=[kk[1], f"cosb{tb}"], writes=[kk[2]])
            yield
            qv = qn[:, 0:n].rearrange("p (h f s x) -> p h f s x", f=2, s=2, x=16)
            tv = t1[:, 0:n].rearrange("p (h f s x) -> p h f s x", f=2, s=2, x=16)
            sin3 = sb_[:, 0, :].rearrange("p (f x) -> p f x", x=16).unsqueeze(1).to_broadcast([128, H, 2, 16])
            nsin3 = sb_[:, 1, :].rearrange("p (f x) -> p f x", x=16).unsqueeze(1).to_broadcast([128, H, 2, 16])
            op("pool", lambda e: e.tensor_tensor(out=tv[:, :, :, 0, :], in0=qv[:, :, :, 1, :], in1=nsin3, op=ALU.mult), reads=[kk[1], f"sinb{tb}"], writes=[kk[3]])
            yield
            op("pool", lambda e: e.tensor_tensor(out=tv[:, :, :, 1, :], in0=qv[:, :, :, 0, :], in1=sin3, op=ALU.mult), reads=[kk[1], f"sinb{tb}"], writes=[kk[3]])
            yield
            op("dve", lambda e: e.tensor_tensor(out=ob[:, 0:n], in0=A_[:, 0:n], in1=t1[:, 0:n], op=ALU.add), reads=[kk[2], kk[3]], writes=[kk[4]])
            yield
            return ob, kk[4]

        def q_chain(tt):
            tb = tt % 2
            dma("sp", lambda e: e.dma_start(out=cosb[tb][:], in_=cosT_d[tt * 128:(tt + 1) * 128, :]), f"cosb{tb}", writes=[f"cosb{tb}"])
            dma("sp", lambda e: e.dma_start(out=sinb[tb][:, 0, :], in_=sinT_d[tt * 128:(tt + 1) * 128, :]), f"sinb{tb}", writes=[f"sinb{tb}"])
            dma("sp", lambda e: e.dma_start(out=sinb[tb][:, 1, :], in_=nsinT_d[tt * 128:(tt + 1) * 128, :]), f"sinb{tb}", writes=[f"sinb{tb}"])
            yield
            pb, pk = nextp()
            for kc in range(8):
                op("pe", lambda e, kc=kc: e.matmul(pb[:, 0:512], lhsT=uT[:, kc, tt * 128:(tt + 1) * 128], rhs=Wq[:, kc, :], start=(kc == 0), stop=(kc == 7)), reads=["Wq"] + uk(tt * 128, tt * 128 + 128), writes=[pk], pe_acc=True)
            yield
            qr, qrk = yield from qk_gen(pb[:, 0:512], pk, 8, qnw, "qnw", True, tb)
            for h in range(8):
                op("pe", lambda e, h=h: e.transpose(out=pT[0:64, h * 128:(h + 1) * 128], in_=qr[:, h * 64:(h + 1) * 64], identity=ident_f[:]), reads=[qrk, "ident_f"], writes=["pT"], pe_acc=True)
            yield
            op("act", lambda e: e.activation(out=QTst[tb][:], in_=pT[0:64, :].rearrange("p (h t) -> p h t", t=128), func=AF.Copy), reads=["pT"], writes=[f"QTst{tb}"])
            yield
            dma("sp", lambda e: e.dma_start(out=qscr[:, :, tt * 128:(tt + 1) * 128], in_=QTst[tb][:]), f"QTst{tb}", reads=[f"QTst{tb}"], writes=["qscr"])
            yield

        def kv_chain(tt):
            tb = tt % 2
            lat = tt < 16
            pb, pk = nextp()
            for kc in range(8):
                op("pe", lambda e, kc=kc: e.matmul(pb[:, 0:256], lhsT=uT[:, kc, tt * 128:(tt + 1) * 128], rhs=Wkv[:, kc, :], start=(kc == 0), stop=(kc == 7)), reads=["Wkv"] + uk(tt * 128, tt * 128 + 128), writes=[pk], pe_acc=True)
            yield
            op("act", lambda e: e.activation(out=Vst[tb][:], in_=pb[:, 128:256], func=AF.Copy), reads=[pk], writes=[f"Vst{tb}"])
            yield
            if lat:
                dma("sp", lambda e: e.dma_start(out=cc2v_in[tt * 128:(tt + 1) * 128, :], in_=Vst[tb][:]), f"Vst{tb}", reads=[f"Vst{tb}"], writes=["cc2v_in"])
            else:
                dma("sp", lambda e: e.dma_start(out=vctx[(tt - 16) * 128:(tt - 15) * 128, :], in_=Vst[tb][:]), f"Vst{tb}", reads=[f"Vst{tb}"], writes=["vctx"])
            yield
            kr, krk = yield from qk_gen(pb[:, 0:128], pk, 2, knw, "knw", lat, tb)
            for h in range(2):
                op("pe", lambda e, h=h: e.transpose(out=pY[2][0:64, h * 128:(h + 1) * 128], in_=kr[:, h * 64:(h + 1) * 64], identity=ident_f[:]), reads=[krk, "ident_f"], writes=["pY2"], pe_acc=True)
            yield
            op("act", lambda e: e.activation(out=KTst[tb][:], in_=pY[2][0:64, 0:256].rearrange("p (h t) -> p h t", t=128), func=AF.Copy), reads=["pY2"], writes=[f"KTst{tb}"])
            yield
            if lat:
                dma("sp", lambda e: e.dma_start(out=cc2k_in.rearrange("p (h t) -> p h t", h=2)[:, :, tt * 128:(tt + 1) * 128], in_=KTst[tb][:]), f"KTst{tb}", reads=[f"KTst{tb}"], writes=["cc2k_in"])
            else:
                dma("sp", lambda e: e.dma_start(out=kctx[:, :, (tt - 16) * 128:(tt - 15) * 128], in_=KTst[tb][:]), f"KTst{tb}", reads=[f"KTst{tb}"], writes=["kctx"])
            yield

        chains = []
        for tt in range(18):
            if tt < 16:
                chains.append(q_chain(tt))
            chains.append(kv_chain(tt))
        self.interleave(chains, 2)
        dma("pool", lambda e: e.collective_compute("AllGather", ALU.bypass, replica_groups=RG, ins=[cc2k_in.opt()], outs=[cc2k_out.opt()]), "cc2k", reads=["cc2k_in"], writes=["cc2k_out"], inc=1)
        dma("pool", lambda e: e.collective_compute("AllGather", ALU.bypass, replica_groups=RG, ins=[cc2v_in.opt()], outs=[cc2v_out.opt()]), "cc2v", reads=["cc2v_in"], writes=["cc2v_out"], inc=1)
        self.stage_end()
        self.stage_begin()
        Wgqk = self.sb("Wgqk", [128, 8, 512], BF16)
        for kc in range(8):
            dma("pool", lambda e, kc=kc: e.dma_start(out=Wgqk[:, kc, :], in_=w_in_o[kc * 128:(kc + 1) * 128, 0:512]), "Wgqk", writes=["Wgqk"])
        vtok = self.sb("vtok", [128, 18, 512], BF16)
        OL = mixA
        qd = self.sb("qd", [128, 2, 2, 2048], BF16)
        lrT = self.sb("lrT", [16, 2, 2304], BF16)
        PK = self.sb("PK", [128, 2, 2, 129], F32)
        Sctx = self.sb("Sctx", [128, 2, 2, 128], F32)
        cm = self.sb("cm", [128, 2304], BF16)
        op("pool", lambda e: e.memset(cm[:], 1.0), writes=["cm"])
        op("pool", lambda e: e.memset(cm[:].rearrange("p (c t) -> p c t", t=64)[:, :, 0:1], 0.0), reads=["cm"], writes=["cm"])
        gmask = self.sb("gmask", [128, 4, 128], F32)
        dma("sp", lambda e: e.dma_start(out=gmask[:], in_=gmask_d.ap()), "gmask", writes=["gmask"])
        wa2 = self.sb("wa2", [16, 2, 256], BF16)
        for d in range(2):
            dma("pool", lambda e, d=d: e.dma_start(out=wa2[:, d, :], in_=gla_wa2_d[d]), "wa2", writes=["wa2"])
        barow = self.sb("barow", [1, 512], F32)
        dma("sp", lambda e: e.dma_start(out=barow[:], in_=barow_d.ap()), "barow", writes=["barow"])
        nba = self.sb("nba", [128, 2, 2], F32)
        for d in range(2):
            op("dve", lambda e, d=d: e.tensor_scalar(out=nba[:, d, :], in0=V(f"gba{d}", 2), scalar1=-1.0, scalar2=None, op0=ALU.mult), reads=["vecs"], writes=["nba"])
        ones_b = self.sb("ones_b", [128, 128], BF16)
        op("pool", lambda e: e.tensor_copy(out=ones_b[:], in_=ones_f[:]), reads=["ones_f"], writes=["ones_b"])
        Sst = self.sb("Sst", [128, 128], F32)
        Sbf = [self.sb(f"Sbf{i}", [128, 128], BF16) for i in range(2)]
        PTl4 = [self.sb(f"PTl{i}", [128, 128], BF16) for i in range(4)]
        G5 = [(g * 512, 512) for g in range(4)] + [(2048, 256)]
        if self.stop_after == "gla0":
            self.dump("cm", cm[:], [128, 2304], ["cm", "gmask", "wa2", "barow", "nba", "ones_b"], BF16)
            return self.finish()
        self.stage_begin()
        Wgv = self.sb("Wgv", [128, 8, 512], BF16)
        Wlr = self.sb("Wlr", [128, 8, 32], BF16)
        for kc in range(8):
            dma("pool", lambda e, kc=kc: e.dma_start(out=Wgv[:, kc, :], in_=w_in_o[kc * 128:(kc + 1) * 128, 512:1024]), "Wgv", writes=["Wgv"])
            dma("pool", lambda e, kc=kc: e.dma_start(out=Wlr[:, kc, :], in_=w_in_o[kc * 128:(kc + 1) * 128, 1536:1568]), "Wlr", writes=["Wlr"])
        for tt in range(18):
            pb, pk = nextp()
            for kc in range(8):
                op("pe", lambda e, kc=kc, tt=tt, pb=pb: e.matmul(pb[:, 0:512], lhsT=uT[:, kc, tt * 128:(tt + 1) * 128], rhs=Wgv[:, kc, :], start=(kc == 0), stop=(kc == 7)), reads=["Wgv"] + uk(tt * 128, tt * 128 + 128), writes=[pk], pe_acc=True)
            op("act", lambda e, tt=tt, pb=pb: e.activation(out=vtok[:, tt, :], in_=pb[:, 0:512], func=AF.Copy), reads=[pk], writes=[("vtok", tt)])
        for d in range(2):
            for (c0, n) in G5:
                pb, pk = nextp()
                for kc in range(8):
                    op("pe", lambda e, kc=kc, d=d, c0=c0, n=n, pb=pb: e.matmul(pb[0:16, 0:n], lhsT=Wlr[:, kc, d * 16:(d + 1) * 16], rhs=uT[:, kc, c0:c0 + n], start=(kc == 0), stop=(kc == 7)), reads=["Wlr"] + uk(c0, c0 + n), writes=[pk], pe_acc=True)
                op("act", lambda e, d=d, c0=c0, n=n, pb=pb: e.activation(out=lrT[:, d, c0:c0 + n], in_=pb[0:16, 0:n], func=AF.Copy), reads=[pk], writes=["lrT"])
        if self.stop_after == "gla1":
            self.dump("lrT", lrT[:], [16, 2, 2304], ["lrT"], BF16)
            return self.finish()
        self.stage_end()

        def gla_dir(d):
            self.stage_begin()
            kdec = self.sb("kdec", [128, 18, 2, 192], BF16)
            op("pool", lambda e: e.memset(kdec[:], 0.0), writes=["kdec"])
            self.stage_begin()
            gtb = [self.sb(f"gtb{i}", [128, 256], F32) for i in range(2)]
            ekb = [self.sb(f"ekb{i}", [128, 256], F32) for i in range(2)]
            ktb = [self.sb(f"ktb{i}", [128, 256], BF16) for i in range(2)]
            def tok_chain(tt):
                    kt_, ktk = ktb[tt % 2], f"ktb{tt % 2}"
                    pb, pk = nextp()
                    for kc in range(8):
                        op("pe", lambda e, kc=kc, tt=tt, pb=pb: e.matmul(pb[:, 0:256], lhsT=uT[:, kc, tt * 128:(tt + 1) * 128], rhs=Wgqk[:, kc, 256:512], start=(kc == 0), stop=(kc == 7)), reads=["Wgqk"] + uk(tt * 128, tt * 128 + 128), writes=[pk], pe_acc=True)
                    op("dve", lambda e, kt_=kt_, pb=pb: e.tensor_copy(out=kt_[:], in_=pb[:, 0:256]), reads=[pk], writes=[ktk])
                    yield
                    gt, gk = gtb[tt % 2], f"gtb{tt % 2}"
                    ek, ekk = ekb[tt % 2], f"ekb{tt % 2}"
                    pz, pzk = nextp()
                    op("pe", lambda e, tt=tt, pz=pz: e.matmul(pz[:, 0:256], lhsT=lrT[:, d, tt * 128:(tt + 1) * 128], rhs=wa2[:, d, :], start=True, stop=False), reads=["lrT", "wa2"], writes=[pzk])
                    yield
                    op("pe", lambda e, pz=pz: e.matmul(pz[:, 0:256], lhsT=ones_f[0:1, :], rhs=barow[0:1, d * 256:(d + 1) * 256], start=False, stop=True), reads=["ones_f", "barow"], writes=[pzk], pe_acc=True)
                    yield
                    op("act", lambda e, gt=gt, pz=pz: e.activation(out=gt[:], in_=pz[:, 0:256], func=AF.Exp, scale=-1.0), reads=[pzk], writes=[gk])
                    yield
                    op("act", lambda e, gt=gt: e.activation(out=gt[:], in_=gt[:], func=AF.Ln, bias=1.0), reads=[gk], writes=[gk])
                    yield
                    op("dve", lambda e, gt=gt: e.tensor_scalar(out=gt[:], in0=gt[:], scalar1=-1.0 / 16, scalar2=None, op0=ALU.mult), reads=[gk], writes=[gk])
                    yield
                    pc, pck = nextp()
                    op("pe", lambda e, gt=gt, pc=pc: e.matmul(pc[:, 0:256], lhsT=gmask[:, d, :], rhs=gt[:], start=True, stop=True), reads=["gmask", gk], writes=[pck])
                    yield
                    op("act", lambda e, ek=ek, pc=pc: e.activation(out=ek[:], in_=pc[:, 0:256], func=AF.Exp), reads=[pck], writes=[ekk])
                    yield
                    kv_ = kdec[:, tt].rearrange("p q (j c) -> p q j c", c=64)[:, :, 0:3:2, :]
                    op("dve", lambda e, kt_=kt_, ek=ek, kv_=kv_: e.tensor_tensor(out=kv_, in0=kt_[:].rearrange("p (q j c) -> p q j c", q=2, j=2), in1=ek[:].rearrange("p (q j c) -> p q j c", q=2, j=2), op=ALU.mult), reads=[ktk, ekk, "kdec"], writes=[("kdec", tt)])
                    yield

            self.interleave([tok_chain(tt) for tt in range(18)], 2)
            if self.stop_after == "gla2" and d == self.sd:
                self.dump("kdec", kdec[:], [128, 18, 2, 192], ["kdec"] + [("kdec", t) for t in range(18)], BF16)
                return True
            self.stage_end()
            for pair in range(2):
                if gla_pair(d, pair, kdec):
                    return True
            self.stage_end()

        def gla_pair(d, pair, kdec):
            self.stage_begin()
            B1 = self.sb("B1", [128, 2304], F32)
            B2 = self.sb("B2", [128, 2304], F32)
            qin = self.sb("qin", [128, 2304], BF16)
            kin = self.sb("kin", [128, 2304], BF16)
            bl = self.sb("bl", [128, 36], F32)
            dec = self.sb("dec", [128, 36], F32)
            Linc = self.sb("Linc", [128, 32], F32)
            eL = self.sb("eL", [128, 32], F32)
            hA, hB = 2 * pair, 2 * pair + 1
            for (c0, n) in G5:
                pb, pk = nextp()
                op("pe", lambda e, c0=c0, n=n, pb=pb: e.matmul(pb[:, 0:n], lhsT=wa2[:, d, pair * 128:(pair + 1) * 128], rhs=lrT[:, d, c0:c0 + n], start=True, stop=True), reads=["wa2", "lrT"], writes=[pk])
                op("act", lambda e, c0=c0, n=n, pb=pb: e.activation(out=B1[:, c0:c0 + n], in_=pb[:, 0:n], func=AF.Exp, scale=-1.0, bias=nba[:, d, pair:pair + 1]), reads=[pk, "nba"], writes=["B1"])
            op("act", lambda e: e.activation(out=B1[:], in_=B1[:], func=AF.Ln, bias=1.0), reads=["B1"], writes=["B1"])
            op("dve", lambda e: e.tensor_scalar(out=B1[:], in0=B1[:], scalar1=-1.0 / 16, scalar2=None, op0=ALU.mult), reads=["B1"], writes=["B1"])
            for (a_, b_) in ((0, 2048), (2048, 2304)):
                op("dve", lambda e, a_=a_, b_=b_: e.tensor_tensor_scan(out=rv(B2[:, a_:b_], d), data0=cm[:, a_:b_], data1=rv(B1[:, a_:b_], d), initial=0.0, op0=ALU.mult, op1=ALU.add), reads=["B1", "cm"], writes=["B2"])
            o0 = 63 if d == 0 else 0
            op("dve", lambda e: e.tensor_copy(out=bl[:], in_=B2[:].rearrange("p (c t) -> p c t", t=64)[:, :, o0]), reads=["B2"], writes=["bl"])
            op("act", lambda e: e.activation(out=dec[:], in_=bl[:], func=AF.Exp), reads=["bl"], writes=["dec"])
            op("dve", lambda e: e.tensor_tensor_scan(out=rv(Linc[:], d), data0=rv(ones_f[:, 0:32], d), data1=rv(bl[:, 0:32], d), initial=0.0, op0=ALU.mult, op1=ALU.add), reads=["bl", "ones_f"], writes=["Linc"])
            op("dve", lambda e: e.tensor_tensor(out=eL[:], in0=Linc[:], in1=bl[:, 0:32], op=ALU.subtract), reads=["Linc", "bl"], writes=["eL"])
            op("act", lambda e: e.activation(out=eL[:], in_=eL[:], func=AF.Exp), reads=["eL"], writes=["eL"])
            lastc = 31 if d == 0 else 0
            op("act", lambda e: e.activation(out=PK[:, d, pair, 128:129], in_=Linc[:, lastc:lastc + 1], func=AF.Exp), reads=["Linc"], writes=[("PK", d, pair, 1)])
            op("act", lambda e: e.activation(out=B1[:], in_=B2[:], func=AF.Exp), reads=["B2"], writes=["B1"])
            for (c0, n) in G5:
                pb, pk = nextp()
                for kc in range(8):
                    op("pe", lambda e, kc=kc, c0=c0, n=n, pb=pb: e.matmul(pb[:, 0:n], lhsT=Wgqk[:, kc, pair * 128:(pair + 1) * 128], rhs=uT[:, kc, c0:c0 + n], start=(kc == 0), stop=(kc == 7)), reads=["Wgqk"] + uk(c0, c0 + n), writes=[pk], pe_acc=True)
                op("dve", lambda e, c0=c0, n=n, pb=pb: e.scalar_tensor_tensor(out=qin[:, c0:c0 + n], in0=pb[:, 0:n], scalar=0.125, in1=B1[:, c0:c0 + n], op0=ALU.mult, op1=ALU.mult), reads=[pk, "B1"], writes=["qin"])
            op("act", lambda e: e.activation(out=B1[:], in_=B2[:], func=AF.Exp, scale=-1.0), reads=["B2", "qin"], writes=["B1"])
            for (c0, n) in G5:
                pb, pk = nextp()
                for kc in range(8):
                    op("pe", lambda e, kc=kc, c0=c0, n=n, pb=pb: e.matmul(pb[:, 0:n], lhsT=Wgqk[:, kc, 256 + pair * 128:256 + (pair + 1) * 128], rhs=uT[:, kc, c0:c0 + n], start=(kc == 0), stop=(kc == 7)), reads=["Wgqk"] + uk(c0, c0 + n), writes=[pk], pe_acc=True)
                op("dve", lambda e, c0=c0, n=n, pb=pb: e.tensor_tensor(out=kin[:, c0:c0 + n], in0=pb[:, 0:n], in1=B1[:, c0:c0 + n], op=ALU.mult), reads=[pk, "B1"], writes=["kin"])
            op("dve", lambda e: e.tensor_tensor(out=qd[:, d, pair, :].rearrange("p (c t) -> p c t", t=64), in0=qin[:, 0:2048].rearrange("p (c t) -> p c t", t=64), in1=eL[:].unsqueeze(2).to_broadcast([128, 32, 64]), op=ALU.mult), reads=["qin", "eL"], writes=[("qd", d, pair)])

            if self.stop_after == "gla3" and d == self.sd and pair == self.spair:
                self.dump("qin", qin[:], [128, 2304], ["qin", "kin", ("qd", d, pair)], BF16)
                return True

            def run(tiles, with_out):
                op("pool", lambda e: e.memset(Sst[:], 0.0), writes=["Sst"])
                op("pool", lambda e: e.memset(Sbf[0][:], 0.0), writes=["Sbf0"])
                cur = 0
                chs = [0, 1] if d == 0 else [1, 0]
                c1, c2 = chs
                prep = {}

                def partA(idx):
                    tt = tiles[idx]
                    par = idx % 2
                    pus = []
                    for ci, ch in enumerate(chs):
                        rb = ch * 64
                        if par == 0:
                            pu, puk = pY[2 + ci][:, 0:128], f"pY{2 + ci}"
                        else:
                            pu, puk = pT[:, ci * 512:ci * 512 + 128], f"pTu{ci}"
                        op("pe", lambda e, rb=rb, pu=pu: e.matmul(pu, lhsT=kdec[rb:rb + 64, tt, pair, 0:128], rhs=vtok[rb:rb + 64, tt, hA * 128:(hA + 1) * 128], start=True, stop=False), reads=[("kdec", tt), "kdec", ("vtok", tt)], writes=[puk])
                        op("pe", lambda e, rb=rb, pu=pu: e.matmul(pu, lhsT=kdec[rb:rb + 64, tt, pair, 64:192], rhs=vtok[rb:rb + 64, tt, hB * 128:(hB + 1) * 128], start=False, stop=True), reads=[("kdec", tt), "kdec", ("vtok", tt)], writes=[puk], pe_acc=True)
                        pus.append((pu, puk))
                    pts = []
                    if with_out:
                        t0 = tt * 128
                        for hh_ in range(2):
                            hb = hh_ * 64
                            pS, pSk = nextp()
                            P_, Pk = PTl4[par * 2 + hh_], f"PTl{par * 2 + hh_}"
                            op("pe", lambda e, hb=hb, pS=pS: e.matmul(pS[:, 0:128], lhsT=kin[hb:hb + 64, t0:t0 + 128], rhs=qin[hb:hb + 64, t0:t0 + 128], start=True, stop=True), reads=["kin", "qin"], writes=[pSk])
                            op("dve", lambda e, pS=pS, P_=P_: e.tensor_tensor(out=P_[:], in0=pS[:, 0:128], in1=gmask[:, 2 + d, :], op=ALU.mult), reads=[pSk, "gmask"], writes=[Pk])
                            pts.append((P_, Pk))
                    prep[idx] = (pus, pts)

                def partB(idx, cur):
                    tt = tiles[idx]
                    pus, pts = prep.pop(idx)
                    t0 = tt * 128
                    if with_out:
                        for hh_ in range(2):
                            hb = hh_ * 64
                            h = 2 * pair + hh_
                            P_, Pk = pts[hh_]
                            pO, pOk = pY[hh_], f"pY{hh_}"
                            op("pe", lambda e, h=h, P_=P_, pO=pO: e.matmul(pO[:, 0:128], lhsT=vtok[:, tt, h * 128:(h + 1) * 128], rhs=P_[:], start=True, stop=False), reads=[("vtok", tt), Pk], writes=[pOk])
                            op("pe", lambda e, hb=hb, pO=pO: e.matmul(pO[:, c1 * 64:(c1 + 1) * 64], lhsT=Sbf[cur][hb:hb + 64, :], rhs=qin[hb:hb + 64, t0 + c1 * 64:t0 + (c1 + 1) * 64], start=False, stop=False), reads=[f"Sbf{cur}", "qin"], writes=[pOk], pe_acc=True)
                    i1 = tt * 2 + c1
                    op("dve", lambda e: e.scalar_tensor_tensor(out=Sst[:], in0=Sst[:], scalar=dec[:, i1:i1 + 1], in1=pus[0][0], op0=ALU.mult, op1=ALU.add), reads=["Sst", "dec", pus[0][1]], writes=["Sst"])
                    nxt = 1 - cur
                    op("act", lambda e: e.activation(out=Sbf[nxt][:], in_=Sst[:], func=AF.Copy), reads=["Sst"], writes=[f"Sbf{nxt}"])
                    if with_out:
                        for hh_ in range(2):
                            hb = hh_ * 64
                            h = 2 * pair + hh_
                            pO, pOk = pY[hh_], f"pY{hh_}"
                            op("pe", lambda e, hb=hb, pO=pO: e.matmul(pO[:, c2 * 64:(c2 + 1) * 64], lhsT=Sbf[nxt][hb:hb + 64, :], rhs=qin[hb:hb + 64, t0 + c2 * 64:t0 + (c2 + 1) * 64], start=False, stop=True), reads=[f"Sbf{nxt}", "qin"], writes=[pOk], pe_acc=True)
                            if d == 0:
                                op("act", lambda e, h=h, pO=pO: e.activation(out=OL[:, h, t0:t0 + 128], in_=pO[:, 0:128], func=AF.Copy), reads=[pOk], writes=[("OL", h)])
                            else:
                                op("dve", lambda e, h=h, pO=pO: e.tensor_tensor(out=OL[:, h, t0:t0 + 128], in0=OL[:, h, t0:t0 + 128], in1=pO[:, 0:128], op=ALU.add), reads=[pOk, ("OL", h)], writes=[("OL", h)])
                    i2 = tt * 2 + c2
                    op("dve", lambda e: e.scalar_tensor_tensor(out=Sst[:], in0=Sst[:], scalar=dec[:, i2:i2 + 1], in1=pus[1][0], op0=ALU.mult, op1=ALU.add), reads=["Sst", "dec", pus[1][1]], writes=["Sst"])
                    op("act", lambda e: e.activation(out=Sbf[cur][:], in_=Sst[:], func=AF.Copy), reads=["Sst"], writes=[f"Sbf{cur}"])

                partA(0)
                for idx in range(len(tiles)):
                    if idx + 1 < len(tiles):
                        partA(idx + 1)
                    partB(idx, cur)

            run([16, 17] if d == 0 else [17, 16], False)
            op("dve", lambda e: e.tensor_copy(out=Sctx[:, d, pair, :], in_=Sst[:]), reads=["Sst"], writes=[("Sctx", d, pair)])
            if self.stop_after == "gla4" and d == self.sd and pair == self.spair:
                self.dump("Sctx", Sctx[:, d, pair, :], [128, 128], [("Sctx", d, pair)])
                return True
            run(list(range(16)) if d == 0 else list(range(15, -1, -1)), True)
            op("dve", lambda e: e.tensor_copy(out=PK[:, d, pair, 0:128], in_=Sst[:]), reads=["Sst"], writes=[("PK", d, pair, 0)])
            if self.stop_after == "gla5" and d == self.sd and pair == self.spair:
                self.dump("PK", PK[:, d, pair, :], [128, 129], [("PK", d, pair, 0), ("PK", d, pair, 1)])
                return True
            self.stage_end()

        for d in range(2):
            if gla_dir(d):
                return self.finish()
        if self.stop_after == "glalocal":
            self.dump("OL", OL[:, :, 0:2048], [128, 4, 2048], [("OL", h) for h in range(4)], BF16)
            self.dump("PK", PK[:], [128, 2, 2, 129], [("PK", d, p, q) for d in range(2) for p in range(2) for q in range(2)])
            return self.finish()
        PKall = self.sb("PKall", [128, 4, 516], F32)
        pkkeys = [("PK", d, p, q) for d in range(2) for p in range(2) for q in range(2)]
        dma("sp", lambda e: e.dma_start(out=cc3_in, in_=PK[:].rearrange("p d q n -> p (d q n)")), "cc3a", reads=pkkeys, writes=["cc3_in"])
        dma("pool", lambda e: e.collective_compute("AllGather", ALU.bypass, replica_groups=RG, ins=[cc3_in.opt()], outs=[cc3_out.opt()]), "cc3", reads=["cc3_in"], writes=["cc3_out"], inc=1)
        dma("sp", lambda e: e.dma_start(out=PKall[:], in_=cc3_out.rearrange("(r p) f -> p r f", p=128)), "cc3b", reads=["cc3_out"], writes=["PKall"])
        Sin = self.sb("Sin", [128, 2, 2, 128], F32)
        Sinb = self.sb("Sinb", [128, 2, 2, 128], BF16)
        Scur = self.sb("Scur", [128, 128], F32)
        PKv = PKall[:].rearrange("p r (d q n) -> p r d q n", d=2, q=2)
        for d in range(2):
            order = [0, 1, 2, 3] if d == 0 else [3, 2, 1, 0]
            for pair in range(2):
                op("dve", lambda e, d=d, pair=pair: e.tensor_copy(out=Scur[:], in_=Sctx[:, d, pair, :]), reads=[("Sctx", d, pair)], writes=["Scur"])
                op("dve", lambda e, d=d, pair=pair, r0=order[0]: e.tensor_scalar(out=Sin[:, d, pair, :], in0=Scur[:], scalar1=V("sel", 1, r0), scalar2=None, op0=ALU.mult), reads=["Scur", "vecs"], writes=["Sin"])
                for i_ in range(3):
                    r, rn = order[i_], order[i_ + 1]
                    op("dve", lambda e, d=d, pair=pair, r=r: e.scalar_tensor_tensor(out=Scur[:], in0=Scur[:], scalar=PKv[:, r, d, pair, 128:129], in1=PKv[:, r, d, pair, 0:128], op0=ALU.mult, op1=ALU.add), reads=["Scur", "PKall"], writes=["Scur"])
                    op("dve", lambda e, d=d, pair=pair, rn=rn: e.scalar_tensor_tensor(out=Sin[:, d, pair, :], in0=Scur[:], scalar=V("sel", 1, rn), in1=Sin[:, d, pair, :], op0=ALU.mult, op1=ALU.add), reads=["Scur", "Sin", "vecs"], writes=["Sin"])
        op("pool", lambda e: e.tensor_copy(out=Sinb[:], in_=Sin[:]), reads=["Sin"], writes=["Sinb"])
        Wgg = self.sb("Wgg", [128, 8, 512], BF16)
        for kc in range(8):
            dma("pool", lambda e, kc=kc: e.dma_start(out=Wgg[:, kc, :], in_=w_in_o[kc * 128:(kc + 1) * 128, 1024:1536]), "Wgg", writes=["Wgg"])
        ofs = [self.sb(f"of{i}", [128, 512], F32) for i in range(2)]
        sqbs = [self.sb(f"sqb{i}", [128, 512], BF16) for i in range(2)]
        sgs = [self.sb(f"sg{i}", [128, 512], F32) for i in range(2)]
        rinvs = [self.sb(f"rinv{i}", [128, 512], F32) for i in range(2)]

        def tail_chain(idx):
            h, g = idx // 4, idx % 4
            j_ = idx % 2
            of_, sqb, sg, rinv = ofs[j_], sqbs[j_], sgs[j_], rinvs[j_]
            ofk, sqk, sgk, rik = f"of{j_}", f"sqb{j_}", f"sg{j_}", f"rinv{j_}"
            pair, hb = h // 2, (h % 2) * 64
            c0 = g * 512
            pc, pck = (pA, "pA") if j_ == 0 else (pB, "pB")
            op("pe", lambda e: e.matmul(pc[:], lhsT=Sinb[hb:hb + 64, 0, pair, :], rhs=qd[hb:hb + 64, 0, pair, c0:c0 + 512], start=True, stop=False), reads=["Sinb", ("qd", 0, pair)], writes=[pck])
            op("pe", lambda e: e.matmul(pc[:], lhsT=Sinb[hb:hb + 64, 1, pair, :], rhs=qd[hb:hb + 64, 1, pair, c0:c0 + 512], start=False, stop=True), reads=["Sinb", ("qd", 1, pair)], writes=[pck], pe_acc=True)
            yield
            op("dve", lambda e: e.tensor_tensor(out=of_[:], in0=OL[:, h, c0:c0 + 512], in1=pc[:], op=ALU.add), reads=[("OL", h), pck], writes=[ofk])
            yield
            op("pool", lambda e: e.tensor_tensor(out=sqb[:], in0=of_[:], in1=of_[:], op=ALU.mult), reads=[ofk], writes=[sqk])
            yield
            pss, pssk = (pY[2], "pY2") if j_ == 0 else (pY[3], "pY3")
            op("pe", lambda e: e.matmul(pss[:], lhsT=ones_b[:], rhs=sqb[:], start=True, stop=True), reads=["ones_b", sqk], writes=[pssk])
            yield
            op("act", lambda e: e.activation(out=rinv[:], in_=pss[:], func=AF.Sqrt, scale=1.0 / 128, bias=EPS), reads=[pssk], writes=[rik])
            yield
            op("dve", lambda e: e.reciprocal(out=rinv[:], in_=rinv[:]), reads=[rik], writes=[rik])
            yield
            pg, pgk = pY[j_], f"pY{j_}"
            for kc in range(8):
                op("pe", lambda e, kc=kc: e.matmul(pg[:], lhsT=Wgg[:, kc, h * 128:(h + 1) * 128], rhs=uT[:, kc, c0:c0 + 512], start=(kc == 0), stop=(kc == 7)), reads=["Wgg"] + uk(c0, c0 + 512), writes=[pgk], pe_acc=True)
            yield
            op("act", lambda e: e.activation(out=sg[:], in_=pg[:], func=AF.Silu), reads=[pgk], writes=[sgk])
            yield
            op("dve", lambda e: e.scalar_tensor_tensor(out=of_[:], in0=of_[:], scalar=V("gnw", 1), in1=rinv[:], op0=ALU.mult, op1=ALU.mult), reads=[ofk, "vecs", rik], writes=[ofk])
            yield
            op("pool", lambda e: e.tensor_tensor(out=mixB[:, h, c0:c0 + 512], in0=of_[:], in1=sg[:], op=ALU.mult), reads=[ofk, sgk], writes=[("mixB", h)])
            yield

        self.interleave([tail_chain(i) for i in range(16)], 2)
        if self.stop_after == "gla":
            self.dump("mixB", mixB[:, :, 0:2048], [128, 4, 2048], [("mixB", c) for c in range(4)], BF16)
            return self.finish()
        self.stage_end()

        self.stage_begin()
        QTa = self.sb("QTa", [128, 8, 2048], BF16)
        KTa = self.sb("KTa", [128, 8448], BF16)
        Va = self.sb("Va", [128, 66, 2, 66], BF16)
        op("pool", lambda e: e.memset(Va[:], 1.0), writes=["Va"])
        op("pool", lambda e: e.memset(QTa[:], 0.0), writes=["QTa"])
        for h in range(8):
            r0 = (h // 4) * 64
            dma("sp", lambda e, h=h, r0=r0: e.dma_start(out=QTa[r0:r0 + 64, h, :], in_=qscr[:, h, :]), "QTa", reads=["qscr", "QTa"], writes=["QTa"])
        for kv in range(2):
            r0 = kv * 64
            dma("sp", lambda e, kv=kv, r0=r0: e.dma_start(out=KTa[r0:r0 + 64, 0:256], in_=kctx[:, kv, :]), "KTa", reads=["kctx"], writes=["KTa"])
            for r in range(4):
                dma("sp", lambda e, kv=kv, r=r, r0=r0: e.dma_start(out=KTa[r0:r0 + 64, 256 + r * 2048:256 + (r + 1) * 2048], in_=cc2k_out[r * 64:(r + 1) * 64, kv * 2048:(kv + 1) * 2048]), "KTa", reads=["cc2k_out"], writes=["KTa"])
        for t in range(66):
            if t < 2:
                dma("sp", lambda e, t=t: e.dma_start(out=Va[:, t, :, 0:64], in_=vctx[t * 128:(t + 1) * 128, :].rearrange("p (h d) -> p h d", d=64)), "Va", reads=["vctx", "Va"], writes=["Va"])
            else:
                dma("sp", lambda e, t=t: e.dma_start(out=Va[:, t, :, 0:64], in_=cc2v_out[(t - 2) * 128:(t - 1) * 128, :].rearrange("p (h d) -> p h d", d=64)), "Va", reads=["cc2v_out", "Va"], writes=["Va"])
        PTg = [self.sb(f"PTg{i}", [128, 512], BF16) for i in range(8)]
        recg = self.sb("recg", [128, 512], F32)
        Oug = self.sb("Oug", [64, 512], F32)
        mtmpg = self.sb("mtmpg", [64, 2048], BF16)
        SB2 = [(pA, "pA"), (pB, "pB"), (pY[0], "pY0"), (pY[1], "pY1")]
        items = [(h, qb, kt) for h in range(8) for qb in range(4) for kt in range(66)]
        LA = 3
        pend = {}

        def g_qk(i):
            h, qb, kt = items[i]
            kv, q0 = h // 4, qb * 512
            Sp, Sk = SB2[i % 4]
            P_, Pk = PTg[i % 8], f"PTg{i % 8}"
            op("pe", lambda e: e.matmul(Sp[:], lhsT=KTa[:, kt * 128:(kt + 1) * 128], rhs=QTa[:, h, q0:q0 + 512], start=True, stop=True), reads=["KTa", "QTa"], writes=[Sk])
            op("act", lambda e: e.activation(out=P_[:], in_=Sp[:], func=AF.Exp, scale=0.125), reads=[Sk], writes=[Pk])

        def g_pv(i, step):
            h, qb, kt = items[i]
            kv, q0 = h // 4, qb * 512
            blk = h * 4 + qb
            O, Ok = pY[2 + blk % 2], f"pY{2 + blk % 2}"
            P_, Pk = PTg[i % 8], f"PTg{i % 8}"
            op("pe", lambda e: e.matmul(O[0:66, :], lhsT=Va[:, kt, kv, :], rhs=P_[:], start=(kt == 0), stop=(kt == 65)), reads=["Va", Pk], writes=[Ok], pe_acc=True)
            if kt == 65:
                op("dve", lambda e: e.reciprocal(out=recg[64:65, :], in_=O[64:65, :]), reads=[Ok], writes=["recg"])
                op("act", lambda e: e.activation(out=Oug[:], in_=O[0:64, :], func=AF.Copy), reads=[Ok], writes=["Oug"])

                def fin():
                    op("pe", lambda e: e.matmul(pT[0:64, 0:512], lhsT=ones_f[64:65, 0:64], rhs=recg[64:65, :], start=True, stop=True), reads=["ones_f", "recg"], writes=["pT"])
                    if h % 2 == 0:
                        op("dve", lambda e: e.tensor_tensor(out=mixA[0:64, h // 2, q0:q0 + 512], in0=Oug[:], in1=pT[0:64, 0:512], op=ALU.mult), reads=["Oug", "pT"], writes=[("mixA", h // 2, 0)])
                    else:
                        op("dve", lambda e: e.tensor_tensor(out=mtmpg[:, q0:q0 + 512], in0=Oug[:], in1=pT[0:64, 0:512], op=ALU.mult), reads=["Oug", "pT"], writes=["mtmpg"])
                        if qb == 3:
                            dma("sp", lambda e: e.dma_start(out=mixA[64:128, h // 2, 0:2048], in_=mtmpg[:, :]), "mixAup", reads=["mtmpg"], writes=[("mixA", h // 2, 1)])
                pend.setdefault(step + 2, []).append(fin)

        nI = len(items)
        for step in range(nI + LA + 4):
            if step < nI:
                g_qk(step)
            if 0 <= step - LA < nI:
                g_pv(step - LA, step)
            for f_ in pend.pop(step, []):
                f_()
        if self.stop_after == "gqa":
            self.dump("mixA", mixA[:], [128, 4, 2304], [("mixA", c, q) for c in range(4) for q in range(2)], BF16)
            return self.finish()
        self.stage_end()

        def old1(tt):
            return h1[tt * 128:(tt + 1) * 128, :], [("h1", tt)]

        def mix1(kc):
            if kc < 4:
                return mixB[:, kc, :], [("mixB", kc)]
            return mixA[:, kc - 4, :], [("mixA", kc - 4, 0), ("mixA", kc - 4, 1)]

        self.stage_begin()
        W1a1, W2h1 = ffn_alloc()
        out_proj(1, w_out_o, old1, mix1, ntile=16, after_wo=lambda: ffn_load(1, 0, W1a1, "W1h0", W2h1))
        fnw = self.sb("fnw", [128, 1024], F32)
        dma("sp", lambda e: e.dma_start(out=fnw[:], in_=fnw_d.ap()), "fnw", writes=["fnw"])

        def fin1(tt, xb, xk, tb, tk):
            norm_from_sbuf(xb, xk, 0, None, None, None, None, None, ntile=tb, nkey=tk)
            op("pool", lambda e, tb=tb: e.tensor_tensor(out=tb[:], in0=tb[:], in1=fnw[:], op=ALU.mult), reads=[tk, "fnw"], writes=[tk])
            dma("sp", lambda e, tb=tb, tt=tt: e.dma_start(out=out_d[tt * 128:(tt + 1) * 128, :], in_=tb[:]), "outst_" + tk, reads=[tk], writes=[("out", tt)])

        ffn(1, fin1, W1a1, W2h1, ngrp=8)
        S.wait_all("sp", [("out", tt) for tt in range(16)])
        return self.finish()

    def finish(self):
        S = self.S
        keys = [("dbg", n) for n in self.dbg_out]
        S.wait_all("sp", keys)
        S.emit()
        while self.cur is not self.es:
            self.cur.close()
            self.cur = self.stk.pop()
        self.es.close()
        return self.nc


def _fm(v):
    v = np.asarray(v, np.float32)
    return np.ascontiguousarray(v.reshape(-1, 128).T)


def make_natab(rpb):
    rpb = np.asarray(rpb, np.float32)
    cols = np.arange(64)
    col_start = np.clip(cols - 8, 0, 48)
    kc = cols[:, None]
    qc = cols[None, :]
    valid = (kc >= col_start[None, :]) & (kc < col_start[None, :] + 16)
    dcol = np.clip(kc - qc, -15, 15) + 15
    tab = np.full((128, 8, 22, 64), NEG, np.float32)
    for idx in range(22):
        for half in range(2):
            drow = 17 - idx + half
            if 0 <= drow <= 14:
                blk = np.where(valid[None], rpb[:, drow][:, dcol], NEG)
                tab[half * 64:(half + 1) * 64, :, idx, :] = blk.transpose(1, 0, 2)
    return tab.reshape(128, 8, 22 * 64)


def make_aug(k):
    qaug = np.full((40, 2048), NEG, np.float32)
    for r in range(32):
        R = 32 * k + r
        start = min(max(R - 4, 0), 120)
        for lr in range(40):
            gr = 32 * k - 4 + lr
            if start <= gr < start + 8:
                qaug[lr, r * 64:(r + 1) * 64] = 0.0
    kaug = np.zeros((40, 2560), np.float32)
    for lr in range(40):
        kaug[lr, lr * 64:(lr + 1) * 64] = 1.0
    return qaug, kaug


def _gmask():
    i = np.arange(128)
    same = (i[:, None] // 64) == (i[None, :] // 64)
    m = np.zeros((128, 4, 128), np.float32)
    m[:, 0, :] = same & (i[:, None] > i[None, :])
    m[:, 1, :] = same & (i[:, None] < i[None, :])
    m[:, 2, :] = same & (i[:, None] <= i[None, :])
    m[:, 3, :] = same & (i[:, None] >= i[None, :])
    return m


GMASK = _gmask()


def make_in_maps(inp):
    f = lambda a: np.ascontiguousarray(np.asarray(a, np.float32))
    x = f(inp["x"])
    natab = make_natab(inp["na_rpb"][0])
    maps = []
    for i in range(8):
        b, k = i // 4, i % 4
        xp = np.zeros((2560, 1024), np.float32)
        lo, hi = k * 2048 - 256, (k + 1) * 2048 + 256
        slo, shi = max(lo, 0), min(hi, 8192)
        xp[slo - lo:shi - lo] = x[b, slo:shi]
        vec = np.zeros((128, NV), np.float32)

        def put(name, v, off=0):
            v = _fm(v)
            vec[:, VC[name] + off:VC[name] + off + v.shape[1]] = v

        put("c", inp["c"][b]); put("cctx", inp["c_ctx"])
        for l in range(2):
            put(f"n1w{l}", inp["norm1_w"][l]); put(f"n2w{l}", inp["norm2_w"][l]); put(f"bmod{l}", inp["b_mod"][l])
        put("fnw", inp["final_norm_w"])
        cw = np.asarray(inp["lru_conv_w"][0], np.float32)
        for j in range(4):
            v = _fm(cw[j])
            for c in range(4):
                vec[:, VC["cw"] + c * 4 + j] = v[:, c]
        put("cb", inp["lru_conv_b"][0])
        for d in range(2):
            put(f"ba{d}", inp["lru_ba"][0, d]); put(f"bx{d}", inp["lru_bx"][0, d]); put(f"lam{d}", inp["lru_lambda"][0, d])
        vec[:, VC["sel"] + k] = 1.0
        vec[:, VC["flag"] + 0] = 1.0 if k > 0 else 0.0
        vec[:, VC["flag"] + 1] = 1.0 if k < 3 else 0.0
        qaug, kaug = make_aug(k)
        gba = np.asarray(inp["gla_ba"][0], np.float32)
        for d in range(2):
            put(f"gba{d}", gba[d])
        put("gnw", inp["gla_norm_w"][0])
        tpos = np.arange(k * 2048, (k + 1) * 2048, dtype=np.int32)
        inv = (np.float32(10000.0) ** (-np.arange(16, dtype=np.float32) / np.float32(16))).astype(np.float32)
        ang_r = (tpos // 64).astype(np.float32)[:, None] * inv[None, :]
        ang_c = (tpos % 64).astype(np.float32)[:, None] * inv[None, :]
        cr, sr, cc_, sc_ = np.cos(ang_r), np.sin(ang_r), np.cos(ang_c), np.sin(ang_c)
        cosT = np.concatenate([cr, cr, cc_, cc_], axis=1).astype(np.float32)
        sinT = np.concatenate([sr, sc_], axis=1).astype(np.float32)
        maps.append({
            "xh": xp, "ctxb": f(inp["ctx"][b]), "vecs": vec,
            "w_mod": f(inp["w_mod"]), "b_mod": f(inp["b_mod"]),
            "w_ff1": f(inp["w_ff1"]), "w_ff2": f(inp["w_ff2"]),
            "w_in_even": f(inp["w_in_even"][0]), "w_out_even": f(inp["w_out_even"][0]),
            "lru_wa": f(inp["lru_wa"][0]), "lru_wx": f(inp["lru_wx"][0]),
            "natab": natab, "qaug": qaug, "kaug": kaug,
            "w_in_odd": f(inp["w_in_odd"][0]), "w_out_odd": f(inp["w_out_odd"][0]),
            "gla_wa2": f(inp["gla_wa2"][0]), "barow": f(gba.reshape(1, 512)),
            "cosT": cosT, "sinT": sinT, "nsinT": f(-sinT),
            "qnw_bc": f(np.broadcast_to(np.asarray(inp["gqa_q_norm_w"][0], np.float32)[None, :], (128, 64))),
            "knw_bc": f(np.broadcast_to(np.asarray(inp["gqa_k_norm_w"][0], np.float32)[None, :], (128, 64))),
            "gmask": GMASK, "fnw_bc": f(np.broadcast_to(np.asarray(inp["final_norm_w"], np.float32)[None, :], (128, 1024))),
        })
    return maps


def kernel(**inputs):
    bld = Builder()
    nc = bld.build()
    maps = [{k: v for k, v in m.items() if k in bld.ins} for m in make_in_maps(inputs)]
    res = run_bass_kernel_spmd(nc, maps, core_ids=list(range(8)))
    out = np.zeros((2, 8192, 1024), np.float32)
    for i in range(8):
        b, k = i // 4, i % 4
        out[b, k * 2048:(k + 1) * 2048] = res.results[i]["out"]
    return out
```

```python
import numpy as np
from contextlib import ExitStack
import concourse.bass as bass
import concourse.mybir as mybir
from concourse.bass_utils import run_bass_kernel_spmd

F32 = mybir.dt.float32
BF16 = mybir.dt.bfloat16
AF = mybir.ActivationFunctionType
ALU = mybir.AluOpType
AX = mybir.AxisListType

NEG = -30000.0
EPS = 1e-6
RG = [[0, 1, 2, 3], [4, 5, 6, 7]]


class Sched:
    ENG = ("pe", "dve", "act", "pool", "sp")

    def __init__(self, nc):
        self.nc = nc
        self.streams = {e: [] for e in self.ENG}
        self.sem = {e: nc.alloc_semaphore(name="s_" + e) for e in self.ENG}
        self.cnt = {e: 0 for e in self.ENG}
        self.waited = {e: {} for e in self.ENG}
        self.lastw = {}
        self.readers = {}
        self.dsem = {}
        self.dcnt = {}

    def _deps(self, eng, reads, writes, pe_acc=False):
        waits = {}

        def add(ev):
            if ev is None:
                return
            s, v = ev
            if waits.get(s, 0) < v:
                waits[s] = v

        for k in reads:
            add(self.lastw.get(k))
        for k in writes:
            lw = self.lastw.get(k)
            rds = self.readers.get(k, ())
            same = lw is not None and lw[0] == ("e", eng)
            if not ((pe_acc and same) or (same and len(rds) > 0)):
                add(lw)
            for r in rds:
                add(r)
        out = []
        for s, v in waits.items():
            if self.waited[eng].get(s, 0) >= v:
                continue
            self.waited[eng][s] = v
            out.append((s, v))
        return out

    def _commit(self, ev, reads, writes):
        for k in reads:
            self.readers.setdefault(k, []).append(ev)
        for k in writes:
            self.lastw[k] = ev
            self.readers[k] = []

    def op(self, eng, fn, reads=(), writes=(), pe_acc=False):
        waits = self._deps(eng, reads, writes, pe_acc)
        self.cnt[eng] += 1
        ev = (("e", eng), self.cnt[eng])
        self.streams[eng].append((waits, fn, ("e", eng), 1))
        self._commit(ev, reads, writes)

    def dma(self, eng, fn, dsem, reads=(), writes=(), inc=16):
        if dsem not in self.dsem:
            self.dsem[dsem] = self.nc.alloc_semaphore(name="d_" + dsem)
            self.dcnt[dsem] = 0
        waits = self._deps(eng, reads, writes)
        self.dcnt[dsem] += inc
        ev = (("d", dsem), self.dcnt[dsem])
        self.streams[eng].append((waits, fn, ("d", dsem), inc))
        self._commit(ev, reads, writes)

    def wait_all(self, eng, keys):
        waits = self._deps(eng, keys, ())
        if waits:
            self.streams[eng].append((waits, None, None, 0))

    def barrier(self):
        evs = [(("e", e), self.cnt[e]) for e in self.ENG if self.cnt[e] > 0]
        evs += [(("d", n), self.dcnt[n]) for n in self.dsem if self.dcnt[n] > 0]
        for e in self.ENG:
            waits = []
            for s_, v in evs:
                if self.waited[e].get(s_, 0) < v:
                    self.waited[e][s_] = v
                    waits.append((s_, v))
            if waits:
                self.streams[e].append((waits, None, None, 0))
        self.lastw = {}
        self.readers = {}

    def _semh(self, s):
        kind, name = s
        return self.sem[name] if kind == "e" else self.dsem[name]

    def emit(self):
        nc = self.nc
        engs = {"pe": "tensor", "dve": "vector", "act": "scalar", "pool": "gpsimd", "sp": "sync"}
        with nc.Block() as block:
            for e, bname in engs.items():
                stream = self.streams[e]

                def body(engine, stream=stream):
                    for waits, fn, s, inc in stream:
                        for ws, wv in waits:
                            engine.wait_ge(self._semh(ws), wv)
                        if fn is not None:
                            ins = fn(engine)
                            ins.then_inc(self._semh(s), inc)

                getattr(block, bname)(body)


VC = {}
_off = 0


def _vc(name, n):
    global _off
    VC[name] = _off
    _off += n


_vc("c", 8); _vc("cctx", 8)
for _l in range(2):
    _vc(f"n1w{_l}", 8); _vc(f"n2w{_l}", 8); _vc(f"bmod{_l}", 48)
_vc("fnw", 8)
_vc("cw", 16); _vc("cb", 4)
for _d in range(2):
    _vc(f"ba{_d}", 4); _vc(f"bx{_d}", 4); _vc(f"lam{_d}", 4)
_vc("sel", 4); _vc("flag", 2)
_vc("gba0", 2); _vc("gba1", 2); _vc("gnw", 1)
NV = _off


class Builder:
    def __init__(self, stop_after=None, debug=(), skip_l0=False):
        self.skip_l0 = skip_l0
        self.sd = 0
        self.spair = 0
        self.stop_after = stop_after
        self.debug = set(debug)
        self.nc = nc = bass.Bass("TRN2", target_bir_lowering=False)
        self.es = ExitStack()
        self.cur = self.es
        self.stk = []
        self.S = Sched(nc)
        self.dbg_out = {}
        self.uid = 0

    def sb(self, name, shape, dt):
        self.nalloc = getattr(self, "nalloc", 0) + 1
        return self.cur.enter_context(self.nc.sbuf_tensor(f"sb_{name}_{self.nalloc}", list(shape), dt))

    def ps(self, name, shape, dt):
        return self.es.enter_context(self.nc.psum_tensor("ps_" + name, list(shape), dt))

    def din(self, name, shape, dt=F32):
        return self.nc.dram_tensor(name, list(shape), dt, kind="ExternalInput").ap()

    def dout(self, name, shape, dt=F32):
        return self.nc.dram_tensor(name, list(shape), dt, kind="ExternalOutput").ap()

    def dscr(self, name, shape, dt=F32):
        return self.nc.dram_tensor(name, list(shape), dt).ap()

    def stage_begin(self):
        self.stk.append(self.cur)
        self.cur = ExitStack()

    def stage_end(self):
        self.marks = getattr(self, "marks", [])
        self.marks.append(dict(self.S.cnt))
        self.S.barrier()
        self.cur.close()
        self.cur = self.stk.pop()

    @staticmethod
    def interleave(gens, width):
        gens = list(gens)
        active = []
        while gens or active:
            while gens and len(active) < width:
                active.append(gens.pop(0))
            nxt = []
            for g in active:
                try:
                    next(g)
                    nxt.append(g)
                except StopIteration:
                    pass
            active = nxt

    def dump(self, name, tile_ap, shape, keys, dt=F32):
        if name not in self.debug:
            return
        d = self.dout("dbg_" + name, shape, dt)
        self.dbg_out[name] = d
        self.S.dma("sp", lambda e: e.dma_start(out=d, in_=tile_ap), "dbg", reads=keys, writes=[("dbg", name)])

    def build(self):
        nc, S = self.nc, self.S
        op, dma = S.op, S.dma
        shapes = {
            "xh": [2560, 1024], "ctxb": [256, 1024], "vecs": [128, NV],
            "w_mod": [2, 1024, 6144], "b_mod": [2, 6144], "w_ff1": [2, 1024, 4096], "w_ff2": [2, 4096, 1024],
            "w_in_even": [1024, 2560], "w_out_even": [1024, 1024], "lru_wa": [2, 8, 64, 64], "lru_wx": [2, 8, 64, 64],
            "natab": [128, 8, 22 * 64], "qaug": [40, 2048], "kaug": [40, 2560],
            "w_in_odd": [1024, 2336], "w_out_odd": [1024, 1024], "gla_wa2": [2, 16, 256], "barow": [1, 512],
            "cosT": [2048, 64], "sinT": [2048, 32], "nsinT": [2048, 32], "qnw_bc": [128, 64], "knw_bc": [128, 64],
            "gmask": [128, 4, 128], "fnw_bc": [128, 1024],
        }
        self.ins = {}

        class Lazy:
            def __init__(s2, name):
                s2.name = name

            def ap(s2):
                if s2.name not in self.ins:
                    self.ins[s2.name] = self.din(s2.name, shapes[s2.name])
                return self.ins[s2.name]

            def __getitem__(s2, k):
                return s2.ap()[k]

        xh, ctxb, vecs_d = Lazy("xh"), Lazy("ctxb"), Lazy("vecs")
        w_mod, b_mod, w_ff1, w_ff2 = Lazy("w_mod"), Lazy("b_mod"), Lazy("w_ff1"), Lazy("w_ff2")
        w_in_e, w_out_e, lru_wa, lru_wx = Lazy("w_in_even"), Lazy("w_out_even"), Lazy("lru_wa"), Lazy("lru_wx")
        natab_d, qaug_d, kaug_d = Lazy("natab"), Lazy("qaug"), Lazy("kaug")
        w_in_o, w_out_o, gla_wa2_d, barow_d = Lazy("w_in_odd"), Lazy("w_out_odd"), Lazy("gla_wa2"), Lazy("barow")
        cosT_d, sinT_d, nsinT_d, qnw_d, knw_d = Lazy("cosT"), Lazy("sinT"), Lazy("nsinT"), Lazy("qnw_bc"), Lazy("knw_bc")
        gmask_d, fnw_d = Lazy("gmask"), Lazy("fnw_bc")
        out_d = self.dout("out", [2048, 1024])
        hmid = self.dscr("hmid", [2304, 1024])
        h1 = self.dscr("h1", [2304, 1024])
        cc1_in = self.dscr("cc1_in", [128, 16])
        cc1_out = self.dscr("cc1_out", [512, 16])
        mbscr2 = self.dscr("mbscr2", [2, 128, 2, 2048])
        qscr = self.dscr("qscr", [64, 8, 2048], BF16)
        kctx = self.dscr("kctx", [64, 2, 256], BF16)
        vctx = self.dscr("vctx", [256, 128], BF16)
        cc2k_in = self.dscr("cc2k_in", [64, 4096], BF16)
        cc2k_out = self.dscr("cc2k_out", [256, 4096], BF16)
        cc2v_in = self.dscr("cc2v_in", [2048, 128], BF16)
        cc2v_out = self.dscr("cc2v_out", [8192, 128], BF16)
        cc3_in = self.dscr("cc3_in", [128, 516])
        cc3_out = self.dscr("cc3_out", [512, 516])

        ident_f = self.sb("ident_f", [128, 128], F32)
        ident_b = self.sb("ident_b", [128, 128], BF16)
        ones_f = self.sb("ones_f", [128, 128], F32)
        vecs = self.sb("vecs", [128, NV], F32)
        op("pool", lambda e: e.memset(ident_f[:], 1.0), writes=["ident_f"])
        op("pool", lambda e: e.affine_select(out=ident_f[:], in_=ident_f[:], pattern=[[-1, 128]], compare_op=ALU.is_equal, fill=0.0, base=0, channel_multiplier=1), reads=["ident_f"], writes=["ident_f"])
        op("pool", lambda e: e.tensor_copy(out=ident_b[:], in_=ident_f[:]), reads=["ident_f"], writes=["ident_b"])
        op("pool", lambda e: e.memset(ones_f[:], 1.0), writes=["ones_f"])
        dma("sp", lambda e: e.dma_start(out=vecs[:], in_=vecs_d.ap()), "vecs", writes=["vecs"])

        def V(name, n=1, off=0):
            o = VC[name] + off
            return vecs[:, o:o + n]

        pT = self.ps("pT", [128, 1024], F32)
        pA = self.ps("pA", [128, 512], F32)
        pB = self.ps("pB", [128, 512], F32)
        pY = [self.ps(f"pY{i}", [128, 512], F32) for i in range(4)]
        pAB = [pA, pB]
        kAB = ["pA", "pB"]

        uT = self.sb("uT", [128, 8, 2816], BF16)
        mixAB = self.sb("mixAB", [128, 8, 2304], BF16)
        mixA = mixAB[:, 0:4, :]
        mixB = mixAB[:, 4:8, :]
        G1 = self.sb("G1", [128, 8, 2], F32)
        S1 = self.sb("S1", [128, 8, 2], F32)
        G2 = self.sb("G2", [128, 8, 2], F32)
        S2 = self.sb("S2", [128, 8, 2], F32)
        junk = self.sb("junk", [128, 1024], BF16)
        st = self.sb("st", [128, 16], F32)
        junk2 = [junk, self.sb("junkb", [128, 1024], BF16)]
        tmpf = self.sb("tmpf", [128, 1024], F32)

        def uk(c0, c1):
            return [("uT", t) for t in range(c0 // 128, (c1 + 127) // 128)]

        G1b = self.sb("G1b", [128, 8, 2], F32)
        S1b = self.sb("S1b", [128, 8, 2], F32)
        G2b = self.sb("G2b", [128, 8, 2], F32)
        S2b = self.sb("S2b", [128, 8, 2], F32)
        GS = {0: (G1, S1, G2, S2, "G1", "S1", "G2", "S2"), 1: (G1b, S1b, G2b, S2b, "G1b", "S1b", "G2b", "S2b")}

        def mod_gen(l, dual, look=False):
            G1_, S1_, G2_, S2_, g1k, s1k, g2k, s2k = GS[l]
            sc_b = self.sb("sc_b", [128, 8, 2], BF16)
            sc_rep = self.sb("sc_rep", [128, 2, 8, 128], BF16)
            nbuf = 2 if (dual or look) else 1
            wm = [self.sb(f"wm{i}", [128, 8, 1024], BF16) for i in range(nbuf)]
            wf = self.sb("wf", [128, 8, 1024], F32) if dual else None
            sct = self.sb(f"sct{l}", [128, 8, 2], F32)
            MBt = [self.sb(f"MBt{i}", [128, 512], F32) for i in range(2)]
            sck = f"sct{l}"
            op("act", lambda e: e.activation(out=sct[:, :, 0], in_=V("c", 8), func=AF.Silu), reads=["vecs"], writes=[sck])
            op("act", lambda e: e.activation(out=sct[:, :, 1], in_=V("cctx", 8), func=AF.Silu), reads=["vecs", sck], writes=[sck])
            op("dve", lambda e: e.tensor_copy(out=sc_b[:], in_=sct[:]), reads=[sck], writes=["sc_b"])
            for w in range(2):
                op("dve", lambda e, w=w: e.tensor_copy(out=sc_rep[:, w], in_=sct[:, :, w:w + 1].to_broadcast([128, 8, 128])), reads=[sck], writes=["sc_rep"])
            modT = self.sb(f"modT{l}", [128, 4, 8, 2], F32)
            fm_idx = {0: 0, 1: 1, 3: 2, 4: 3}
            yield

            def D(m):
                buf = wm[m % nbuf]
                bk = f"wm{m % nbuf}"
                if m % 2 == 0 or not dual:
                    for kc in range(8):
                        dma("pool", lambda e, kc=kc: e.dma_start(out=buf[:, kc, :], in_=w_mod[l, kc * 128:(kc + 1) * 128, m * 1024:(m + 1) * 1024]), bk, writes=[bk])
                else:
                    for kc in range(8):
                        dma("sp" if kc % 2 == 0 else "act", lambda e, kc=kc: e.dma_start(out=wf[:, kc, :], in_=w_mod[l, kc * 128:(kc + 1) * 128, m * 1024:(m + 1) * 1024]), f"wf{kc}", writes=[("wf", kc)])
                        ceng = "dve" if kc % 2 == 0 else "pool"
                        op(ceng, lambda e, kc=kc: e.tensor_copy(out=buf[:, kc, :], in_=wf[:, kc, :]), reads=[("wf", kc)], writes=[bk])

            def M(m):
                buf = wm[m % nbuf]
                bk = f"wm{m % nbuf}"
                if m in fm_idx:
                    for oc in range(8):
                        for kc in range(8):
                            op("pe", lambda e, oc=oc, kc=kc: e.matmul(pA[:, oc * 2:oc * 2 + 2], lhsT=buf[:, kc, oc * 128:(oc + 1) * 128], rhs=sc_b[:, kc, :], start=(kc == 0), stop=(kc == 7)), reads=[bk, "sc_b"], writes=["pA"], pe_acc=True)
                    mi = fm_idx[m]
                    op("dve", lambda e: e.tensor_tensor(out=modT[:, mi], in0=pA[:, 0:16].rearrange("p (c w) -> p c w", w=2), in1=V(f"bmod{l}", 8, m * 8).unsqueeze(2).to_broadcast([128, 8, 2]), op=ALU.add), reads=["pA", "vecs"], writes=[("modT", l, mi)])
                else:
                    gi = 0 if m == 2 else 1
                    dma("sp", lambda e: e.dma_start(out=tmpf[0:1, :], in_=b_mod[l:l + 1, m * 1024:(m + 1) * 1024]), "tmpf", writes=["tmpf"])
                    for w in range(2):
                        for nh in range(2):
                            pb = pAB[(w * 2 + nh) % 2]
                            pk = kAB[(w * 2 + nh) % 2]
                            op("pe", lambda e, nh=nh, pb=pb: e.matmul(pb[:], lhsT=ones_f[0:1, :], rhs=tmpf[0:1, nh * 512:(nh + 1) * 512], start=True, stop=False), reads=["ones_f", "tmpf"], writes=[pk], pe_acc=True)
                            for kc in range(8):
                                op("pe", lambda e, w=w, nh=nh, kc=kc, pb=pb: e.matmul(pb[:], lhsT=sc_rep[:, w, kc, :], rhs=buf[:, kc, nh * 512:(nh + 1) * 512], start=False, stop=(kc == 7)), reads=[bk, "sc_rep"], writes=[pk], pe_acc=True)
                            mt_ = MBt[(w * 2 + nh) % 2]
                            mtk = f"MBt{(w * 2 + nh) % 2}"
                            op("act", lambda e, pb=pb, mt_=mt_: e.activation(out=mt_[:], in_=pb[:], func=AF.Copy), reads=[pk], writes=[mtk])
                            dma("sp", lambda e, w=w, nh=nh, mt_=mt_: e.dma_start(out=mbscr2[l, :, gi, w * 1024 + nh * 512:w * 1024 + (nh + 1) * 512], in_=mt_[:]), mtk, reads=[mtk], writes=[("mbscr", l, gi)])

            def GSset(Gt, St, nw, ms, mh, gk_, sk_):
                op("dve", lambda e: e.tensor_scalar(out=Gt[:], in0=modT[:, ms], scalar1=1.0, scalar2=None, op0=ALU.add), reads=[("modT", l, ms)], writes=[gk_])
                op("dve", lambda e: e.tensor_tensor(out=Gt[:], in0=Gt[:], in1=V(nw, 8).unsqueeze(2).to_broadcast([128, 8, 2]), op=ALU.mult), reads=[gk_, "vecs"], writes=[gk_])
                op("dve", lambda e: e.tensor_copy(out=St[:], in_=modT[:, mh]), reads=[("modT", l, mh)], writes=[sk_])

            if nbuf == 2:
                D(0)
                yield
            for m in range(6):
                if nbuf == 2:
                    if m + 1 < 6:
                        D(m + 1)
                else:
                    D(m)
                M(m)
                if m == 1:
                    GSset(G1_, S1_, f"n1w{l}", 1, 0, g1k, s1k)
                if m == 4:
                    GSset(G2_, S2_, f"n2w{l}", 3, 2, g2k, s2k)
                yield

        def mod_stage(l):
            self.stage_begin()
            for _ in mod_gen(l, True):
                pass
            self.stage_end()

        def norm_gen(xtile, xkey, w, dst0, Gt, St, gk, sk, ntile=None, nkey=None):
            if ntile is None:
                ntile, nkey = xtile, xkey
            u = self.uid = self.uid + 1
            sc = st[:, (u % 4) * 4:(u % 4) * 4 + 4]
            skk = ("st", u % 4)
            jk = junk2[u % 2]
            jkk = f"junk{u % 2}"
            op("act", lambda e: e.activation(out=jk[:], in_=xtile[:], func=AF.Square, accum_out=sc[:, 0:1]), reads=[xkey], writes=[jkk, skk])
            yield
            op("dve", lambda e: e.tensor_scalar(out=sc[:, 1:2], in0=sc[:, 0:1], scalar1=1.0 / 1024, scalar2=EPS, op0=ALU.mult, op1=ALU.add), reads=[skk], writes=[skk])
            yield
            op("act", lambda e: e.activation(out=sc[:, 2:3], in_=sc[:, 1:2], func=AF.Sqrt), reads=[skk], writes=[skk])
            yield
            op("dve", lambda e: e.reciprocal(out=sc[:, 3:4], in_=sc[:, 2:3]), reads=[skk], writes=[skk])
            yield
            op("dve", lambda e: e.tensor_scalar(out=ntile[:], in0=xtile[:], scalar1=sc[:, 3:4], scalar2=None, op0=ALU.mult), reads=[xkey, skk], writes=[nkey])
            yield
            if dst0 is None:
                return

            def tps(c):
                if u % 2 == 0:
                    return pT[:, c * 128:(c + 1) * 128], "pT"
                return pY[2 + c // 4][:, (c % 4) * 128:(c % 4 + 1) * 128], f"pY{2 + c // 4}"

            for c in range(8):
                tp, tpk = tps(c)
                op("pe", lambda e, c=c, tp=tp: e.transpose(out=tp, in_=ntile[:, c * 128:(c + 1) * 128], identity=ident_f[:]), reads=[nkey, "ident_f"], writes=[tpk], pe_acc=True)
                yield
            for c in range(8):
                dst = uT[:, c, dst0:dst0 + 128]
                src, tpk = tps(c)
                if c % 2 == 0:
                    op("dve", lambda e, dst=dst, src=src, c=c: e.tensor_scalar(out=dst, in0=src, scalar1=Gt[:, c, w:w + 1], scalar2=St[:, c, w:w + 1], op0=ALU.mult, op1=ALU.add), reads=[tpk, gk, sk], writes=uk(dst0, dst0 + 128))
                    yield
                else:
                    op("act", lambda e, dst=dst, src=src, c=c: e.activation(out=dst, in_=src, func=AF.Identity, scale=Gt[:, c, w:w + 1], bias=St[:, c, w:w + 1]), reads=[tpk, gk, sk], writes=uk(dst0, dst0 + 128))
                    yield

        def norm_from_sbuf(*a_, **k_):
            for _ in norm_gen(*a_, **k_):
                pass

        pi = [0]

        def nextp():
            pi[0] += 1
            return pAB[pi[0] % 2], kAB[pi[0] % 2]

        def out_proj(l, Wout, old_tile, mixsel, ntile=18, after_wo=None):
            self.stage_begin()
            Wo = self.sb("Wo", [128, 8, 1024], BF16)
            for kc in range(8):
                dma("pool", lambda e, kc=kc: e.dma_start(out=Wo[:, kc, :], in_=Wout[kc * 128:(kc + 1) * 128, :]), "Wo", writes=["Wo"])
            if after_wo is not None:
                after_wo()
            xt = [self.sb(f"xt{i}", [128, 1024], F32) for i in range(3)]
            tm = [self.sb(f"tm{i}", [128, 1024], F32) for i in range(2)]
            MBg = self.sb("MBg", [128, 2, 1024], F32)
            dma("sp", lambda e: e.dma_start(out=MBg[:].rearrange("p w f -> p (w f)"), in_=mbscr2[l, :, 0, :]), "MBg", reads=[("mbscr", l, 0)], writes=["MBg"])
            def op_tile(tt):
                w = 0 if tt < 16 else 1
                xb, xk = xt[tt % 3], f"xt{tt % 3}"
                tb, tk = tm[tt % 2], f"tm{tt % 2}"
                ybk = [(pY[0], "pY0"), (pY[1], "pY1")] if tt % 2 == 0 else [(pA, "pA"), (pB, "pB")]
                src, skeys = old_tile(tt)
                dma("sp", lambda e: e.dma_start(out=xb[:], in_=src), xk, reads=skeys, writes=[xk])
                yield
                for nh in range(2):
                    yb, ybkk = ybk[nh]
                    for kc in range(8):
                        mt, mk = mixsel(kc)
                        op("pe", lambda e, nh=nh, kc=kc, mt=mt, yb=yb: e.matmul(yb[:], lhsT=mt[:, tt * 128:(tt + 1) * 128], rhs=Wo[:, kc, nh * 512:(nh + 1) * 512], start=(kc == 0), stop=(kc == 7)), reads=["Wo"] + mk, writes=[ybkk], pe_acc=True)
                    yield
                    op("dve", lambda e, nh=nh, yb=yb: e.tensor_tensor(out=tb[:, nh * 512:(nh + 1) * 512], in0=yb[:], in1=MBg[:, w, nh * 512:(nh + 1) * 512], op=ALU.mult), reads=[ybkk, "MBg"], writes=[tk])
                    yield
                op("dve", lambda e: e.tensor_tensor(out=xb[:], in0=xb[:], in1=tb[:], op=ALU.add), reads=[xk, tk], writes=[xk])
                yield
                dma("sp", lambda e: e.dma_start(out=hmid[tt * 128:(tt + 1) * 128, :], in_=xb[:]), f"hmst{tt}", reads=[xk], writes=[("hmid", tt)])
                yield
                yield from norm_gen(xb, xk, w, tt * 128, GS[l][2], GS[l][3], GS[l][6], GS[l][7], ntile=tb, nkey=tk)

            self.interleave([op_tile(tt) for tt in range(ntile)], 2)
            self.stage_end()

        def ffn_alloc():
            W1a = self.sb("W1a", [128, 8, 2048], BF16)
            W2h = self.sb("W2h", [128, 16, 1024], BF16)
            return W1a, W2h

        def ffn_load(l, half, W1, w1k, W2h):
            for kc in range(8):
                dma("pool", lambda e, kc=kc: e.dma_start(out=W1[:, kc, 0:2048], in_=w_ff1[l, kc * 128:(kc + 1) * 128, half * 2048:(half + 1) * 2048]), w1k, writes=[w1k])
            if W2h is not None:
                for fc in range(16):
                    dma("pool", lambda e, fc=fc: e.dma_start(out=W2h[:, fc, :], in_=w_ff2[l, half * 2048 + fc * 128:half * 2048 + (fc + 1) * 128, :]), "W2h", writes=["W2h"])

        def ffn(l, final_tile, W1a, W2h, ngrp=9):
            self.stage_begin()
            h1T = [self.sb(f"h1T{i}", [128, 16, 256], BF16) for i in range(2)]
            rl = [self.sb(f"rl{i}", [128, 256], F32) for i in range(2)]
            xt = [self.sb(f"xt{i}", [128, 1024], F32) for i in range(3)]
            tm = [self.sb(f"tm{i}", [128, 1024], F32) for i in range(2)]
            MBg = self.sb("MBg", [128, 2, 1024], F32)
            dma("sp", lambda e: e.dma_start(out=MBg[:].rearrange("p w f -> p (w f)"), in_=mbscr2[l, :, 1, :]), "MBg", reads=[("mbscr", l, 1)], writes=["MBg"])
            cnt = 0
            W1bufs = [(W1a, "W1h0"), (mixAB, "W1h1")]
            ffn_load(l, 1, mixAB, "W1h1", None)
            for half in range(2):
                W1h, w1k = W1bufs[half]
                W2b = W1a[:].rearrange("p k (t n) -> p (k t) n", t=2)
                W2cur, w2k = (W2h, "W2h") if half == 0 else (W2b, "W1h0")
                for grp in range(ngrp):
                    t0 = grp * 256
                    hbuf, hk = h1T[grp % 2], f"h1T{grp % 2}"
                    for fc in range(16):
                        pb, pk = nextp()
                        for kc in range(8):
                            op("pe", lambda e, kc=kc, fc=fc, t0=t0, pb=pb, W1h=W1h: e.matmul(pb[:, 0:256], lhsT=W1h[:, kc, fc * 128:(fc + 1) * 128], rhs=uT[:, kc, t0:t0 + 256], start=(kc == 0), stop=(kc == 7)), reads=[w1k] + uk(t0, t0 + 256), writes=[pk], pe_acc=True)
                        r_, rk = rl[fc % 2], f"rl{fc % 2}"
                        op("act", lambda e, pb=pb, r_=r_: e.activation(out=r_[:], in_=pb[:, 0:256], func=AF.Relu), reads=[pk], writes=[rk])
                        eng = "pool" if fc % 2 == 0 else "dve"
                        op(eng, lambda e, r_=r_, hbuf=hbuf, fc=fc: e.tensor_tensor(out=hbuf[:, fc, :], in0=r_[:], in1=r_[:], op=ALU.mult), reads=[rk], writes=[(hk, fc)])
                    if half == 0 and grp == ngrp - 1:
                        for fc in range(16):
                            dma("pool", lambda e, fc=fc, W2b=W2b: e.dma_start(out=W2b[:, fc, :], in_=w_ff2[l, 2048 + fc * 128:2048 + (fc + 1) * 128, :]), "W1h0", writes=["W1h0"])
                    for s_ in range(2):
                        tt = grp * 2 + s_
                        w = 0 if tt < 16 else 1
                        cnt += 1
                        xb, xk = xt[cnt % 3], f"xt{cnt % 3}"
                        tb, tk = tm[cnt % 2], f"tm{cnt % 2}"
                        dma("sp", lambda e, xb=xb, tt=tt: e.dma_start(out=xb[:], in_=hmid[tt * 128:(tt + 1) * 128, :]), xk, reads=[("hmid", tt)], writes=[xk])
                        for nh in range(2):
                            py, pyk = pY[s_ * 2 + nh], f"pY{s_ * 2 + nh}"
                            for fc in range(16):
                                op("pe", lambda e, nh=nh, fc=fc, s_=s_, hbuf=hbuf, py=py, W2cur=W2cur: e.matmul(py[:], lhsT=hbuf[:, fc, s_ * 128:(s_ + 1) * 128], rhs=W2cur[:, fc, nh * 512:(nh + 1) * 512], start=(fc == 0), stop=(fc == 15)), reads=[w2k, (hk, fc)], writes=[pyk], pe_acc=True)
                            op("dve", lambda e, nh=nh, tb=tb, w=w, py=py: e.tensor_tensor(out=tb[:, nh * 512:(nh + 1) * 512], in0=py[:], in1=MBg[:, w, nh * 512:(nh + 1) * 512], op=ALU.mult), reads=[pyk, "MBg"], writes=[tk])
                        op("pool", lambda e, xb=xb, tb=tb: e.tensor_tensor(out=xb[:], in0=xb[:], in1=tb[:], op=ALU.add), reads=[xk, tk], writes=[xk])
                        if half == 0:
                            dma("sp", lambda e, xb=xb, tt=tt: e.dma_start(out=hmid[tt * 128:(tt + 1) * 128, :], in_=xb[:]), f"hmst{tt}", reads=[xk], writes=[("hmid", tt)])
                        else:
                            final_tile(tt, xb, xk, tb, tk)
            self.stage_end()

        def rv(ap_, d):
            return ap_ if d == 0 else ap_[:, ::-1]

        def layer0():
            self.stage_begin()
            mg0 = mod_gen(0, False, look=True)
            for _ in range(4):
                next(mg0)
            xt = [self.sb(f"xt{i}", [128, 1024], F32) for i in range(3)]

            def n1tile(t):
                xb = xt[t % 3]
                xk = f"xt{t % 3}"
                src = xh[t * 128:(t + 1) * 128, :] if t < 20 else ctxb[(t - 20) * 128:(t - 19) * 128, :]
                dma("sp", lambda e: e.dma_start(out=xb[:], in_=src), xk, writes=[xk])
                yield
                yield from norm_gen(xb, xk, 0 if t < 20 else 1, t * 128, G1, S1, "G1", "S1")

            chains = [n1tile(t) for t in range(22)]
            active, ndone = [], 0
            while chains or active:
                while chains and len(active) < 2:
                    active.append(chains.pop(0))
                nxt_ = []
                for g_ in active:
                    try:
                        next(g_)
                        nxt_.append(g_)
                    except StopIteration:
                        ndone += 1
                        if ndone % 5 == 0:
                            next(mg0, None)
                active = nxt_
            for _ in mg0:
                pass
            self.stage_end()

            self.stage_begin()
            xcv = self.sb("xcv", [128, 4, 2304], F32)
            gg = self.sb("gg", [128, 4, 2304], BF16)
            PS = self.sb("PS", [128, 4, 4], F32)
            CF = self.sb("CF", [128, 4, 2], F32)
            sumr = self.sb("sumr", [128, 2, 2], F32)
            cl = self.sb("cl", [128, 2, 4], F32)
            for d in range(2):
                op("act", lambda e, d=d: e.activation(out=cl[:, d, :], in_=V(f"lam{d}", 4), func=AF.Exp, scale=-1.0), reads=["vecs"], writes=["cl"])
            op("act", lambda e: e.activation(out=cl[:], in_=cl[:], func=AF.Ln, bias=1.0), reads=["cl"], writes=["cl"])
            op("dve", lambda e: e.tensor_scalar(out=cl[:], in0=cl[:], scalar1=-8.0, scalar2=None, op0=ALU.mult), reads=["cl"], writes=["cl"])
            self.stage_begin()
            Wxg = self.sb("Wxg", [128, 8, 1024], BF16)
            for kc in range(8):
                dma("pool", lambda e, kc=kc: e.dma_start(out=Wxg[:, kc, :], in_=w_in_e[kc * 128:(kc + 1) * 128, 1536:2560]), "Wxg", writes=["Wxg"])
            xls = self.sb("xls", [128, 2820], F32)
            op("pool", lambda e: e.memset(xls[:, 2560:2820], 0.0), writes=["xls"])
            for c in range(4):
                groups = [(g * 512, 512, g * 512) for g in range(5)] + [(2560, 256, 2562)]
                for (c0, n, d0) in groups:
                    pb, pk = nextp()
                    for kc in range(8):
                        op("pe", lambda e, kc=kc, c=c, c0=c0, n=n, pb=pb: e.matmul(pb[:, 0:n], lhsT=Wxg[:, kc, c * 128:(c + 1) * 128], rhs=uT[:, kc, c0:c0 + n], start=(kc == 0), stop=(kc == 7)), reads=["Wxg"] + uk(c0, c0 + n), writes=[pk], pe_acc=True)
                    op("act", lambda e, n=n, d0=d0, pb=pb: e.activation(out=xls[:, d0:d0 + n], in_=pb[:, 0:n], func=AF.Copy), reads=[pk], writes=["xls"])
                ggroups = [(256 + g * 512, 512, g * 512) for g in range(4)] + [(2560, 256, 2048)]
                for (c0, n, d0) in ggroups:
                    pb, pk = nextp()
                    for kc in range(8):
                        op("pe", lambda e, kc=kc, c=c, c0=c0, n=n, pb=pb: e.matmul(pb[:, 0:n], lhsT=Wxg[:, kc, 512 + c * 128:512 + (c + 1) * 128], rhs=uT[:, kc, c0:c0 + n], start=(kc == 0), stop=(kc == 7)), reads=["Wxg"] + uk(c0, c0 + n), writes=[pk], pe_acc=True)
                    op("act", lambda e, n=n, d0=d0, pb=pb, c=c: e.activation(out=gg[:, c, d0:d0 + n], in_=pb[:, 0:n], func=AF.Gelu), reads=[pk], writes=[("gg", c)])
                op("dve", lambda e: e.tensor_scalar(out=xls[:, 254:256], in0=xls[:, 254:256], scalar1=V("flag", 1, 0), scalar2=None, op0=ALU.mult), reads=["xls", "vecs"], writes=["xls"])
                op("dve", lambda e: e.tensor_scalar(out=xls[:, 2304:2305], in0=xls[:, 2304:2305], scalar1=V("flag", 1, 1), scalar2=None, op0=ALU.mult), reads=["xls", "vecs"], writes=["xls"])
                for (dst0, n, src0) in ((0, 2048, 256), (2048, 256, 2562)):
                    for j in range(4):
                        wj = V("cw", 1, c * 4 + j)
                        srcap = xls[:, src0 + j - 2:src0 + j - 2 + n]
                        dstap = xcv[:, c, dst0:dst0 + n]
                        if j == 0:
                            op("dve", lambda e, dstap=dstap, srcap=srcap, wj=wj, c=c: e.tensor_scalar(out=dstap, in0=srcap, scalar1=wj, scalar2=V("cb", 1, c), op0=ALU.mult, op1=ALU.add), reads=["xls", "vecs"], writes=[("xcv", c)])
                        else:
                            op("dve", lambda e, dstap=dstap, srcap=srcap, wj=wj: e.scalar_tensor_tensor(out=dstap, in0=srcap, scalar=wj, in1=dstap, op0=ALU.mult, op1=ALU.add), reads=["xls", "vecs", ("xcv", c)], writes=[("xcv", c)])
            if self.stop_after == "lruconv":
                self.dump("xcv", xcv[:], [128, 4, 2304], [("xcv", c) for c in range(4)])
                self.dump("gg", gg[:], [128, 4, 2304], [("gg", c) for c in range(4)], BF16)
                return True
            self.stage_end()
            self.stage_begin()
            WBD = self.sb("WBD", [128, 4, 2, 2, 128], F32)
            op("pool", lambda e: e.memset(WBD[:], 0.0), writes=["WBD"])
            for d in range(2):
                for hh_ in range(8):
                    for wi, src in enumerate((lru_wa, lru_wx)):
                        p0 = (hh_ % 2) * 64
                        dma("sp", lambda e, d=d, hh_=hh_, wi=wi, src=src, p0=p0: e.dma_start(out=WBD[p0:p0 + 64, hh_ // 2, d, wi, p0:p0 + 64], in_=src[d, hh_]), "WBD", writes=["WBD"])
            rr = self.sb("rr", [128, 2304], F32)
            aa = self.sb("aa", [128, 2304], F32)
            bb = self.sb("bb", [128, 2304], F32)
            hh = self.sb("hh", [128, 2304], F32)

            SEGS = [(0, 1024), (1024, 2048), (2048, 2304)]

            def seg_chain(c, d, si, need_sum):
                a0, a1 = SEGS[si]
                kr, kb, ka = ("rr", si), ("bb", si), ("aa", si)
                for c0 in range(a0, a1, 512):
                    n = min(512, a1 - c0)
                    for wi, (dst, dk, bn) in enumerate(((rr, kr, f"ba{d}"), (bb, kb, f"bx{d}"))):
                        pb, pk = nextp()
                        op("pe", lambda e, wi=wi, c0=c0, n=n, pb=pb: e.matmul(pb[:, 0:n], lhsT=WBD[:, c, d, wi, :], rhs=xcv[:, c, c0:c0 + n], start=True, stop=True), reads=["WBD", ("xcv", c)], writes=[pk])
                        op("act", lambda e, dst=dst, c0=c0, n=n, pb=pb, bn=bn: e.activation(out=dst[:, c0:c0 + n], in_=pb[:, 0:n], func=AF.Sigmoid, bias=V(bn, 1, c)), reads=[pk, "vecs"], writes=[dk])
                        yield
                if need_sum and si < 2:
                    op("dve", lambda e: e.tensor_reduce(out=sumr[:, d, si:si + 1], in_=rr[:, a0:a1], axis=AX.X, op=ALU.add), reads=[kr], writes=[("sumr", d, si)])
                    yield
                op("dve", lambda e: e.tensor_tensor(out=bb[:, a0:a1], in0=bb[:, a0:a1], in1=xcv[:, c, a0:a1], op=ALU.mult), reads=[kb, ("xcv", c)], writes=[kb])
                yield
                op("act", lambda e: e.activation(out=aa[:, a0:a1], in_=rr[:, a0:a1], func=AF.Exp, scale=cl[:, d, c:c + 1]), reads=[kr, "cl"], writes=[ka])
                yield
                op("pool", lambda e: e.tensor_tensor(out=rr[:, a0:a1], in0=aa[:, a0:a1], in1=aa[:, a0:a1], op=ALU.mult), reads=[ka], writes=[kr])
                yield
                op("act", lambda e: e.activation(out=rr[:, a0:a1], in_=rr[:, a0:a1], func=AF.Sqrt, scale=-1.0, bias=1.0), reads=[kr], writes=[kr])
                yield
                op("pool", lambda e: e.tensor_tensor(out=bb[:, a0:a1], in0=bb[:, a0:a1], in1=rr[:, a0:a1], op=ALU.mult), reads=[kr, kb], writes=[kb])
                yield

            def seg_scans(c, d, dst, dn, init_ap):
                sc_ = lambda a0, a1, ini: (lambda e: e.tensor_tensor_scan(out=rv(dst[:, a0:a1], d), data0=rv(aa[:, a0:a1], d), data1=rv(bb[:, a0:a1], d), initial=ini, op0=ALU.mult, op1=ALU.add))
                op("dve", sc_(2048, 2304, 0.0), reads=[("aa", 2), ("bb", 2)], writes=[(dn, 2)])
                order = [0, 1] if d == 0 else [1, 0]
                s0, s1 = order
                a0, a1 = SEGS[s0]
                extra = ["hin"] if init_ap is not None else []
                op("dve", sc_(a0, a1, init_ap if init_ap is not None else 0.0), reads=[("aa", s0), ("bb", s0)] + extra, writes=[(dn, s0)])
                carry = a1 - 1 if d == 0 else a0
                b0, b1 = SEGS[s1]
                op("dve", sc_(b0, b1, dst[:, carry:carry + 1]), reads=[("aa", s1), ("bb", s1), (dn, s0)], writes=[(dn, s1)])

            def lru_cd(c, d, need_sum):
                order = [2, 0, 1] if d == 0 else [2, 1, 0]
                self.interleave([seg_chain(c, d, si, need_sum) for si in order], 3)

            for c in range(4):
                for d in range(2):
                    lru_cd(c, d, True)
                    seg_scans(c, d, hh, "hh", None)
                    last = 2303 if d == 0 else 2048
                    op("dve", lambda e, c=c, d=d, last=last: e.tensor_copy(out=CF[:, c, d:d + 1], in_=hh[:, last:last + 1]), reads=[("hh", 2)], writes=["CF"])
                    last = 2047 if d == 0 else 0
                    op("dve", lambda e, c=c, d=d, last=last: e.tensor_copy(out=PS[:, c, 2 * d + 1:2 * d + 2], in_=hh[:, last:last + 1]), reads=[("hh", 0), ("hh", 1)], writes=["PS"])
                    op("dve", lambda e, d=d: e.tensor_tensor(out=sumr[:, d, 0:1], in0=sumr[:, d, 0:1], in1=sumr[:, d, 1:2], op=ALU.add), reads=[("sumr", d, 0), ("sumr", d, 1)], writes=[("sumr", d, 0)])
                    op("act", lambda e, c=c, d=d: e.activation(out=PS[:, c, 2 * d:2 * d + 1], in_=sumr[:, d, 0:1], func=AF.Exp, scale=cl[:, d, c:c + 1]), reads=[("sumr", d, 0), "cl"], writes=["PS"])
            PSall = self.sb("PSall", [128, 4, 16], F32)
            dma("sp", lambda e: e.dma_start(out=cc1_in, in_=PS[:].rearrange("p c q -> p (c q)")), "cc1a", reads=["PS"], writes=["cc1_in"])
            dma("pool", lambda e: e.collective_compute("AllGather", ALU.bypass, replica_groups=RG, ins=[cc1_in.opt()], outs=[cc1_out.opt()]), "cc1", reads=["cc1_in"], writes=["cc1_out"], inc=1)
            dma("sp", lambda e: e.dma_start(out=PSall[:], in_=cc1_out.rearrange("(r p) f -> p r f", p=128)), "cc1b", reads=["cc1_out"], writes=["PSall"])
            hin = self.sb("hin", [128, 2, 4], F32)
            hcur = self.sb("hcur", [128, 4], F32)
            PSv = PSall[:].rearrange("p r (c q) -> p r c q", q=4)
            for d in range(2):
                order = [0, 1, 2, 3] if d == 0 else [3, 2, 1, 0]
                op("dve", lambda e, d=d: e.tensor_copy(out=hcur[:], in_=CF[:, :, d]), reads=["CF"], writes=["hcur"])
                op("dve", lambda e, d=d, r0=order[0]: e.tensor_scalar(out=hin[:, d, :], in0=hcur[:], scalar1=V("sel", 1, r0), scalar2=None, op0=ALU.mult), reads=["hcur", "vecs"], writes=["hin"])
                for i_ in range(3):
                    r = order[i_]
                    rn = order[i_ + 1]
                    op("dve", lambda e, d=d, r=r: e.tensor_tensor(out=hcur[:], in0=hcur[:], in1=PSv[:, r, :, 2 * d], op=ALU.mult), reads=["hcur", "PSall"], writes=["hcur"])
                    op("dve", lambda e, d=d, r=r: e.tensor_tensor(out=hcur[:], in0=hcur[:], in1=PSv[:, r, :, 2 * d + 1], op=ALU.add), reads=["hcur", "PSall"], writes=["hcur"])
                    op("dve", lambda e, d=d, rn=rn: e.scalar_tensor_tensor(out=hin[:, d, :], in0=hcur[:], scalar=V("sel", 1, rn), in1=hin[:, d, :], op0=ALU.mult, op1=ALU.add), reads=["hcur", "hin", "vecs"], writes=["hin"])
            for c in range(4):
                for d in range(2):
                    lru_cd(c, d, False)
                    if d == 0:
                        seg_scans(c, d, hh, "hh", hin[:, d, c:c + 1])
                    else:
                        seg_scans(c, d, rr, "rr", hin[:, d, c:c + 1])
                allk = [("hh", i) for i in range(3)] + [("rr", i) for i in range(3)]
                op("dve", lambda e: e.tensor_tensor(out=hh[:], in0=hh[:], in1=rr[:], op=ALU.add), reads=allk, writes=[("hh", i) for i in range(3)])
                op("pool", lambda e, c=c: e.tensor_tensor(out=mixB[:, c, :], in0=hh[:], in1=gg[:, c, :], op=ALU.mult), reads=[("hh", i) for i in range(3)] + [("gg", c)], writes=[("mixB", c)])
            if self.stop_after == "lru":
                self.dump("mixB", mixB[:], [128, 4, 2304], [("mixB", c) for c in range(4)], BF16)
                self.dump("hin", hin[:], [128, 2, 4], ["hin"])
                return True
            self.stage_end()
            self.stage_end()
            self.stage_begin()
            Wv = self.sb("Wv", [128, 8, 512], BF16)
            Vaug = self.sb("Vaug", [128, 22, 8, 66], BF16)
            op("pool", lambda e: e.memset(Vaug[:], 1.0), writes=["Vaug"])
            for kc in range(8):
                dma("pool", lambda e, kc=kc: e.dma_start(out=Wv[:, kc, :], in_=w_in_e[kc * 128:(kc + 1) * 128, 1024:1536]), "Wv", writes=["Wv"])
            QT = [self.sb(f"QT{i}", [128, 2304], BF16) for i in range(2)]
            KT = [self.sb(f"KT{i}", [128, 2816], BF16) for i in range(2)]
            Wqk = [self.sb(f"Wqk{i}", [128, 8, 128], BF16) for i in range(2)]
            tab = [self.sb(f"tab{i}", [128, 22 * 64], BF16) for i in range(2)]
            PT = [self.sb(f"PT{i}", [128, 512], BF16) for i in range(3)]
            rec = self.sb("rec", [128, 512], F32)
            Ou = self.sb("Ou", [64, 512], F32)
            mtmp = self.sb("mtmp", [64, 2304], BF16)
            for i in range(2):
                op("pool", lambda e, i=i: e.memset(QT[i][64:104, 2048:2304], 0.0), writes=[f"QTa{i}"])
                op("pool", lambda e, i=i: e.memset(KT[i][64:104, 2560:2816], 0.0), writes=[f"KTa{i}"])
                dma("pool", lambda e, i=i: e.dma_start(out=QT[i][64:104, 0:2048], in_=qaug_d.ap()), f"QTa{i}", reads=[f"QTa{i}"], writes=[f"QTa{i}"])
                dma("pool", lambda e, i=i: e.dma_start(out=KT[i][64:104, 0:2560], in_=kaug_d.ap()), f"KTa{i}", reads=[f"KTa{i}"], writes=[f"KTa{i}"])
            for kt in range(22):
                c0 = kt * 128
                pb, pk = nextp()
                for kc in range(8):
                    op("pe", lambda e, kc=kc, c0=c0, pb=pb: e.matmul(pb[:], lhsT=uT[:, kc, c0:c0 + 128], rhs=Wv[:, kc, :], start=(kc == 0), stop=(kc == 7)), reads=["Wv"] + uk(c0, c0 + 128), writes=[pk], pe_acc=True)
                src = pb[:].rearrange("p (h d) -> p h d", d=64)
                if kt % 2 == 0:
                    op("act", lambda e, kt=kt, src=src: e.activation(out=Vaug[:, kt, :, 0:64], in_=src, func=AF.Copy), reads=[pk], writes=[("Vaug", kt)])
                else:
                    op("dve", lambda e, kt=kt, src=src: e.tensor_copy(out=Vaug[:, kt, :, 0:64], in_=src), reads=[pk], writes=[("Vaug", kt)])
            SB = [(pA, "pA"), (pB, "pB"), (pY[0], "pY0"), (pY[1], "pY1")]
            si = [0]
            pti = [0]

            PT4 = PT + [self.sb(f"PT3x{i}", [128, 512], BF16) for i in range(5)]
            recs = [rec, self.sb("rec_b", [128, 512], F32)]
            Ous = [Ou, self.sb("Ou_b", [64, 512], F32)]

            def na_head(h, hb_):
                items = []
                for qb in range(4):
                    kts = [(4 * qb + j, j) for j in range(8)] + [(20, None), (21, None)]
                    for idx, (kt, j) in enumerate(kts):
                        items.append((qb, qb * 512, 512, kt, j, idx == 0, idx == len(kts) - 1))
                for idx, kt in enumerate((20, 21)):
                    items.append((4, 2048, 256, kt, None, idx == 0, idx == 1))
                pend = {}
                LA = 3

                def qk(i):
                    blk, q0, nq, kt, j, first, last = items[i]
                    Sp, Sk = SB[i % 4]
                    P_, Pk = PT4[i % 8], f"PTn{i % 8}"
                    if j is not None:
                        op("pe", lambda e: e.matmul(Sp[:, 0:nq], lhsT=KT[hb_][0:104, kt * 128:(kt + 1) * 128], rhs=QT[hb_][0:104, q0:q0 + nq], start=True, stop=False), reads=[f"KT{hb_}", f"KTa{hb_}", f"QT{hb_}", f"QTa{hb_}"], writes=[Sk])
                        t0 = (14 - 2 * j) * 64
                        op("pe", lambda e: e.matmul(Sp[:, 0:nq], lhsT=ident_b[:], rhs=tab[hb_][:, t0:t0 + nq], start=False, stop=True), reads=["ident_b", f"tab{hb_}"], writes=[Sk], pe_acc=True)
                    else:
                        op("pe", lambda e: e.matmul(Sp[:, 0:nq], lhsT=KT[hb_][0:104, kt * 128:(kt + 1) * 128], rhs=QT[hb_][0:104, q0:q0 + nq], start=True, stop=True), reads=[f"KT{hb_}", f"KTa{hb_}", f"QT{hb_}", f"QTa{hb_}"], writes=[Sk])
                    op("act", lambda e: e.activation(out=P_[:, 0:nq], in_=Sp[:, 0:nq], func=AF.Exp), reads=[Sk], writes=[Pk])

                def pv(i, step):
                    blk, q0, nq, kt, j, first, last = items[i]
                    O, Ok = pY[2 + blk % 2], f"pY{2 + blk % 2}"
                    P_, Pk = PT4[i % 8], f"PTn{i % 8}"
                    op("pe", lambda e: e.matmul(O[0:66, 0:nq], lhsT=Vaug[:, kt, h, :], rhs=P_[:, 0:nq], start=first, stop=last), reads=[("Vaug", kt), Pk], writes=[Ok], pe_acc=True)
                    if last:
                        rec, rk_ = recs[blk % 2], f"rec{blk % 2}"
                        Ou, ok_ = Ous[blk % 2], f"Ou{blk % 2}"
                        op("dve", lambda e: e.reciprocal(out=rec[64:65, 0:nq], in_=O[64:65, 0:nq]), reads=[Ok], writes=[rk_])
                        op("act", lambda e: e.activation(out=Ou[:, 0:nq], in_=O[0:64, 0:nq], func=AF.Copy), reads=[Ok], writes=[ok_])

                        def fin():
                            op("pe", lambda e: e.matmul(pT[0:64, 0:nq], lhsT=ones_f[64:65, 0:64], rhs=rec[64:65, 0:nq], start=True, stop=True), reads=["ones_f", rk_], writes=["pT"])
                            if h % 2 == 0:
                                op("dve", lambda e: e.tensor_tensor(out=mixA[0:64, h // 2, q0:q0 + nq], in0=Ou[:, 0:nq], in1=pT[0:64, 0:nq], op=ALU.mult), reads=[ok_, "pT"], writes=[("mixA", h // 2, 0)])
                            else:
                                op("dve", lambda e: e.tensor_tensor(out=mtmp[:, q0:q0 + nq], in0=Ou[:, 0:nq], in1=pT[0:64, 0:nq], op=ALU.mult), reads=[ok_, "pT"], writes=["mtmp"])
                        pend.setdefault(step + 2, []).append(fin)

                nI = len(items)
                for step in range(nI + LA + 4):
                    if step < nI:
                        qk(step)
                    if 0 <= step - LA < nI:
                        pv(step - LA, step)
                    for f_ in pend.pop(step, []):
                        f_()

            mg1 = mod_gen(1, False)
            for h in range(8):
                hb_ = h % 2
                dma("pool", lambda e, h=h, hb_=hb_: e.dma_start(out=Wqk[hb_][:, :, 0:64], in_=w_in_e[:, h * 64:(h + 1) * 64].rearrange("(kc p) n -> p kc n", p=128)), f"Wqk{hb_}", writes=[f"Wqk{hb_}"])
                dma("pool", lambda e, h=h, hb_=hb_: e.dma_start(out=Wqk[hb_][:, :, 64:128], in_=w_in_e[:, 512 + h * 64:512 + (h + 1) * 64].rearrange("(kc p) n -> p kc n", p=128)), f"Wqk{hb_}", writes=[f"Wqk{hb_}"])
                dma("pool", lambda e, h=h, hb_=hb_: e.dma_start(out=tab[hb_][:], in_=natab_d[:, h, :]), f"tab{hb_}", writes=[f"tab{hb_}"])
                for (c0, n, d0) in [(256 + g * 512, 512, g * 512) for g in range(4)] + [(2560, 256, 2048)]:
                    pb, pk = nextp()
                    for kc in range(8):
                        op("pe", lambda e, kc=kc, c0=c0, n=n, pb=pb, hb_=hb_: e.matmul(pb[0:64, 0:n], lhsT=Wqk[hb_][:, kc, 0:64], rhs=uT[:, kc, c0:c0 + n], start=(kc == 0), stop=(kc == 7)), reads=[f"Wqk{hb_}"] + uk(c0, c0 + n), writes=[pk], pe_acc=True)
                    op("act", lambda e, n=n, d0=d0, pb=pb, hb_=hb_: e.activation(out=QT[hb_][0:64, d0:d0 + n], in_=pb[0:64, 0:n], func=AF.Copy, scale=0.125), reads=[pk], writes=[f"QT{hb_}"])
                for (c0, n) in [(g * 512, 512) for g in range(5)] + [(2560, 256)]:
                    pb, pk = nextp()
                    for kc in range(8):
                        op("pe", lambda e, kc=kc, c0=c0, n=n, pb=pb, hb_=hb_: e.matmul(pb[0:64, 0:n], lhsT=Wqk[hb_][:, kc, 64:128], rhs=uT[:, kc, c0:c0 + n], start=(kc == 0), stop=(kc == 7)), reads=[f"Wqk{hb_}"] + uk(c0, c0 + n), writes=[pk], pe_acc=True)
                    op("dve", lambda e, n=n, c0=c0, pb=pb, hb_=hb_: e.tensor_copy(out=KT[hb_][0:64, c0:c0 + n], in_=pb[0:64, 0:n]), reads=[pk], writes=[f"KT{hb_}"])
                na_head(h, hb_)
                next(mg1, None)
                if h % 2 == 1:
                    dma("sp", lambda e, h=h: e.dma_start(out=mixA[64:128, h // 2, :], in_=mtmp[:, :]), "mixAup", reads=["mtmp"], writes=[("mixA", h // 2, 1)])
            for _ in mg1:
                pass
            if self.stop_after == "na":
                self.dump("mixA", mixA[:], [128, 4, 2304], [("mixA", c, q) for c in range(4) for q in range(2)], BF16)
                return True
            self.stage_end()
            def old0(tt):
                return (xh[256 + tt * 128:256 + (tt + 1) * 128, :] if tt < 16 else ctxb[(tt - 16) * 128:(tt - 15) * 128, :]), []

            def mix0(kc):
                if kc < 4:
                    return mixA[:, kc, :], [("mixA", kc, 0), ("mixA", kc, 1)]
                return mixB[:, kc - 4, :], [("mixB", kc - 4)]

            self.stage_begin()
            W1a0, W2h0 = ffn_alloc()
            out_proj(0, w_out_e, old0, mix0, after_wo=lambda: ffn_load(0, 0, W1a0, "W1h0", W2h0))
            if self.stop_after == "p4":
                self.dump("u2T", uT[:, :, 0:2304], [128, 8, 2304], uk(0, 2304), BF16)
                return True

            def fin0(tt, xb, xk, tb, tk):
                dma("sp", lambda e, xb=xb, tt=tt: e.dma_start(out=h1[tt * 128:(tt + 1) * 128, :], in_=xb[:]), "h1st_" + xk, reads=[xk], writes=[("h1", tt)])

            ffn(0, fin0, W1a0, W2h0)
            self.stage_end()
            if self.stop_after == "l0":
                if "h1" in self.debug:
                    d = self.dout("dbg_h1", [2304, 1024])
                    self.dbg_out["h1"] = d
                    S.dma("sp", lambda e: e.dma_start(out=d, in_=h1), "dbg", reads=[("h1", t) for t in range(18)], writes=[("dbg", "h1")])
                return True

        if not self.skip_l0:
            if layer0():
                return self.finish()
        else:
            self.stage_begin()
            xtz = [self.sb(f"xtz{i}", [128, 1024], F32) for i in range(2)]
            for tt in range(18):
                xb, xk = xtz[tt % 2], f"xtz{tt % 2}"
                src = xh[256 + tt * 128:256 + (tt + 1) * 128, :] if tt < 16 else ctxb[(tt - 16) * 128:(tt - 15) * 128, :]
                dma("sp", lambda e, xb=xb, src=src: e.dma_start(out=xb[:], in_=src), xk, writes=[xk])
                dma("sp", lambda e, xb=xb, tt=tt: e.dma_start(out=h1[tt * 128:(tt + 1) * 128, :], in_=xb[:]), f"xtzs{tt % 2}", reads=[xk], writes=[("h1", tt)])
            self.stage_end()
        self.stage_begin()
        xt = [self.sb(f"xt{i}", [128, 1024], F32) for i in range(3)]
        def n1btile(tt):
            xb, xk = xt[tt % 3], f"xt{tt % 3}"
            dma("sp", lambda e: e.dma_start(out=xb[:], in_=h1[tt * 128:(tt + 1) * 128, :]), xk, reads=[("h1", tt)], writes=[xk])
            yield
            yield from norm_gen(xb, xk, 0 if tt < 16 else 1, tt * 128, G1b, S1b, "G1b", "S1b")

        self.interleave([n1btile(tt) for tt in range(18)], 2)
        self.stage_end()

        self.stage_begin()
        Wq = self.sb("Wq", [128, 8, 512], BF16)
        Wkv = self.sb("Wkv", [128, 8, 256], BF16)
        for kc in range(8):
            dma("pool", lambda e, kc=kc: e.dma_start(out=Wq[:, kc, :], in_=w_in_o[kc * 128:(kc + 1) * 128, 1568:2080]), "Wq", writes=["Wq"])
            dma("pool", lambda e, kc=kc: e.dma_start(out=Wkv[:, kc, :], in_=w_in_o[kc * 128:(kc + 1) * 128, 2080:2336]), "Wkv", writes=["Wkv"])
        qnw = self.sb("qnw", [128, 64], F32)
        knw = self.sb("knw", [128, 64], F32)
        dma("sp", lambda e: e.dma_start(out=qnw[:], in_=qnw_d.ap()), "qnw", writes=["qnw"])
        dma("sp", lambda e: e.dma_start(out=knw[:], in_=knw_d.ap()), "knw", writes=["knw"])
        cosb = [self.sb(f"cosb{i}", [128, 64], F32) for i in range(2)]
        sinb = [self.sb(f"sinb{i}", [128, 2, 32], F32) for i in range(2)]
        wkA = [[self.sb(f"wk{j}_{i}", [128, 512], F32) for i in range(5)] for j in range(2)]
        ssbA = [self.sb(f"ssb{j}", [128, 4, 8], F32) for j in range(2)]
        wkc = [0]
        QTst = [self.sb(f"QTst{i}", [64, 8, 128], BF16) for i in range(2)]
        KTst = [self.sb(f"KTst{i}", [64, 2, 128], BF16) for i in range(2)]
        Vst = [self.sb(f"Vst{i}", [128, 128], BF16) for i in range(2)]

        def qk_gen(ps_ap, pkey, H, wbc, wbk, rope, tb):
            n = H * 64
            wkc[0] += 1
            j_ = wkc[0] % 2
            sq, qn, A_, t1, ob = wkA[j_]
            ssb = ssbA[j_]
            kk = [f"wk{j_}_{i}" for i in range(5)]
            sk_ = f"ssb{j_}"
            ps3 = ps_ap.rearrange("p (h d) -> p h d", d=64)
            op("act", lambda e: e.activation(out=sq[:, 0:n], in_=ps_ap, func=AF.Square), reads=[pkey], writes=[kk[0]])
            yield
            op("dve", lambda e: e.tensor_reduce(out=ssb[:, 0, 0:H], in_=sq[:, 0:n].rearrange("p (h d) -> p h d", d=64), axis=AX.X, op=ALU.add), reads=[kk[0]], writes=[sk_])
            yield
            op("dve", lambda e: e.tensor_scalar(out=ssb[:, 1, 0:H], in0=ssb[:, 0, 0:H], scalar1=1.0 / 64, scalar2=EPS, op0=ALU.mult, op1=ALU.add), reads=[sk_], writes=[sk_])
            yield
            op("act", lambda e: e.activation(out=ssb[:, 2, 0:H], in_=ssb[:, 1, 0:H], func=AF.Sqrt), reads=[sk_], writes=[sk_])
            yield
            op("dve", lambda e: e.reciprocal(out=ssb[:, 3, 0:H], in_=ssb[:, 2, 0:H]), reads=[sk_], writes=[sk_])
            yield
            qn3 = qn[:, 0:n].rearrange("p (h d) -> p h d", d=64)
            op("dve", lambda e: e.tensor_tensor(out=qn3, in0=ps3, in1=ssb[:, 3, 0:H].unsqueeze(2).to_broadcast([128, H, 64]), op=ALU.mult), reads=[pkey, sk_], writes=[kk[1]])
            yield
            op("pool", lambda e: e.tensor_tensor(out=qn3, in0=qn3, in1=wbc[:].unsqueeze(1).to_broadcast([128, H, 64]), op=ALU.mult), reads=[kk[1], wbk], writes=[kk[1]])
            yield
            if not rope:
                return qn, kk[1]
            cb_, sb_ = cosb[tb], sinb[tb]
            A3 = A_[:, 0:n].rearrange("p (h d) -> p h d", d=64)
            op("dve", lambda e: e.tensor_tensor(out=A3, in0=qn3, in1=cb_[:].unsqueeze(1).to_broadcast([128, H, 64]), op=ALU.mult), reads=[kk[1], f"cosb{tb}"], writes=[kk[2]])
            yield
            qv = qn[:, 0:n].rearrange("p (h f s x) -> p h f s x", f=2, s=2, x=16)
            tv = t1[:, 0:n].rearrange("p (h f s x) -> p h f s x", f=2, s=2, x=16)
            sin3 = sb_[:, 0, :].rearrange("p (f x) -> p f x", x=16).unsqueeze(1).to_broadcast([128, H, 2, 16])
            nsin3 = sb_[:, 1, :].rearrange("p (f x) -> p f x", x=16).unsqueeze(1).to_broadcast([128, H, 2, 16])
            op("pool", lambda e: e.tensor_tensor(out=tv[:, :, :, 0, :], in0=qv[:, :, :, 1, :], in1=nsin3, op=ALU.mult), reads=[kk[1], f"sinb{tb}"], writes=[kk[3]])
            yield
            op("pool", lambda e: e.tensor_tensor(out=tv[:, :, :, 1, :], in0=qv[:, :, :, 0, :], in1=sin3, op=ALU.mult), reads=[kk[1], f"sinb{tb}"], writes=[kk[3]])
            yield
            op("dve", lambda e: e.tensor_tensor(out=ob[:, 0:n], in0=A_[:, 0:n], in1=t1[:, 0:n], op=ALU.add), reads=[kk[2], kk[3]], writes=[kk[4]])
            yield
            return ob, kk[4]

        def q_chain(tt):
            tb = tt % 2
            dma("sp", lambda e: e.dma_start(out=cosb[tb][:], in_=cosT_d[tt * 128:(tt + 1) * 128, :]), f"cosb{tb}", writes=[f"cosb{tb}"])
            dma("sp", lambda e: e.dma_start(out=sinb[tb][:, 0, :], in_=sinT_d[tt * 128:(tt + 1) * 128, :]), f"sinb{tb}", writes=[f"sinb{tb}"])
            dma("sp", lambda e: e.dma_start(out=sinb[tb][:, 1, :], in_=nsinT_d[tt * 128:(tt + 1) * 128, :]), f"sinb{tb}", writes=[f"sinb{tb}"])
            yield
            pb, pk = nextp()
            for kc in range(8):
                op("pe", lambda e, kc=kc: e.matmul(pb[:, 0:512], lhsT=uT[:, kc, tt * 128:(tt + 1) * 128], rhs=Wq[:, kc, :], start=(kc == 0), stop=(kc == 7)), reads=["Wq"] + uk(tt * 128, tt * 128 + 128), writes=[pk], pe_acc=True)
            yield
            qr, qrk = yield from qk_gen(pb[:, 0:512], pk, 8, qnw, "qnw", True, tb)
            for h in range(8):
                op("pe", lambda e, h=h: e.transpose(out=pT[0:64, h * 128:(h + 1) * 128], in_=qr[:, h * 64:(h + 1) * 64], identity=ident_f[:]), reads=[qrk, "ident_f"], writes=["pT"], pe_acc=True)
            yield
            op("act", lambda e: e.activation(out=QTst[tb][:], in_=pT[0:64, :].rearrange("p (h t) -> p h t", t=128), func=AF.Copy), reads=["pT"], writes=[f"QTst{tb}"])
            yield
            dma("sp", lambda e: e.dma_start(out=qscr[:, :, tt * 128:(tt + 1) * 128], in_=QTst[tb][:]), f"QTst{tb}", reads=[f"QTst{tb}"], writes=["qscr"])
            yield

        def kv_chain(tt):
            tb = tt % 2
            lat = tt < 16
            pb, pk = nextp()
            for kc in range(8):
                op("pe", lambda e, kc=kc: e.matmul(pb[:, 0:256], lhsT=uT[:, kc, tt * 128:(tt + 1) * 128], rhs=Wkv[:, kc, :], start=(kc == 0), stop=(kc == 7)), reads=["Wkv"] + uk(tt * 128, tt * 128 + 128), writes=[pk], pe_acc=True)
            yield
            op("act", lambda e: e.activation(out=Vst[tb][:], in_=pb[:, 128:256], func=AF.Copy), reads=[pk], writes=[f"Vst{tb}"])
            yield
            if lat:
                dma("sp", lambda e: e.dma_start(out=cc2v_in[tt * 128:(tt + 1) * 128, :], in_=Vst[tb][:]), f"Vst{tb}", reads=[f"Vst{tb}"], writes=["cc2v_in"])
            else:
                dma("sp", lambda e: e.dma_start(out=vctx[(tt - 16) * 128:(tt - 15) * 128, :], in_=Vst[tb][:]), f"Vst{tb}", reads=[f"Vst{tb}"], writes=["vctx"])
            yield
            kr, krk = yield from qk_gen(pb[:, 0:128], pk, 2, knw, "knw", lat, tb)
            for h in range(2):
                op("pe", lambda e, h=h: e.transpose(out=pY[2][0:64, h * 128:(h + 1) * 128], in_=kr[:, h * 64:(h + 1) * 64], identity=ident_f[:]), reads=[krk, "ident_f"], writes=["pY2"], pe_acc=True)
            yield
            op("act", lambda e: e.activation(out=KTst[tb][:], in_=pY[2][0:64, 0:256].rearrange("p (h t) -> p h t", t=128), func=AF.Copy), reads=["pY2"], writes=[f"KTst{tb}"])
            yield
            if lat:
                dma("sp", lambda e: e.dma_start(out=cc2k_in.rearrange("p (h t) -> p h t", h=2)[:, :, tt * 128:(tt + 1) * 128], in_=KTst[tb][:]), f"KTst{tb}", reads=[f"KTst{tb}"], writes=["cc2k_in"])
            else:
                dma("sp", lambda e: e.dma_start(out=kctx[:, :, (tt - 16) * 128:(tt - 15) * 128], in_=KTst[tb][:]), f"KTst{tb}", reads=[f"KTst{tb}"], writes=["kctx"])
            yield

        chains = []
        for tt in range(18):
            if tt < 16:
                chains.append(q_chain(tt))
            chains.append(kv_chain(tt))
        self.interleave(chains, 2)
        dma("pool", lambda e: e.collective_compute("AllGather", ALU.bypass, replica_groups=RG, ins=[cc2k_in.opt()], outs=[cc2k_out.opt()]), "cc2k", reads=["cc2k_in"], writes=["cc2k_out"], inc=1)
        dma("pool", lambda e: e.collective_compute("AllGather", ALU.bypass, replica_groups=RG, ins=[cc2v_in.opt()], outs=[cc2v_out.opt()]), "cc2v", reads=["cc2v_in"], writes=["cc2v_out"], inc=1)
        self.stage_end()
        self.stage_begin()
        Wgqk = self.sb("Wgqk", [128, 8, 512], BF16)
        for kc in range(8):
            dma("pool", lambda e, kc=kc: e.dma_start(out=Wgqk[:, kc, :], in_=w_in_o[kc * 128:(kc + 1) * 128, 0:512]), "Wgqk", writes=["Wgqk"])
        vtok = self.sb("vtok", [128, 18, 512], BF16)
        OL = mixA
        qd = self.sb("qd", [128, 2, 2, 2048], BF16)
        lrT = self.sb("lrT", [16, 2, 2304], BF16)
        PK = self.sb("PK", [128, 2, 2, 129], F32)
        Sctx = self.sb("Sctx", [128, 2, 2, 128], F32)
        cm = self.sb("cm", [128, 2304], BF16)
        op("pool", lambda e: e.memset(cm[:], 1.0), writes=["cm"])
        op("pool", lambda e: e.memset(cm[:].rearrange("p (c t) -> p c t", t=64)[:, :, 0:1], 0.0), reads=["cm"], writes=["cm"])
        gmask = self.sb("gmask", [128, 4, 128], F32)
        dma("sp", lambda e: e.dma_start(out=gmask[:], in_=gmask_d.ap()), "gmask", writes=["gmask"])
        wa2 = self.sb("wa2", [16, 2, 256], BF16)
        for d in range(2):
            dma("pool", lambda e, d=d: e.dma_start(out=wa2[:, d, :], in_=gla_wa2_d[d]), "wa2", writes=["wa2"])
        barow = self.sb("barow", [1, 512], F32)
        dma("sp", lambda e: e.dma_start(out=barow[:], in_=barow_d.ap()), "barow", writes=["barow"])
        nba = self.sb("nba", [128, 2, 2], F32)
        for d in range(2):
            op("dve", lambda e, d=d: e.tensor_scalar(out=nba[:, d, :], in0=V(f"gba{d}", 2), scalar1=-1.0, scalar2=None, op0=ALU.mult), reads=["vecs"], writes=["nba"])
        ones_b = self.sb("ones_b", [128, 128], BF16)
        op("pool", lambda e: e.tensor_copy(out=ones_b[:], in_=ones_f[:]), reads=["ones_f"], writes=["ones_b"])
        Sst = self.sb("Sst", [128, 128], F32)
        Sbf = [self.sb(f"Sbf{i}", [128, 128], BF16) for i in range(2)]
        PTl4 = [self.sb(f"PTl{i}", [128, 128], BF16) for i in range(4)]
        G5 = [(g * 512, 512) for g in range(4)] + [(2048, 256)]
        if self.stop_after == "gla0":
            self.dump("cm", cm[:], [128, 2304], ["cm", "gmask", "wa2", "barow", "nba", "ones_b"], BF16)
            return self.finish()
        self.stage_begin()
        Wgv = self.sb("Wgv", [128, 8, 512], BF16)
        Wlr = self.sb("Wlr", [128, 8, 32], BF16)
        for kc in range(8):
            dma("pool", lambda e, kc=kc: e.dma_start(out=Wgv[:, kc, :], in_=w_in_o[kc * 128:(kc + 1) * 128, 512:1024]), "Wgv", writes=["Wgv"])
            dma("pool", lambda e, kc=kc: e.dma_start(out=Wlr[:, kc, :], in_=w_in_o[kc * 128:(kc + 1) * 128, 1536:1568]), "Wlr", writes=["Wlr"])
        for tt in range(18):
            pb, pk = nextp()
            for kc in range(8):
                op("pe", lambda e, kc=kc, tt=tt, pb=pb: e.matmul(pb[:, 0:512], lhsT=uT[:, kc, tt * 128:(tt + 1) * 128], rhs=Wgv[:, kc, :], start=(kc == 0), stop=(kc == 7)), reads=["Wgv"] + uk(tt * 128, tt * 128 + 128), writes=[pk], pe_acc=True)
            op("act", lambda e, tt=tt, pb=pb: e.activation(out=vtok[:, tt, :], in_=pb[:, 0:512], func=AF.Copy), reads=[pk], writes=[("vtok", tt)])
        for d in range(2):
            for (c0, n) in G5:
                pb, pk = nextp()
                for kc in range(8):
                    op("pe", lambda e, kc=kc, d=d, c0=c0, n=n, pb=pb: e.matmul(pb[0:16, 0:n], lhsT=Wlr[:, kc, d * 16:(d + 1) * 16], rhs=uT[:, kc, c0:c0 + n], start=(kc == 0), stop=(kc == 7)), reads=["Wlr"] + uk(c0, c0 + n), writes=[pk], pe_acc=True)
                op("act", lambda e, d=d, c0=c0, n=n, pb=pb: e.activation(out=lrT[:, d, c0:c0 + n], in_=pb[0:16, 0:n], func=AF.Copy), reads=[pk], writes=["lrT"])
        if self.stop_after == "gla1":
            self.dump("lrT", lrT[:], [16, 2, 2304], ["lrT"], BF16)
            return self.finish()
        self.stage_end()

        def gla_dir(d):
            self.stage_begin()
            kdec = self.sb("kdec", [128, 18, 2, 192], BF16)
            op("pool", lambda e: e.memset(kdec[:], 0.0), writes=["kdec"])
            self.stage_begin()
            gtb = [self.sb(f"gtb{i}", [128, 256], F32) for i in range(2)]
            ekb = [self.sb(f"ekb{i}", [128, 256], F32) for i in range(2)]
            ktb = [self.sb(f"ktb{i}", [128, 256], BF16) for i in range(2)]
            def tok_chain(tt):
                    kt_, ktk = ktb[tt % 2], f"ktb{tt % 2}"
                    pb, pk = nextp()
                    for kc in range(8):
                        op("pe", lambda e, kc=kc, tt=tt, pb=pb: e.matmul(pb[:, 0:256], lhsT=uT[:, kc, tt * 128:(tt + 1) * 128], rhs=Wgqk[:, kc, 256:512], start=(kc == 0), stop=(kc == 7)), reads=["Wgqk"] + uk(tt * 128, tt * 128 + 128), writes=[pk], pe_acc=True)
                    op("dve", lambda e, kt_=kt_, pb=pb: e.tensor_copy(out=kt_[:], in_=pb[:, 0:256]), reads=[pk], writes=[ktk])
                    yield
                    gt, gk = gtb[tt % 2], f"gtb{tt % 2}"
                    ek, ekk = ekb[tt % 2], f"ekb{tt % 2}"
                    pz, pzk = nextp()
                    op("pe", lambda e, tt=tt, pz=pz: e.matmul(pz[:, 0:256], lhsT=lrT[:, d, tt * 128:(tt + 1) * 128], rhs=wa2[:, d, :], start=True, stop=False), reads=["lrT", "wa2"], writes=[pzk])
                    yield
                    op("pe", lambda e, pz=pz: e.matmul(pz[:, 0:256], lhsT=ones_f[0:1, :], rhs=barow[0:1, d * 256:(d + 1) * 256], start=False, stop=True), reads=["ones_f", "barow"], writes=[pzk], pe_acc=True)
                    yield
                    op("act", lambda e, gt=gt, pz=pz: e.activation(out=gt[:], in_=pz[:, 0:256], func=AF.Exp, scale=-1.0), reads=[pzk], writes=[gk])
                    yield
                    op("act", lambda e, gt=gt: e.activation(out=gt[:], in_=gt[:], func=AF.Ln, bias=1.0), reads=[gk], writes=[gk])
                    yield
                    op("dve", lambda e, gt=gt: e.tensor_scalar(out=gt[:], in0=gt[:], scalar1=-1.0 / 16, scalar2=None, op0=ALU.mult), reads=[gk], writes=[gk])
                    yield
                    pc, pck = nextp()
                    op("pe", lambda e, gt=gt, pc=pc: e.matmul(pc[:, 0:256], lhsT=gmask[:, d, :], rhs=gt[:], start=True, stop=True), reads=["gmask", gk], writes=[pck])
                    yield
                    op("act", lambda e, ek=ek, pc=pc: e.activation(out=ek[:], in_=pc[:, 0:256], func=AF.Exp), reads=[pck], writes=[ekk])
                    yield
                    kv_ = kdec[:, tt].rearrange("p q (j c) -> p q j c", c=64)[:, :, 0:3:2, :]
                    op("dve", lambda e, kt_=kt_, ek=ek, kv_=kv_: e.tensor_tensor(out=kv_, in0=kt_[:].rearrange("p (q j c) -> p q j c", q=2, j=2), in1=ek[:].rearrange("p (q j c) -> p q j c", q=2, j=2), op=ALU.mult), reads=[ktk, ekk, "kdec"], writes=[("kdec", tt)])
                    yield

            self.interleave([tok_chain(tt) for tt in range(18)], 2)
            if self.stop_after == "gla2" and d == self.sd:
                self.dump("kdec", kdec[:], [128, 18, 2, 192], ["kdec"] + [("kdec", t) for t in range(18)], BF16)
                return True
            self.stage_end()
            for pair in range(2):
                if gla_pair(d, pair, kdec):
                    return True
            self.stage_end()

        def gla_pair(d, pair, kdec):
            self.stage_begin()
            B1 = self.sb("B1", [128, 2304], F32)
            B2 = self.sb("B2", [128, 2304], F32)
            qin = self.sb("qin", [128, 2304], BF16)
            kin = self.sb("kin", [128, 2304], BF16)
            bl = self.sb("bl", [128, 36], F32)
            dec = self.sb("dec", [128, 36], F32)
            Linc = self.sb("Linc", [128, 32], F32)
            eL = self.sb("eL", [128, 32], F32)
            hA, hB = 2 * pair, 2 * pair + 1
            for (c0, n) in G5:
                pb, pk = nextp()
                op("pe", lambda e, c0=c0, n=n, pb=pb: e.matmul(pb[:, 0:n], lhsT=wa2[:, d, pair * 128:(pair + 1) * 128], rhs=lrT[:, d, c0:c0 + n], start=True, stop=True), reads=["wa2", "lrT"], writes=[pk])
                op("act", lambda e, c0=c0, n=n, pb=pb: e.activation(out=B1[:, c0:c0 + n], in_=pb[:, 0:n], func=AF.Exp, scale=-1.0, bias=nba[:, d, pair:pair + 1]), reads=[pk, "nba"], writes=["B1"])
            op("act", lambda e: e.activation(out=B1[:], in_=B1[:], func=AF.Ln, bias=1.0), reads=["B1"], writes=["B1"])
            op("dve", lambda e: e.tensor_scalar(out=B1[:], in0=B1[:], scalar1=-1.0 / 16, scalar2=None, op0=ALU.mult), reads=["B1"], writes=["B1"])
            for (a_, b_) in ((0, 2048), (2048, 2304)):
                op("dve", lambda e, a_=a_, b_=b_: e.tensor_tensor_scan(out=rv(B2[:, a_:b_], d), data0=cm[:, a_:b_], data1=rv(B1[:, a_:b_], d), initial=0.0, op0=ALU.mult, op1=ALU.add), reads=["B1", "cm"], writes=["B2"])
            o0 = 63 if d == 0 else 0
            op("dve", lambda e: e.tensor_copy(out=bl[:], in_=B2[:].rearrange("p (c t) -> p c t", t=64)[:, :, o0]), reads=["B2"], writes=["bl"])
            op("act", lambda e: e.activation(out=dec[:], in_=bl[:], func=AF.Exp), reads=["bl"], writes=["dec"])
            op("dve", lambda e: e.tensor_tensor_scan(out=rv(Linc[:], d), data0=rv(ones_f[:, 0:32], d), data1=rv(bl[:, 0:32], d), initial=0.0, op0=ALU.mult, op1=ALU.add), reads=["bl", "ones_f"], writes=["Linc"])
            op("dve", lambda e: e.tensor_tensor(out=eL[:], in0=Linc[:], in1=bl[:, 0:32], op=ALU.subtract), reads=["Linc", "bl"], writes=["eL"])
            op("act", lambda e: e.activation(out=eL[:], in_=eL[:], func=AF.Exp), reads=["eL"], writes=["eL"])
            lastc = 31 if d == 0 else 0
            op("act", lambda e: e.activation(out=PK[:, d, pair, 128:129], in_=Linc[:, lastc:lastc + 1], func=AF.Exp), reads=["Linc"], writes=[("PK", d, pair, 1)])
            op("act", lambda e: e.activation(out=B1[:], in_=B2[:], func=AF.Exp), reads=["B2"], writes=["B1"])
            for (c0, n) in G5:
                pb, pk = nextp()
                for kc in range(8):
                    op("pe", lambda e, kc=kc, c0=c0, n=n, pb=pb: e.matmul(pb[:, 0:n], lhsT=Wgqk[:, kc, pair * 128:(pair + 1) * 128], rhs=uT[:, kc, c0:c0 + n], start=(kc == 0), stop=(kc == 7)), reads=["Wgqk"] + uk(c0, c0 + n), writes=[pk], pe_acc=True)
                op("dve", lambda e, c0=c0, n=n, pb=pb: e.scalar_tensor_tensor(out=qin[:, c0:c0 + n], in0=pb[:, 0:n], scalar=0.125, in1=B1[:, c0:c0 + n], op0=ALU.mult, op1=ALU.mult), reads=[pk, "B1"], writes=["qin"])
            op("act", lambda e: e.activation(out=B1[:], in_=B2[:], func=AF.Exp, scale=-1.0), reads=["B2", "qin"], writes=["B1"])
            for (c0, n) in G5:
                pb, pk = nextp()
                for kc in range(8):
                    op("pe", lambda e, kc=kc, c0=c0, n=n, pb=pb: e.matmul(pb[:, 0:n], lhsT=Wgqk[:, kc, 256 + pair * 128:256 + (pair + 1) * 128], rhs=uT[:, kc, c0:c0 + n], start=(kc == 0), stop=(kc == 7)), reads=["Wgqk"] + uk(c0, c0 + n), writes=[pk], pe_acc=True)
                op("dve", lambda e, c0=c0, n=n, pb=pb: e.tensor_tensor(out=kin[:, c0:c0 + n], in0=pb[:, 0:n], in1=B1[:, c0:c0 + n], op=ALU.mult), reads=[pk, "B1"], writes=["kin"])
            op("dve", lambda e: e.tensor_tensor(out=qd[:, d, pair, :].rearrange("p (c t) -> p c t", t=64), in0=qin[:, 0:2048].rearrange("p (c t) -> p c t", t=64), in1=eL[:].unsqueeze(2).to_broadcast([128, 32, 64]), op=ALU.mult), reads=["qin", "eL"], writes=[("qd", d, pair)])

            if self.stop_after == "gla3" and d == self.sd and pair == self.spair:
                self.dump("qin", qin[:], [128, 2304], ["qin", "kin", ("qd", d, pair)], BF16)
                return True

            def run(tiles, with_out):
                op("pool", lambda e: e.memset(Sst[:], 0.0), writes=["Sst"])
                op("pool", lambda e: e.memset(Sbf[0][:], 0.0), writes=["Sbf0"])
                cur = 0
                chs = [0, 1] if d == 0 else [1, 0]
                c1, c2 = chs
                prep = {}

                def partA(idx):
                    tt = tiles[idx]
                    par = idx % 2
                    pus = []
                    for ci, ch in enumerate(chs):
                        rb = ch * 64
                        if par == 0:
                            pu, puk = pY[2 + ci][:, 0:128], f"pY{2 + ci}"
                        else:
                            pu, puk = pT[:, ci * 512:ci * 512 + 128], f"pTu{ci}"
                        op("pe", lambda e, rb=rb, pu=pu: e.matmul(pu, lhsT=kdec[rb:rb + 64, tt, pair, 0:128], rhs=vtok[rb:rb + 64, tt, hA * 128:(hA + 1) * 128], start=True, stop=False), reads=[("kdec", tt), "kdec", ("vtok", tt)], writes=[puk])
                        op("pe", lambda e, rb=rb, pu=pu: e.matmul(pu, lhsT=kdec[rb:rb + 64, tt, pair, 64:192], rhs=vtok[rb:rb + 64, tt, hB * 128:(hB + 1) * 128], start=False, stop=True), reads=[("kdec", tt), "kdec", ("vtok", tt)], writes=[puk], pe_acc=True)
                        pus.append((pu, puk))
                    pts = []
                    if with_out:
                        t0 = tt * 128
                        for hh_ in range(2):
                            hb = hh_ * 64
                            pS, pSk = nextp()
                            P_, Pk = PTl4[par * 2 + hh_], f"PTl{par * 2 + hh_}"
                            op("pe", lambda e, hb=hb, pS=pS: e.matmul(pS[:, 0:128], lhsT=kin[hb:hb + 64, t0:t0 + 128], rhs=qin[hb:hb + 64, t0:t0 + 128], start=True, stop=True), reads=["kin", "qin"], writes=[pSk])
                            op("dve", lambda e, pS=pS, P_=P_: e.tensor_tensor(out=P_[:], in0=pS[:, 0:128], in1=gmask[:, 2 + d, :], op=ALU.mult), reads=[pSk, "gmask"], writes=[Pk])
                            pts.append((P_, Pk))
                    prep[idx] = (pus, pts)

                def partB(idx, cur):
                    tt = tiles[idx]
                    pus, pts = prep.pop(idx)
                    t0 = tt * 128
                    if with_out:
                        for hh_ in range(2):
                            hb = hh_ * 64
                            h = 2 * pair + hh_
                            P_, Pk = pts[hh_]
                            pO, pOk = pY[hh_], f"pY{hh_}"
                            op("pe", lambda e, h=h, P_=P_, pO=pO: e.matmul(pO[:, 0:128], lhsT=vtok[:, tt, h * 128:(h + 1) * 128], rhs=P_[:], start=True, stop=False), reads=[("vtok", tt), Pk], writes=[pOk])
                            op("pe", lambda e, hb=hb, pO=pO: e.matmul(pO[:, c1 * 64:(c1 + 1) * 64], lhsT=Sbf[cur][hb:hb + 64, :], rhs=qin[hb:hb + 64, t0 + c1 * 64:t0 + (c1 + 1) * 64], start=False, stop=False), reads=[f"Sbf{cur}", "qin"], writes=[pOk], pe_acc=True)
                    i1 = tt * 2 + c1
                    op("dve", lambda e: e.scalar_tensor_tensor(out=Sst[:], in0=Sst[:], scalar=dec[:, i1:i1 + 1], in1=pus[0][0], op0=ALU.mult, op1=ALU.add), reads=["Sst", "dec", pus[0][1]], writes=["Sst"])
                    nxt = 1 - cur
                    op("act", lambda e: e.activation(out=Sbf[nxt][:], in_=Sst[:], func=AF.Copy), reads=["Sst"], writes=[f"Sbf{nxt}"])
                    if with_out:
                        for hh_ in range(2):
                            hb = hh_ * 64
                            h = 2 * pair + hh_
                            pO, pOk = pY[hh_], f"pY{hh_}"
                            op("pe", lambda e, hb=hb, pO=pO: e.matmul(pO[:, c2 * 64:(c2 + 1) * 64], lhsT=Sbf[nxt][hb:hb + 64, :], rhs=qin[hb:hb + 64, t0 + c2 * 64:t0 + (c2 + 1) * 64], start=False, stop=True), reads=[f"Sbf{nxt}", "qin"], writes=[pOk], pe_acc=True)
                            if d == 0:
                                op("act", lambda e, h=h, pO=pO: e.activation(out=OL[:, h, t0:t0 + 128], in_=pO[:, 0:128], func=AF.Copy), reads=[pOk], writes=[("OL", h)])
                            else:
                                op("dve", lambda e, h=h, pO=pO: e.tensor_tensor(out=OL[:, h, t0:t0 + 128], in0=OL[:, h, t0:t0 + 128], in1=pO[:, 0:128], op=ALU.add), reads=[pOk, ("OL", h)], writes=[("OL", h)])
                    i2 = tt * 2 + c2
                    op("dve", lambda e: e.scalar_tensor_tensor(out=Sst[:], in0=Sst[:], scalar=dec[:, i2:i2 + 1], in1=pus[1][0], op0=ALU.mult, op1=ALU.add), reads=["Sst", "dec", pus[1][1]], writes=["Sst"])
                    op("act", lambda e: e.activation(out=Sbf[cur][:], in_=Sst[:], func=AF.Copy), reads=["Sst"], writes=[f"Sbf{cur}"])

                partA(0)
                for idx in range(len(tiles)):
                    if idx + 1 < len(tiles):
                        partA(idx + 1)
                    partB(idx, cur)

            run([16, 17] if d == 0 else [17, 16], False)
            op("dve", lambda e: e.tensor_copy(out=Sctx[:, d, pair, :], in_=Sst[:]), reads=["Sst"], writes=[("Sctx", d, pair)])
            if self.stop_after == "gla4" and d == self.sd and pair == self.spair:
                self.dump("Sctx", Sctx[:, d, pair, :], [128, 128], [("Sctx", d, pair)])
                return True
            run(list(range(16)) if d == 0 else list(range(15, -1, -1)), True)
            op("dve", lambda e: e.tensor_copy(out=PK[:, d, pair, 0:128], in_=Sst[:]), reads=["Sst"], writes=[("PK", d, pair, 0)])
            if self.stop_after == "gla5" and d == self.sd and pair == self.spair:
                self.dump("PK", PK[:, d, pair, :], [128, 129], [("PK", d, pair, 0), ("PK", d, pair, 1)])
                return True
            self.stage_end()

        for d in range(2):
            if gla_dir(d):
                return self.finish()
        if self.stop_after == "glalocal":
            self.dump("OL", OL[:, :, 0:2048], [128, 4, 2048], [("OL", h) for h in range(4)], BF16)
            self.dump("PK", PK[:], [128, 2, 2, 129], [("PK", d, p, q) for d in range(2) for p in range(2) for q in range(2)])
            return self.finish()
        PKall = self.sb("PKall", [128, 4, 516], F32)
        pkkeys = [("PK", d, p, q) for d in range(2) for p in range(2) for q in range(2)]
        dma("sp", lambda e: e.dma_start(out=cc3_in, in_=PK[:].rearrange("p d q n -> p (d q n)")), "cc3a", reads=pkkeys, writes=["cc3_in"])
        dma("pool", lambda e: e.collective_compute("AllGather", ALU.bypass, replica_groups=RG, ins=[cc3_in.opt()], outs=[cc3_out.opt()]), "cc3", reads=["cc3_in"], writes=["cc3_out"], inc=1)
        dma("sp", lambda e: e.dma_start(out=PKall[:], in_=cc3_out.rearrange("(r p) f -> p r f", p=128)), "cc3b", reads=["cc3_out"], writes=["PKall"])
        Sin = self.sb("Sin", [128, 2, 2, 128], F32)
        Sinb = self.sb("Sinb", [128, 2, 2, 128], BF16)
        Scur = self.sb("Scur", [128, 128], F32)
        PKv = PKall[:].rearrange("p r (d q n) -> p r d q n", d=2, q=2)
        for d in range(2):
            order = [0, 1, 2, 3] if d == 0 else [3, 2, 1, 0]
            for pair in range(2):
                op("dve", lambda e, d=d, pair=pair: e.tensor_copy(out=Scur[:], in_=Sctx[:, d, pair, :]), reads=[("Sctx", d, pair)], writes=["Scur"])
                op("dve", lambda e, d=d, pair=pair, r0=order[0]: e.tensor_scalar(out=Sin[:, d, pair, :], in0=Scur[:], scalar1=V("sel", 1, r0), scalar2=None, op0=ALU.mult), reads=["Scur", "vecs"], writes=["Sin"])
                for i_ in range(3):
                    r, rn = order[i_], order[i_ + 1]
                    op("dve", lambda e, d=d, pair=pair, r=r: e.scalar_tensor_tensor(out=Scur[:], in0=Scur[:], scalar=PKv[:, r, d, pair, 128:129], in1=PKv[:, r, d, pair, 0:128], op0=ALU.mult, op1=ALU.add), reads=["Scur", "PKall"], writes=["Scur"])
                    op("dve", lambda e, d=d, pair=pair, rn=rn: e.scalar_tensor_tensor(out=Sin[:, d, pair, :], in0=Scur[:], scalar=V("sel", 1, rn), in1=Sin[:, d, pair, :], op0=ALU.mult, op1=ALU.add), reads=["Scur", "Sin", "vecs"], writes=["Sin"])
        op("pool", lambda e: e.tensor_copy(out=Sinb[:], in_=Sin[:]), reads=["Sin"], writes=["Sinb"])
        Wgg = self.sb("Wgg", [128, 8, 512], BF16)
        for kc in range(8):
            dma("pool", lambda e, kc=kc: e.dma_start(out=Wgg[:, kc, :], in_=w_in_o[kc * 128:(kc + 1) * 128, 1024:1536]), "Wgg", writes=["Wgg"])
        ofs = [self.sb(f"of{i}", [128, 512], F32) for i in range(2)]
        sqbs = [self.sb(f"sqb{i}", [128, 512], BF16) for i in range(2)]
        sgs = [self.sb(f"sg{i}", [128, 512], F32) for i in range(2)]
        rinvs = [self.sb(f"rinv{i}", [128, 512], F32) for i in range(2)]

        def tail_chain(idx):
            h, g = idx // 4, idx % 4
            j_ = idx % 2
            of_, sqb, sg, rinv = ofs[j_], sqbs[j_], sgs[j_], rinvs[j_]
            ofk, sqk, sgk, rik = f"of{j_}", f"sqb{j_}", f"sg{j_}", f"rinv{j_}"
            pair, hb = h // 2, (h % 2) * 64
            c0 = g * 512
            pc, pck = (pA, "pA") if j_ == 0 else (pB, "pB")
            op("pe", lambda e: e.matmul(pc[:], lhsT=Sinb[hb:hb + 64, 0, pair, :], rhs=qd[hb:hb + 64, 0, pair, c0:c0 + 512], start=True, stop=False), reads=["Sinb", ("qd", 0, pair)], writes=[pck])
            op("pe", lambda e: e.matmul(pc[:], lhsT=Sinb[hb:hb + 64, 1, pair, :], rhs=qd[hb:hb + 64, 1, pair, c0:c0 + 512], start=False, stop=True), reads=["Sinb", ("qd", 1, pair)], writes=[pck], pe_acc=True)
            yield
            op("dve", lambda e: e.tensor_tensor(out=of_[:], in0=OL[:, h, c0:c0 + 512], in1=pc[:], op=ALU.add), reads=[("OL", h), pck], writes=[ofk])
            yield
            op("pool", lambda e: e.tensor_tensor(out=sqb[:], in0=of_[:], in1=of_[:], op=ALU.mult), reads=[ofk], writes=[sqk])
            yield
            pss, pssk = (pY[2], "pY2") if j_ == 0 else (pY[3], "pY3")
            op("pe", lambda e: e.matmul(pss[:], lhsT=ones_b[:], rhs=sqb[:], start=True, stop=True), reads=["ones_b", sqk], writes=[pssk])
            yield
            op("act", lambda e: e.activation(out=rinv[:], in_=pss[:], func=AF.Sqrt, scale=1.0 / 128, bias=EPS), reads=[pssk], writes=[rik])
            yield
            op("dve", lambda e: e.reciprocal(out=rinv[:], in_=rinv[:]), reads=[rik], writes=[rik])
            yield
            pg, pgk = pY[j_], f"pY{j_}"
            for kc in range(8):
                op("pe", lambda e, kc=kc: e.matmul(pg[:], lhsT=Wgg[:, kc, h * 128:(h + 1) * 128], rhs=uT[:, kc, c0:c0 + 512], start=(kc == 0), stop=(kc == 7)), reads=["Wgg"] + uk(c0, c0 + 512), writes=[pgk], pe_acc=True)
            yield
            op("act", lambda e: e.activation(out=sg[:], in_=pg[:], func=AF.Silu), reads=[pgk], writes=[sgk])
            yield
            op("dve", lambda e: e.scalar_tensor_tensor(out=of_[:], in0=of_[:], scalar=V("gnw", 1), in1=rinv[:], op0=ALU.mult, op1=ALU.mult), reads=[ofk, "vecs", rik], writes=[ofk])
            yield
            op("pool", lambda e: e.tensor_tensor(out=mixB[:, h, c0:c0 + 512], in0=of_[:], in1=sg[:], op=ALU.mult), reads=[ofk, sgk], writes=[("mixB", h)])
            yield

        self.interleave([tail_chain(i) for i in range(16)], 2)
        if self.stop_after == "gla":
            self.dump("mixB", mixB[:, :, 0:2048], [128, 4, 2048], [("mixB", c) for c in range(4)], BF16)
            return self.finish()
        self.stage_end()

        self.stage_begin()
        QTa = self.sb("QTa", [128, 8, 2048], BF16)
        KTa = self.sb("KTa", [128, 8448], BF16)
        Va = self.sb("Va", [128, 66, 2, 66], BF16)
        op("pool", lambda e: e.memset(Va[:], 1.0), writes=["Va"])
        op("pool", lambda e: e.memset(QTa[:], 0.0), writes=["QTa"])
        for h in range(8):
            r0 = (h // 4) * 64
            dma("sp", lambda e, h=h, r0=r0: e.dma_start(out=QTa[r0:r0 + 64, h, :], in_=qscr[:, h, :]), "QTa", reads=["qscr", "QTa"], writes=["QTa"])
        for kv in range(2):
            r0 = kv * 64
            dma("sp", lambda e, kv=kv, r0=r0: e.dma_start(out=KTa[r0:r0 + 64, 0:256], in_=kctx[:, kv, :]), "KTa", reads=["kctx"], writes=["KTa"])
            for r in range(4):
                dma("sp", lambda e, kv=kv, r=r, r0=r0: e.dma_start(out=KTa[r0:r0 + 64, 256 + r * 2048:256 + (r + 1) * 2048], in_=cc2k_out[r * 64:(r + 1) * 64, kv * 2048:(kv + 1) * 2048]), "KTa", reads=["cc2k_out"], writes=["KTa"])
        for t in range(66):
            if t < 2:
                dma("sp", lambda e, t=t: e.dma_start(out=Va[:, t, :, 0:64], in_=vctx[t * 128:(t + 1) * 128, :].rearrange("p (h d) -> p h d", d=64)), "Va", reads=["vctx", "Va"], writes=["Va"])
            else:
                dma("sp", lambda e, t=t: e.dma_start(out=Va[:, t, :, 0:64], in_=cc2v_out[(t - 2) * 128:(t - 1) * 128, :].rearrange("p (h d) -> p h d", d=64)), "Va", reads=["cc2v_out", "Va"], writes=["Va"])
        PTg = [self.sb(f"PTg{i}", [128, 512], BF16) for i in range(8)]
        recg = self.sb("recg", [128, 512], F32)
        Oug = self.sb("Oug", [64, 512], F32)
        mtmpg = self.sb("mtmpg", [64, 2048], BF16)
        SB2 = [(pA, "pA"), (pB, "pB"), (pY[0], "pY0"), (pY[1], "pY1")]
        items = [(h, qb, kt) for h in range(8) for qb in range(4) for kt in range(66)]
        LA = 3
        pend = {}

        def g_qk(i):
            h, qb, kt = items[i]
            kv, q0 = h // 4, qb * 512
            Sp, Sk = SB2[i % 4]
            P_, Pk = PTg[i % 8], f"PTg{i % 8}"
            op("pe", lambda e: e.matmul(Sp[:], lhsT=KTa[:, kt * 128:(kt + 1) * 128], rhs=QTa[:, h, q0:q0 + 512], start=True, stop=True), reads=["KTa", "QTa"], writes=[Sk])
            op("act", lambda e: e.activation(out=P_[:], in_=Sp[:], func=AF.Exp, scale=0.125), reads=[Sk], writes=[Pk])

        def g_pv(i, step):
            h, qb, kt = items[i]
            kv, q0 = h // 4, qb * 512
            blk = h * 4 + qb
            O, Ok = pY[2 + blk % 2], f"pY{2 + blk % 2}"
            P_, Pk = PTg[i % 8], f"PTg{i % 8}"
            op("pe", lambda e: e.matmul(O[0:66, :], lhsT=Va[:, kt, kv, :], rhs=P_[:], start=(kt == 0), stop=(kt == 65)), reads=["Va", Pk], writes=[Ok], pe_acc=True)
            if kt == 65:
                op("dve", lambda e: e.reciprocal(out=recg[64:65, :], in_=O[64:65, :]), reads=[Ok], writes=["recg"])
                op("act", lambda e: e.activation(out=Oug[:], in_=O[0:64, :], func=AF.Copy), reads=[Ok], writes=["Oug"])

                def fin():
                    op("pe", lambda e: e.matmul(pT[0:64, 0:512], lhsT=ones_f[64:65, 0:64], rhs=recg[64:65, :], start=True, stop=True), reads=["ones_f", "recg"], writes=["pT"])
                    if h % 2 == 0:
                        op("dve", lambda e: e.tensor_tensor(out=mixA[0:64, h // 2, q0:q0 + 512], in0=Oug[:], in1=pT[0:64, 0:512], op=ALU.mult), reads=["Oug", "pT"], writes=[("mixA", h // 2, 0)])
                    else:
                        op("dve", lambda e: e.tensor_tensor(out=mtmpg[:, q0:q0 + 512], in0=Oug[:], in1=pT[0:64, 0:512], op=ALU.mult), reads=["Oug", "pT"], writes=["mtmpg"])
                        if qb == 3:
                            dma("sp", lambda e: e.dma_start(out=mixA[64:128, h // 2, 0:2048], in_=mtmpg[:, :]), "mixAup", reads=["mtmpg"], writes=[("mixA", h // 2, 1)])
                pend.setdefault(step + 2, []).append(fin)

        nI = len(items)
        for step in range(nI + LA + 4):
            if step < nI:
                g_qk(step)
            if 0 <= step - LA < nI:
                g_pv(step - LA, step)
            for f_ in pend.pop(step, []):
                f_()
        if self.stop_after == "gqa":
            self.dump("mixA", mixA[:], [128, 4, 2304], [("mixA", c, q) for c in range(4) for q in range(2)], BF16)
            return self.finish()
        self.stage_end()

        def old1(tt):
            return h1[tt * 128:(tt + 1) * 128, :], [("h1", tt)]

        def mix1(kc):
            if kc < 4:
                return mixB[:, kc, :], [("mixB", kc)]
            return mixA[:, kc - 4, :], [("mixA", kc - 4, 0), ("mixA", kc - 4, 1)]

        self.stage_begin()
        W1a1, W2h1 = ffn_alloc()
        out_proj(1, w_out_o, old1, mix1, ntile=16, after_wo=lambda: ffn_load(1, 0, W1a1, "W1h0", W2h1))
        fnw = self.sb("fnw", [128, 1024], F32)
        dma("sp", lambda e: e.dma_start(out=fnw[:], in_=fnw_d.ap()), "fnw", writes=["fnw"])

        def fin1(tt, xb, xk, tb, tk):
            norm_from_sbuf(xb, xk, 0, None, None, None, None, None, ntile=tb, nkey=tk)
            op("pool", lambda e, tb=tb: e.tensor_tensor(out=tb[:], in0=tb[:], in1=fnw[:], op=ALU.mult), reads=[tk, "fnw"], writes=[tk])
            dma("sp", lambda e, tb=tb, tt=tt: e.dma_start(out=out_d[tt * 128:(tt + 1) * 128, :], in_=tb[:]), "outst_" + tk, reads=[tk], writes=[("out", tt)])

        ffn(1, fin1, W1a1, W2h1, ngrp=8)
        S.wait_all("sp", [("out", tt) for tt in range(16)])
        return self.finish()

    def finish(self):
        S = self.S
        keys = [("dbg", n) for n in self.dbg_out]
        S.wait_all("sp", keys)
        S.emit()
        while self.cur is not self.es:
            self.cur.close()
            self.cur = self.stk.pop()
        self.es.close()
        return self.nc


def _fm(v):
    v = np.asarray(v, np.float32)
    return np.ascontiguousarray(v.reshape(-1, 128).T)


def make_natab(rpb):
    rpb = np.asarray(rpb, np.float32)
    cols = np.arange(64)
    col_start = np.clip(cols - 8, 0, 48)
    kc = cols[:, None]
    qc = cols[None, :]
    valid = (kc >= col_start[None, :]) & (kc < col_start[None, :] + 16)
    dcol = np.clip(kc - qc, -15, 15) + 15
    tab = np.full((128, 8, 22, 64), NEG, np.float32)
    for idx in range(22):
        for half in range(2):
            drow = 17 - idx + half
            if 0 <= drow <= 14:
                blk = np.where(valid[None], rpb[:, drow][:, dcol], NEG)
                tab[half * 64:(half + 1) * 64, :, idx, :] = blk.transpose(1, 0, 2)
    return tab.reshape(128, 8, 22 * 64)


def make_aug(k):
    qaug = np.full((40, 2048), NEG, np.float32)
    for r in range(32):
        R = 32 * k + r
        start = min(max(R - 4, 0), 120)
        for lr in range(40):
            gr = 32 * k - 4 + lr
            if start <= gr < start + 8:
                qaug[lr, r * 64:(r + 1) * 64] = 0.0
    kaug = np.zeros((40, 2560), np.float32)
    for lr in range(40):
        kaug[lr, lr * 64:(lr + 1) * 64] = 1.0
    return qaug, kaug


def _gmask():
    i = np.arange(128)
    same = (i[:, None] // 64) == (i[None, :] // 64)
    m = np.zeros((128, 4, 128), np.float32)
    m[:, 0, :] = same & (i[:, None] > i[None, :])
    m[:, 1, :] = same & (i[:, None] < i[None, :])
    m[:, 2, :] = same & (i[:, None] <= i[None, :])
    m[:, 3, :] = same & (i[:, None] >= i[None, :])
    return m


GMASK = _gmask()


def make_in_maps(inp):
    f = lambda a: np.ascontiguousarray(np.asarray(a, np.float32))
    x = f(inp["x"])
    natab = make_natab(inp["na_rpb"][0])
    maps = []
    for i in range(8):
        b, k = i // 4, i % 4
        xp = np.zeros((2560, 1024), np.float32)
        lo, hi = k * 2048 - 256, (k + 1) * 2048 + 256
        slo, shi = max(lo, 0), min(hi, 8192)
        xp[slo - lo:shi - lo] = x[b, slo:shi]
        vec = np.zeros((128, NV), np.float32)

        def put(name, v, off=0):
            v = _fm(v)
            vec[:, VC[name] + off:VC[name] + off + v.shape[1]] = v

        put("c", inp["c"][b]); put("cctx", inp["c_ctx"])
        for l in range(2):
            put(f"n1w{l}", inp["norm1_w"][l]); put(f"n2w{l}", inp["norm2_w"][l]); put(f"bmod{l}", inp["b_mod"][l])
        put("fnw", inp["final_norm_w"])
        cw = np.asarray(inp["lru_conv_w"][0], np.float32)
        for j in range(4):
            v = _fm(cw[j])
            for c in range(4):
                vec[:, VC["cw"] + c * 4 + j] = v[:, c]
        put("cb", inp["lru_conv_b"][0])
        for d in range(2):
            put(f"ba{d}", inp["lru_ba"][0, d]); put(f"bx{d}", inp["lru_bx"][0, d]); put(f"lam{d}", inp["lru_lambda"][0, d])
        vec[:, VC["sel"] + k] = 1.0
        vec[:, VC["flag"] + 0] = 1.0 if k > 0 else 0.0
        vec[:, VC["flag"] + 1] = 1.0 if k < 3 else 0.0
        qaug, kaug = make_aug(k)
        gba = np.asarray(inp["gla_ba"][0], np.float32)
        for d in range(2):
            put(f"gba{d}", gba[d])
        put("gnw", inp["gla_norm_w"][0])
        tpos = np.arange(k * 2048, (k + 1) * 2048, dtype=np.int32)
        inv = (np.float32(10000.0) ** (-np.arange(16, dtype=np.float32) / np.float32(16))).astype(np.float32)
        ang_r = (tpos // 64).astype(np.float32)[:, None] * inv[None, :]
        ang_c = (tpos % 64).astype(np.float32)[:, None] * inv[None, :]
        cr, sr, cc_, sc_ = np.cos(ang_r), np.sin(ang_r), np.cos(ang_c), np.sin(ang_c)
        cosT = np.concatenate([cr, cr, cc_, cc_], axis=1).astype(np.float32)
        sinT = np.concatenate([sr, sc_], axis=1).astype(np.float32)
        maps.append({
            "xh": xp, "ctxb": f(inp["ctx"][b]), "vecs": vec,
            "w_mod": f(inp["w_mod"]), "b_mod": f(inp["b_mod"]),
            "w_ff1": f(inp["w_ff1"]), "w_ff2": f(inp["w_ff2"]),
            "w_in_even": f(inp["w_in_even"][0]), "w_out_even": f(inp["w_out_even"][0]),
            "lru_wa": f(inp["lru_wa"][0]), "lru_wx": f(inp["lru_wx"][0]),
            "natab": natab, "qaug": qaug, "kaug": kaug,
            "w_in_odd": f(inp["w_in_odd"][0]), "w_out_odd": f(inp["w_out_odd"][0]),
            "gla_wa2": f(inp["gla_wa2"][0]), "barow": f(gba.reshape(1, 512)),
            "cosT": cosT, "sinT": sinT, "nsinT": f(-sinT),
            "qnw_bc": f(np.broadcast_to(np.asarray(inp["gqa_q_norm_w"][0], np.float32)[None, :], (128, 64))),
            "knw_bc": f(np.broadcast_to(np.asarray(inp["gqa_k_norm_w"][0], np.float32)[None, :], (128, 64))),
            "gmask": GMASK, "fnw_bc": f(np.broadcast_to(np.asarray(inp["final_norm_w"], np.float32)[None, :], (128, 1024))),
        })
    return maps


def kernel(**inputs):
    bld = Builder()
    nc = bld.build()
    maps = [{k: v for k, v in m.items() if k in bld.ins} for m in make_in_maps(inputs)]
    res = run_bass_kernel_spmd(nc, maps, core_ids=list(range(8)))
    out = np.zeros((2, 8192, 1024), np.float32)
    for i in range(8):
        b, k = i // 4, i % 4
        out[b, k * 2048:(k + 1) * 2048] = res.results[i]["out"]
    return out
```
